# Optimizing a Trainium2 kernel written in Bass

```python
import math
import jax, jax.numpy as jnp
from jax import lax
import numpy as np

D_MODEL = 1024
BATCH = 4
SEQ = 4096
DEPTH = 4
DEC_BATCH = 32
DEC_SEQ = 1
PAST_LEN = 8192
PAGE_SIZE = 128

F32 = jnp.float32
EPS = 1e-6
N_EVEN = (DEPTH + 1) // 2
N_ODD = DEPTH // 2

GDN_HEADS = 4
GDN_DK = 128
GDN_DV = 128
GDN_CONV = 4
GDN_CHUNK = 64
GDN_QKV = GDN_HEADS * (2 * GDN_DK + GDN_DV)

SWA_GROUPS = ((128, 1), (512, 4), (2048, 16))
N_SWA_GROUPS = 3
SWA_HEADS = 4
SWA_HD = 128
SWA_GW = SWA_HEADS * SWA_HD

LRU_WIDTH = D_MODEL
LRU_BLOCKS = 4
LRU_BW = LRU_WIDTH // LRU_BLOCKS
LRU_CONV = 4
LRU_C = 8.0

MEM_LEN = 256
MEM_HEADS = 4
MEM_HD = 128
MEM_W = MEM_HEADS * MEM_HD

EVEN_SIZES = (GDN_QKV, GDN_HEADS * GDN_DV, GDN_HEADS, GDN_HEADS,
              3 * N_SWA_GROUPS * SWA_GW, SWA_GW, MEM_W, MEM_W)
EVEN_IN = sum(EVEN_SIZES)
EVEN_MIX = GDN_HEADS * GDN_DV + SWA_GW + MEM_W
ODD_SIZES = (LRU_WIDTH, LRU_WIDTH, MEM_W, MEM_W)
ODD_IN = sum(ODD_SIZES)
ODD_MIX = LRU_WIDTH + MEM_W

kernel_name = 'hybrid_gdn_dilated_rglru_mem_decoder_step'


def rms_norm(x, g):
    xf = x.astype(F32)
    y = xf * lax.rsqrt(jnp.mean(xf * xf, axis=-1, keepdims=True) + EPS)
    return (y * g.astype(F32)).astype(x.dtype)


def l2norm(x):
    return x * lax.rsqrt(jnp.sum(x * x, axis=-1, keepdims=True) + EPS)


def split_cols(h, sizes):
    offs = np.cumsum(np.array(sizes))[:-1].tolist()
    return jnp.split(h, offs, axis=-1)


def causal_dwconv(x, buf, w):
    K = w.shape[0]
    T = x.shape[1]
    xp = jnp.concatenate([buf.astype(x.dtype), x], axis=1)
    y = xp[:, 0:T] * w[0]
    for j in range(1, K):
        y = y + xp[:, j:j + T] * w[j]
    return y, xp[:, T:]


def gdn_recurrent(q, k, v, beta, g, S0):
    def step(S, inp):
        qt, kt, vt, bt, gt = inp
        S = S * jnp.exp(gt)[..., None, None]
        u = (vt - jnp.einsum('bhk,bhkv->bhv', kt, S)) * bt[..., None]
        S = S + jnp.einsum('bhk,bhv->bhkv', kt, u)
        return S, jnp.einsum('bhk,bhkv->bhv', qt, S)
    xs = (jnp.moveaxis(q, 1, 0), jnp.moveaxis(k, 1, 0), jnp.moveaxis(v, 1, 0),
          jnp.moveaxis(beta, 1, 0), jnp.moveaxis(g, 1, 0))
    S, o = lax.scan(step, S0, xs)
    return jnp.moveaxis(o, 0, 1), S


def gdn_chunked(q, k, v, beta, g, S0):
    B, T, H, dk = q.shape
    dv = v.shape[-1]
    C = GDN_CHUNK
    N = T // C

    def chunks(t):
        t = t.reshape((B, N, C, H) + t.shape[3:])
        return jnp.moveaxis(t, (1, 3), (0, 2))

    qc, kc, vc, bc = chunks(q), chunks(k), chunks(v), chunks(beta)
    gc = jnp.cumsum(chunks(g), axis=-1)
    pos = jnp.arange(C)
    incl = pos[:, None] >= pos[None, :]
    strict = pos[:, None] > pos[None, :]
    diff = gc[..., :, None] - gc[..., None, :]
    decay = jnp.where(incl, jnp.exp(jnp.where(incl, diff, 0.0)), 0.0)
    kb = kc * bc[..., None]
    a_low = jnp.einsum('nbhik,nbhjk->nbhij', kb, kc) * jnp.where(strict, decay, 0.0)
    rhs = jnp.concatenate([vc * bc[..., None], kb * jnp.exp(gc)[..., None]], axis=-1)
    sol = lax.linalg.triangular_solve(a_low + jnp.eye(C, dtype=F32), rhs,
                                      left_side=True, lower=True, unit_diagonal=True)
    u, w = sol[..., :dv], sol[..., dv:]
    attn = jnp.einsum('nbhik,nbhjk->nbhij', qc, kc) * decay

    def step(S, inp):
        qi, ki, ui, wi, ai, gi = inp
        v_new = ui - jnp.einsum('bhck,bhkv->bhcv', wi, S)
        o = (jnp.einsum('bhck,bhkv->bhcv', qi * jnp.exp(gi)[..., None], S)
             + jnp.einsum('bhij,bhjv->bhiv', ai, v_new))
        g_last = gi[..., -1]
        k_dec = ki * jnp.exp(g_last[..., None] - gi)[..., None]
        S = S * jnp.exp(g_last)[..., None, None] + jnp.einsum('bhck,bhcv->bhkv', k_dec, v_new)
        return S, o

    S, o = lax.scan(step, S0, (qc, kc, u, w, attn, gc))
    o = jnp.moveaxis(o, (0, 2), (1, 3)).reshape(B, T, H, dv)
    return o, S


def dilated_group_prompt(q, k, v, window, dil):
    B, S, H, hd = q.shape
    n = window // dil
    span = n * dil
    Sp = -(-S // span) * span
    NB = Sp // span

    def blocks(t):
        t = jnp.pad(t, ((0, 0), (0, Sp - S), (0, 0), (0, 0))).reshape(B, NB, n, dil, H, hd)
        return jnp.moveaxis(t, 3, 1)

    qb, kb, vb = blocks(q), blocks(k), blocks(v)

    def with_prev(t):
        prev = jnp.pad(t, ((0, 0), (0, 0), (1, 0), (0, 0), (0, 0), (0, 0)))[:, :, :-1]
        return jnp.concatenate([prev, t], axis=3)

    kc, vc = with_prev(kb), with_prev(vb)
    s = jnp.einsum('brnqhd,brnkhd->brnhqk', qb, kc, preferred_element_type=F32) * hd ** -0.5
    qi = jnp.arange(n)[:, None]
    kj = jnp.arange(2 * n)[None, :]
    dist = n + qi - kj
    band = (dist >= 0) & (dist <= n)
    mask = band[None] & ((jnp.arange(NB)[:, None, None] > 0) | (kj >= n)[None])
    s = jnp.where(mask[None, None, :, None], s, -jnp.inf)
    m = jnp.max(s, axis=-1)
    p = jnp.exp(s - m[..., None])
    l = jnp.sum(p, axis=-1)
    num = jnp.einsum('brnhqk,brnkhd->brnqhd', p, vc.astype(F32))

    def unblock(t):
        t = jnp.moveaxis(t, 1, 3)
        return t.reshape((B, Sp) + t.shape[4:])[:, :S]

    return unblock(num), unblock(jnp.swapaxes(m, -1, -2)), unblock(jnp.swapaxes(l, -1, -2))


def dilated_group_sample(q, k, v, buf, window, dil):
    B, T, H, hd = q.shape
    L = buf.shape[1]
    n = window // dil
    cat = jnp.concatenate([buf.astype(k.dtype), jnp.stack([k, v], axis=2)], axis=1)
    idx = L + jnp.arange(T)[:, None] - dil * jnp.arange(n + 1)[None, :]
    valid = idx >= 0
    kv = cat[:, jnp.maximum(idx, 0)]
    s = jnp.einsum('bthd,btjhd->bthj', q, kv[:, :, :, 0], preferred_element_type=F32) * hd ** -0.5
    s = jnp.where(valid[None, :, None, :], s, -jnp.inf)
    m = jnp.max(s, axis=-1)
    p = jnp.exp(s - m[..., None])
    num = jnp.einsum('bthj,btjhd->bthd', p, kv[:, :, :, 1].astype(F32))
    return (num, m, jnp.sum(p, axis=-1)), cat[:, T:]


def combine_dilations(parts):
    m_all = parts[0][1]
    for part in parts[1:]:
        m_all = jnp.maximum(m_all, part[1])
    num = jnp.zeros_like(parts[0][0])
    den = jnp.zeros_like(parts[0][2])
    for nu, m, l in parts:
        c = jnp.exp(m - m_all)
        num = num + nu * c[..., None]
        den = den + l * c
    return num / den[..., None]


def memory_kv(mem, g, w_kv):
    B, M, _ = mem.shape
    return (rms_norm(mem, g) @ w_kv).reshape(B, M, 2, MEM_HEADS, MEM_HD)


def memory_attend(q, kv):
    s = jnp.einsum('bthd,bmhd->bhtm', q, kv[:, :, 0], preferred_element_type=F32) * MEM_HD ** -0.5
    p = jax.nn.softmax(s, axis=-1)
    return jnp.einsum('bhtm,bmhd->bthd', p, kv[:, :, 1].astype(F32))


def linear_scan(a, b, h0):
    def op(l, r):
        return (l[0] * r[0], r[0] * l[1] + r[1])
    A, Bc = lax.associative_scan(op, (a, b), axis=1)
    return A * h0[:, None] + Bc


def rg_lru(xc, wa, ba, wx, bx, lam, h0):
    B, T, _ = xc.shape
    xb = xc.astype(F32).reshape(B, T, LRU_BLOCKS, LRU_BW)
    r = jax.nn.sigmoid(jnp.einsum('btnc,ncd->btnd', xb, wa.astype(F32)) + ba.astype(F32))
    i = jax.nn.sigmoid(jnp.einsum('btnc,ncd->btnd', xb, wx.astype(F32)) + bx.astype(F32))
    log_a = -LRU_C * r * jax.nn.softplus(-lam.astype(F32).reshape(LRU_BLOCKS, LRU_BW))
    a = jnp.exp(log_a)
    b = jnp.sqrt(-jnp.expm1(2.0 * log_a)) * (i * xb)
    h = linear_scan(a.reshape(B, T, LRU_WIDTH), b.reshape(B, T, LRU_WIDTH), h0.astype(F32))
    return h, h[:, -1]


def even_mixer(h, mkv, S0, conv0, swa_bufs, w_in, conv_w, a_log, dt_bias, o_norm, prompt):
    B, T, _ = h.shape
    qkv_a, z_a, b_a, a_a, qkv_b, gate_b, q_m, gate_m = split_cols(h @ w_in, EVEN_SIZES)
    c, conv_new = causal_dwconv(qkv_a, conv0, conv_w)
    c = jax.nn.silu(c.astype(F32))
    qa, ka, va = jnp.split(c, [GDN_HEADS * GDN_DK, 2 * GDN_HEADS * GDN_DK], axis=-1)
    qa = l2norm(qa.reshape(B, T, GDN_HEADS, GDN_DK)) * GDN_DK ** -0.5
    ka = l2norm(ka.reshape(B, T, GDN_HEADS, GDN_DK))
    va = va.reshape(B, T, GDN_HEADS, GDN_DV)
    beta = jax.nn.sigmoid(b_a.astype(F32))
    g = -jnp.exp(a_log.astype(F32)) * jax.nn.softplus(a_a.astype(F32) + dt_bias.astype(F32))
    if prompt:
        o_a, S_new = gdn_chunked(qa, ka, va, beta, g, S0.astype(F32))
    else:
        o_a, S_new = gdn_recurrent(qa, ka, va, beta, g, S0.astype(F32))
    o_a = rms_norm(o_a, o_norm) * jax.nn.silu(z_a.astype(F32)).reshape(B, T, GDN_HEADS, GDN_DV)
    qkv_b = qkv_b.reshape(B, T, 3, N_SWA_GROUPS, SWA_HEADS, SWA_HD)
    parts, bufs_new = [], []
    for gi, (win, dil) in enumerate(SWA_GROUPS):
        qg, kg, vg = qkv_b[:, :, 0, gi], qkv_b[:, :, 1, gi], qkv_b[:, :, 2, gi]
        if prompt:
            parts.append(dilated_group_prompt(qg, kg, vg, win, dil))
            keep = min(win, T)
            bufs_new.append(jnp.stack([kg, vg], axis=2)[:, T - keep:])
        else:
            part, buf = dilated_group_sample(qg, kg, vg, swa_bufs[gi], win, dil)
            parts.append(part)
            bufs_new.append(buf)
    o_b = combine_dilations(parts) * jax.nn.silu(gate_b.astype(F32)).reshape(B, T, SWA_HEADS, SWA_HD)
    o_m = memory_attend(q_m.reshape(B, T, MEM_HEADS, MEM_HD), mkv)
    o_m = o_m * jax.nn.silu(gate_m.astype(F32)).reshape(B, T, MEM_HEADS, MEM_HD)
    mix = jnp.concatenate([o_a.reshape(B, T, -1), o_b.reshape(B, T, -1), o_m.reshape(B, T, -1)], axis=-1)
    return mix, S_new, conv_new, bufs_new


def odd_mixer(h, mkv, h0, conv0, w_in, conv_w, conv_b, wa, ba, wx, bx, lam):
    B, T, _ = h.shape
    xb, gate, q_m, gate_m = split_cols(h @ w_in, ODD_SIZES)
    xc, conv_new = causal_dwconv(xb, conv0, conv_w)
    hs, h_last = rg_lru(xc + conv_b, wa, ba, wx, bx, lam, h0)
    o_c = hs * jax.nn.silu(gate.astype(F32))
    o_m = memory_attend(q_m.reshape(B, T, MEM_HEADS, MEM_HD), mkv)
    o_m = o_m * jax.nn.silu(gate_m.astype(F32)).reshape(B, T, MEM_HEADS, MEM_HD)
    mix = jnp.concatenate([o_c, o_m.reshape(B, T, -1)], axis=-1)
    return mix, h_last, conv_new


def setup_inputs(seed: int = 0) -> dict:
    key = jax.random.key(seed)
    ks = iter(jax.random.split(key, 40))

    def nrm(shape, scale=1.0):
        return jax.random.normal(next(ks), shape, F32) * scale

    def uni(shape, lo, hi):
        return jax.random.uniform(next(ks), shape, F32, lo, hi)

    swa_len = [min(w, PAST_LEN) for w, _ in SWA_GROUPS]
    dt = jnp.exp(uni((N_EVEN, GDN_HEADS), math.log(1e-3), math.log(1e-1)))
    s_lru = uni((N_ODD, LRU_WIDTH), 0.81, 0.998) ** (1.0 / LRU_C)
    return {
        'x_prompt': nrm((BATCH, SEQ, D_MODEL)),
        'x_sample': nrm((DEC_BATCH, DEC_SEQ, D_MODEL)),
        'state_gdn': nrm((N_EVEN, DEC_BATCH, GDN_HEADS, GDN_DK, GDN_DV), 0.1),
        'state_gdn_conv': nrm((N_EVEN, DEC_BATCH, GDN_CONV - 1, GDN_QKV)),
        'cache_swa1': nrm((N_EVEN, DEC_BATCH, swa_len[0], 2, SWA_HEADS, SWA_HD)),
        'cache_swa2': nrm((N_EVEN, DEC_BATCH, swa_len[1], 2, SWA_HEADS, SWA_HD)),
        'cache_swa3': nrm((N_EVEN, DEC_BATCH, swa_len[2], 2, SWA_HEADS, SWA_HD)),
        'state_lru': nrm((N_ODD, DEC_BATCH, LRU_WIDTH), 0.5),
        'state_lru_conv': nrm((N_ODD, DEC_BATCH, LRU_CONV - 1, LRU_WIDTH)),
        'cache_mem': nrm((DEPTH, DEC_BATCH, MEM_LEN, 2, MEM_HEADS, MEM_HD)),
        'mem_prompt': nrm((BATCH, MEM_LEN, D_MODEL)),
        'norm_pre': 1.0 + nrm((DEPTH, D_MODEL), 0.02),
        'norm_post': 1.0 + nrm((DEPTH, D_MODEL), 0.02),
        'mem_norm': 1.0 + nrm((DEPTH, D_MODEL), 0.02),
        'w_mem_kv': nrm((DEPTH, D_MODEL, 2 * MEM_W), D_MODEL ** -0.5),
        'w_in_even': nrm((N_EVEN, D_MODEL, EVEN_IN), D_MODEL ** -0.5),
        'w_out_even': nrm((N_EVEN, EVEN_MIX, D_MODEL), EVEN_MIX ** -0.5),
        'gdn_conv_w': nrm((N_EVEN, GDN_CONV, GDN_QKV), 0.5),
        'gdn_a_log': jnp.log(uni((N_EVEN, GDN_HEADS), 1.0, 16.0)),
        'gdn_dt_bias': dt + jnp.log(-jnp.expm1(-dt)),
        'gdn_norm': 1.0 + nrm((N_EVEN, GDN_DV), 0.02),
        'w_in_odd': nrm((N_ODD, D_MODEL, ODD_IN), D_MODEL ** -0.5),
        'w_out_odd': nrm((N_ODD, ODD_MIX, D_MODEL), ODD_MIX ** -0.5),
        'lru_conv_w': nrm((N_ODD, LRU_CONV, LRU_WIDTH), 0.5),
        'lru_conv_b': nrm((N_ODD, LRU_WIDTH), 0.01),
        'lru_wa': nrm((N_ODD, LRU_BLOCKS, LRU_BW, LRU_BW), LRU_BW ** -0.5),
        'lru_ba': nrm((N_ODD, LRU_BLOCKS, LRU_BW), 0.01),
        'lru_wx': nrm((N_ODD, LRU_BLOCKS, LRU_BW, LRU_BW), LRU_BW ** -0.5),
        'lru_bx': nrm((N_ODD, LRU_BLOCKS, LRU_BW), 0.01),
        'lru_lambda': jnp.log(s_lru) - jnp.log1p(-s_lru),
    }


def reference(x_prompt, x_sample, state_gdn, state_gdn_conv, cache_swa1, cache_swa2, cache_swa3,
              state_lru, state_lru_conv, cache_mem, mem_prompt,
              norm_pre, norm_post, mem_norm, w_mem_kv,
              w_in_even, w_out_even, gdn_conv_w, gdn_a_log, gdn_dt_bias, gdn_norm,
              w_in_odd, w_out_odd, lru_conv_w, lru_conv_b, lru_wa, lru_ba, lru_wx, lru_bx, lru_lambda):
    Bp = x_prompt.shape[0]
    xp, xs = x_prompt, x_sample
    swa_in = (cache_swa1, cache_swa2, cache_swa3)
    gdn_p, gdn_s, gconv_p, gconv_s = [], [], [], []
    swa_p = ([], [], [])
    swa_s = ([], [], [])
    lru_p, lru_s, lconv_p, lconv_s, mem_p = [], [], [], [], []
    for layer in range(DEPTH):
        j = layer // 2
        mkv_p = memory_kv(mem_prompt, mem_norm[layer], w_mem_kv[layer])
        mem_p.append(mkv_p)
        hp = rms_norm(xp, norm_pre[layer])
        hs = rms_norm(xs, norm_pre[layer])
        if layer % 2 == 0:
            ew = (w_in_even[j], gdn_conv_w[j], gdn_a_log[j], gdn_dt_bias[j], gdn_norm[j])
            mix_p, S_p, c_p, b_p = even_mixer(
                hp, mkv_p, jnp.zeros((Bp, GDN_HEADS, GDN_DK, GDN_DV), F32),
                jnp.zeros((Bp, GDN_CONV - 1, GDN_QKV), xp.dtype), None, *ew, True)
            mix_s, S_s, c_s, b_s = even_mixer(
                hs, cache_mem[layer], state_gdn[j], state_gdn_conv[j],
                [c[j] for c in swa_in], *ew, False)
            gdn_p.append(S_p)
            gdn_s.append(S_s)
            gconv_p.append(c_p)
            gconv_s.append(c_s)
            for gi in range(N_SWA_GROUPS):
                swa_p[gi].append(b_p[gi])
                swa_s[gi].append(b_s[gi])
            w_out = w_out_even[j]
        else:
            ow = (w_in_odd[j], lru_conv_w[j], lru_conv_b[j], lru_wa[j], lru_ba[j],
                  lru_wx[j], lru_bx[j], lru_lambda[j])
            mix_p, h_p, c_p = odd_mixer(
                hp, mkv_p, jnp.zeros((Bp, LRU_WIDTH), F32),
                jnp.zeros((Bp, LRU_CONV - 1, LRU_WIDTH), xp.dtype), *ow)
            mix_s, h_s, c_s = odd_mixer(hs, cache_mem[layer], state_lru[j], state_lru_conv[j], *ow)
            lru_p.append(h_p)
            lru_s.append(h_s)
            lconv_p.append(c_p)
            lconv_s.append(c_s)
            w_out = w_out_odd[j]
        xp = xp + rms_norm(mix_p.astype(xp.dtype) @ w_out, norm_post[layer])
        xs = xs + rms_norm(mix_s.astype(xs.dtype) @ w_out, norm_post[layer])
    return (xp, xs,
            jnp.stack(gdn_p), jnp.stack(gdn_s), jnp.stack(gconv_p), jnp.stack(gconv_s),
            jnp.stack(swa_p[0]), jnp.stack(swa_s[0]), jnp.stack(swa_p[1]), jnp.stack(swa_s[1]),
            jnp.stack(swa_p[2]), jnp.stack(swa_s[2]),
            jnp.stack(lru_p), jnp.stack(lru_s), jnp.stack(lconv_p), jnp.stack(lconv_s),
            jnp.stack(mem_p))
```

```python
import numpy as np
from contextlib import ExitStack
import concourse.bass as bass
import concourse.mybir as mybir
from concourse.bass_utils import run_bass_kernel_spmd

F32 = mybir.dt.float32
BF16 = mybir.dt.bfloat16
AF = mybir.ActivationFunctionType
ALU = mybir.AluOpType
AX = mybir.AxisListType

D = 1024
SEQ = 4096
ST = 2048
NST = SEQ // ST
EPS = 1e-6
MEM = 256
NCORES = 8
SPC = 4


class Buf:
    __slots__ = ("w", "r", "name", "excl")

    def __init__(self, name="", excl=False):
        self.w = {}
        self.r = {}
        self.name = name
        self.excl = excl


def inherit(new_bufs, old_bufs):
    merged = {}
    for ob in old_bufs:
        for d in (ob.w, ob.r):
            for key, (sem, val) in d.items():
                if merged.get(key, (None, 0))[1] < val:
                    merged[key] = (sem, val)
    for nb in new_bufs:
        for key, (sem, val) in merged.items():
            if nb.w.get(key, (None, 0))[1] < val:
                nb.w[key] = (sem, val)


class V:
    __slots__ = ("ap", "bufs")

    def __init__(self, ap, bufs):
        self.ap = ap
        self.bufs = bufs


class Tl:
    def __init__(self, t, name, excl=False):
        self.t = t
        self.b = Buf(name, excl)

    def __getitem__(self, key):
        return V(self.t[key], [self.b])

    def v(self, ap):
        return V(ap, [self.b])


class Eng:
    def __init__(self, k, name, eng, is_pe=False):
        self.k = k
        self.name = name
        self.eng = eng
        self.is_pe = is_pe
        self.sem = k.new_sem("c_" + name)
        self.cnt = 0
        self.seen = {}
        self.nsem = 1

    def wait(self, ev):
        if ev is None:
            return
        sem, val = ev
        if self.seen.get(id(sem), 0) >= val:
            return
        self.eng.wait_ge(sem, val)
        self.seen[id(sem)] = val

    def collect(self, R, W):
        need = {}

        def add(ev, own_ok):
            if ev is None:
                return
            sem, val = ev
            if sem is self.sem and own_ok:
                return
            if need.get(id(sem), (None, 0))[1] < val:
                need[id(sem)] = (sem, val)

        for v in R:
            for b in v.bufs:
                for ev in b.w.values():
                    add(ev, self.is_pe)
                if b.excl:
                    for sem, val in b.r.values():
                        add((sem, val), True)
        for v in W:
            for b in v.bufs:
                for ev in b.w.values():
                    add(ev, self.is_pe)
                for sem, val in b.r.values():
                    add((sem, val), self.is_pe)
        for ev in need.values():
            self.wait(ev)

    def issue(self, fn, R, W):
        self.collect(R, W)
        ins = fn()
        if self.cnt >= 30000:
            self.nsem += 1
            self.sem = self.k.new_sem("c_%s%d" % (self.name, self.nsem))
            self.cnt = 0
        self.cnt += 1
        ins.then_inc(self.sem, 1)
        ev = (self.sem, self.cnt)
        self.k.ninstr += 1
        for v in R:
            for b in v.bufs:
                b.r[id(self.sem)] = ev
        for v in W:
            for b in v.bufs:
                b.w = {id(self.sem): ev}
                b.r = {}
        return ins


class K:
    def __init__(self):
        self.nc = bass.Bass("TRN2", target_bir_lowering=False)
        self.es = ExitStack()
        self.ninstr = 0
        nc = self.nc
        self.pe = Eng(self, "pe", nc.tensor, is_pe=True)
        self.act = Eng(self, "act", nc.scalar)
        self.dve = Eng(self, "dve", nc.vector)
        self.pool = Eng(self, "pool", nc.gpsimd)
        self.sp = Eng(self, "sp", nc.sync)
        self.dsem = {"sp": [[self.new_sem("d%d" % i), 0] for i in range(16)],
                     "pool": [[self.new_sem("e%d" % i), 0] for i in range(12)]}
        self.dnext = {"sp": 0, "pool": 0}
        self.out_events = []
        self.bulk_sem = None
        self.bulk_n = 0
        self.in_names = []
        self.out_names = []

    def new_sem(self, name):
        return self.es.enter_context(self.nc.semaphore(name))

    def sb(self, name, shape, dt=F32):
        return Tl(self.es.enter_context(self.nc.sbuf_tensor(name, list(shape), dt)), name)

    def ps(self, name, shape, dt=F32):
        return Tl(self.es.enter_context(self.nc.psum_tensor(name, list(shape), dt)), name, excl=True)

    def din(self, name, shape, dt=F32):
        self.in_names.append(name)
        return Tl(self.nc.dram_tensor(name, list(shape), dt, kind="ExternalInput"), name)

    def dout(self, name, shape, dt=F32):
        self.out_names.append(name)
        return Tl(self.nc.dram_tensor(name, list(shape), dt, kind="ExternalOutput"), name)

    def dtmp(self, name, shape, dt=F32):
        return Tl(self.nc.dram_tensor(name, list(shape), dt, kind="Internal"), name)

    def dma(self, out, in_, q=None, is_output=False):
        q = q or self.sp
        pool = self.dsem[q.name]
        slot = pool[self.dnext[q.name]]
        self.dnext[q.name] = (self.dnext[q.name] + 1) % len(pool)
        if slot[1]:
            q.wait((slot[0], slot[1]))
        q.collect([in_], [out])
        ins = q.eng.dma_start(out=out.ap, in_=in_.ap)
        slot[1] += 16
        ins.then_inc(slot[0], 16)
        ev = (slot[0], slot[1])
        for b in in_.bufs:
            b.r[id(slot[0])] = ev
        for b in out.bufs:
            b.w = {id(slot[0]): ev}
            b.r = {}
        if is_output:
            self.out_events.append(ev)
        self.ninstr += 1

    def mm(self, out, lhsT, rhs, start=True, stop=True, sgc=False):
        if sgc:
            self.pe.issue(lambda: self.nc.tensor.matmul(out.ap, lhsT=lhsT.ap, rhs=rhs.ap, start=start, stop=stop,
                                                        skip_group_check=True), [lhsT, rhs], [out])
        else:
            self.pe.issue(lambda: self.nc.tensor.matmul(out.ap, lhsT=lhsT.ap, rhs=rhs.ap, start=start, stop=stop),
                          [lhsT, rhs], [out])

    def reduce_x(self, out, in_, op=ALU.add):
        self.dve.issue(lambda: self.nc.vector.tensor_reduce(out=out.ap, in_=in_.ap, axis=AX.X, op=op), [in_], [out])

    def bulk_copy(self, out_ap, in_ap):
        if self.bulk_sem is None:
            self.bulk_sem = self.new_sem("bulk")
        ins = self.nc.scalar.dma_start(out=out_ap, in_=in_ap)
        self.bulk_n += 16
        ins.then_inc(self.bulk_sem, 16)

    def tr(self, out, in_, ident):
        self.pe.issue(lambda: self.nc.tensor.transpose(out.ap, in_.ap, ident.ap), [in_, ident], [out])

    def actf(self, out, in_, func, bias=None, scale=None, eng=None):
        R = [in_]
        kw = {}
        if bias is not None:
            if isinstance(bias, V):
                R.append(bias)
                kw["bias"] = bias.ap
            else:
                kw["bias"] = bias
        if scale is not None:
            if isinstance(scale, V):
                R.append(scale)
                kw["scale"] = scale.ap
            else:
                kw["scale"] = scale
        self.act.issue(lambda: self.nc.scalar.activation(out=out.ap, in_=in_.ap, func=func, **kw), R, [out])

    def _ve(self, eng):
        return eng or self.dve

    def tt(self, out, a, b, op, eng=None):
        e = self._ve(eng)
        e.issue(lambda: e.eng.tensor_tensor(out=out.ap, in0=a.ap, in1=b.ap, op=op), [a, b], [out])

    def ts(self, out, a, s1, op0, s2=None, op1=None, eng=None):
        e = self._ve(eng)
        R = [a]
        s1a = s1
        s2a = s2
        if isinstance(s1, V):
            R.append(s1)
            s1a = s1.ap
        if isinstance(s2, V):
            R.append(s2)
            s2a = s2.ap
        if op1 is None:
            e.issue(lambda: e.eng.tensor_scalar(out=out.ap, in0=a.ap, scalar1=s1a, scalar2=None, op0=op0), R, [out])
        else:
            e.issue(lambda: e.eng.tensor_scalar(out=out.ap, in0=a.ap, scalar1=s1a, scalar2=s2a, op0=op0, op1=op1),
                    R, [out])

    def stt(self, out, a, s, b, op0, op1, eng=None):
        e = self._ve(eng)
        R = [a, b]
        sa = s
        if isinstance(s, V):
            R.append(s)
            sa = s.ap
        e.issue(lambda: e.eng.scalar_tensor_tensor(out=out.ap, in0=a.ap, scalar=sa, in1=b.ap, op0=op0, op1=op1),
                R, [out])

    def cp(self, out, in_, eng=None):
        e = self._ve(eng)
        e.issue(lambda: e.eng.tensor_copy(out=out.ap, in_=in_.ap), [in_], [out])

    def acp(self, out, in_):
        self.actf(out, in_, AF.Copy)

    def recip(self, out, in_):
        self.dve.issue(lambda: self.nc.vector.reciprocal(out=out.ap, in_=in_.ap), [in_], [out])

    def memset(self, out, val, eng=None):
        e = self._ve(eng)
        e.issue(lambda: e.eng.memset(out.ap, val), [], [out])

    def scan(self, out, d0, d1, init, op0=ALU.mult, op1=ALU.add):
        R = [d0, d1]
        ia = init
        if isinstance(init, V):
            R.append(init)
            ia = init.ap
        self.dve.issue(lambda: self.nc.vector.tensor_tensor_scan(out=out.ap, data0=d0.ap, data1=d1.ap, initial=ia,
                                                                 op0=op0, op1=op1), R, [out])

    def finish(self):
        best = {}
        for sem, val in self.out_events:
            if best.get(id(sem), (None, 0))[1] < val:
                best[id(sem)] = (sem, val)
        for ev in best.values():
            self.sp.wait(ev)
        if self.bulk_sem is not None:
            self.sp.wait((self.bulk_sem, self.bulk_n))
        self.es.close()


HD = 128
SCALE = float(128 ** -0.5)
GROUPS = ((128, 1), (512, 4), (2048, 16))
NEG = -30000.0

C_ID, C_ONE, C_NSL, C_NIU, C_TRIU, C_SEL, C_MP, C_MC, C_NM1 = range(9)
C_LV = 9
NCST = 15


def even_order():
    items = []
    for h in range(4):
        items.append(("mem", h, [7176 + h * 128, 7688 + h * 128]))
    for h in range(4):
        cols = []
        for g in range(3):
            for t in range(3):
                cols.append(2056 + (t * 12 + g * 4 + h) * 128)
        cols.append(6664 + h * 128)
        items.append(("swa", h, cols))
    for h in range(4):
        items.append(("gdn", h, [h * 128, 512 + h * 128, 1024 + h * 128, 1536 + h * 128]))
    return items


def odd_order():
    items = []
    for n in range(4):
        items.append(("lru", n, [(2 * n) * 128, (2 * n + 1) * 128, 1024 + (2 * n) * 128, 1024 + (2 * n + 1) * 128]))
    for h in range(4):
        items.append(("mem", h, [2048 + h * 128, 2560 + h * 128]))
    return items


class WStream:
    def __init__(self, k, plan, wst, wbf):
        self.k = k
        self.plan = plan
        self.wst = wst
        self.wbf = wbf
        self.n_dma = 0
        self.n_cast = 0
        self.n_use = 0

    def _dma(self):
        i = self.n_dma
        src, nk = self.plan[i]
        t = self.wst[i % len(self.wst)]
        self.k.dma(t.v(t.t[:, 0:nk * 128].rearrange("p (k m) -> p k m", k=nk)), src)
        self.n_dma += 1

    def _cast(self):
        i = self.n_cast
        src, nk = self.plan[i]
        a = self.wst[i % len(self.wst)]
        b = self.wbf[i % len(self.wbf)]
        self.k.cp(b[:, 0:nk * 128], a[:, 0:nk * 128], eng=self.k.pool)
        self.n_cast += 1

    def get(self):
        i = self.n_use
        n = len(self.plan)
        while self.n_cast < min(n, i + 2):
            while self.n_dma <= self.n_cast:
                self._dma()
            self._cast()
        while self.n_dma < min(n, i + 3):
            self._dma()
        self.n_use += 1
        nk = self.plan[i][1]
        b = self.wbf[i % len(self.wbf)]
        return b.v(b.t[:, 0:nk * 128].rearrange("p (k m) -> p k m", k=nk))


def build_program(cfg):
    NL = cfg.get("nlayers", 4)
    k = K()
    nc = k.nc

    def view(tl, ap):
        return tl.v(ap)

    cst_d = k.din("cst", [128, NCST, 128])
    xin_d = k.din("xT", [8, 128, SEQ])
    memT_d = k.din("memT", [128, 8, MEM])
    wkv_d = k.din("wkv", [4, 128, 8, 1024])
    gmem_d = k.din("gmem", [128, 4, 8])
    gpre_d = k.din("gpre", [128, 4, 8])
    gpost_d = k.din("gpost", [128, 4, 8])
    weven_d = k.din("w_even", [2, 64, 128, 8, 128])
    wodd_d = k.din("w_odd", [2, 24, 128, 8, 128])
    wout_d = k.din("w_out", [4, 8, 128, 12, 128])
    wba_d = k.din("wba", [2, 128, 8, 8])
    gconvw_d = k.din("gconvw", [2, 128, 12, 4])
    alog_d = k.din("alog", [2, 128, 64])
    dtb_d = k.din("dtb", [2, 128, 64])
    gnorm_d = k.din("gnorm", [2, 128, 1])
    lconvw_d = k.din("lconvw", [2, 128, 8, 4])
    lvec_d = k.din("lvec", [2, 128, 4, 8])
    lwa_d = k.din("lwa", [2, 128, 4, 2, 256])
    lwx_d = k.din("lwx", [2, 128, 4, 2, 256])

    memo_d = k.dout("o_memT", [4, 8, 128, MEM])
    yT_d = k.dout("o_yT", [8, 128, SEQ])
    ogdn_d = k.dout("o_gdn", [2, 4, 128, 128])
    ogconv_d = k.dout("o_gconv", [2, 128, 12, 3])
    oswa_d = [k.dout("o_swa%d" % (g + 1), [2, 2, 4, 128, GROUPS[g][0]]) for g in range(3)]
    olru_d = k.dout("o_lru", [2, 128, 8])
    olconv_d = k.dout("o_lconv", [2, 128, 8, 3])

    xsT_d = k.din("xsT", [128, 8, 4])
    sgdn_d = k.din("sgdn", [2, 4, 4, 128, 128])
    sgconv_d = k.din("sgconv", [2, 128, 12, 4, 3])
    sgconv_nat = k.din("sgconv_nat", [2, 4, 3, 1536])
    cswa_d = [k.din("cswa%d" % (g + 1), [2, 4, GROUPS[g][0], 2, 4, 128]) for g in range(3)]
    cmem_d = k.din("cmem", [4, 4, MEM, 2, 4, 128])
    slru_d = k.din("slru", [2, 128, 8, 4])
    slconv_d = k.din("slconv", [2, 128, 8, 4, 3])
    slconv_nat = k.din("slconv_nat", [2, 4, 3, 1024])
    oys_d = k.dout("o_ys", [4, 1024])
    ogdns_d = k.dout("o_gdn_s", [2, 4, 4, 128, 128])
    ogconvs_d = k.dout("o_gconv_s", [2, 4, 3, 1536])
    oswas_d = [k.dout("o_swa%d_s" % (g + 1), [2, 4, GROUPS[g][0], 2, 4, 128]) for g in range(3)]
    olrus_d = k.dout("o_lru_s", [2, 4, 1024])
    olconvs_d = k.dout("o_lconv_s", [2, 4, 3, 1024])

    xs_d = [k.dtmp("xs%d" % i, [8, 128, SEQ]) for i in range(2)]
    histk_d = [[k.dtmp("hk%d_%d" % (g, h), [128, GROUPS[g][0]], BF16) for h in range(4)] for g in range(3)]
    histv_d = [[k.dtmp("hv%d_%d" % (g, h), [128, GROUPS[g][1], 128], BF16) for h in range(4)] for g in range(3)]

    cst = k.sb("cst_sb", [128, NCST, 128])
    k.dma(cst[:], cst_d[:, :, :])
    ident_f = view(cst, cst.t[:, C_ID, :])
    ones_f = view(cst, cst.t[:, C_ONE, :])
    ident_b = k.sb("ident_b", [128, 128], BF16)
    ones_b = k.sb("ones_b", [128, 128], BF16)
    k.cp(ident_b[:], ident_f)
    k.cp(ones_b[:], ones_f)
    mP_gen = k.sb("mP_gen", [128, 4, 128], BF16)
    mP_f1 = k.sb("mP_f1", [128, 4, 128], BF16)
    mP_zero = k.sb("mP_zero", [128, 4, 128], BF16)
    mC_gen = k.sb("mC_gen", [128, 4, 128], BF16)
    k.memset(mP_zero[:], 0.0)
    k.memset(mP_f1[:, 0, :], 0.0)
    for q in range(4):
        k.cp(mP_gen[:, q, :], cst[:, C_MP, :])
        k.cp(mC_gen[:, q, :], cst[:, C_MC, :])
        if q > 0:
            k.cp(mP_f1[:, q, :], cst[:, C_MP, :])

    PS = [k.ps("psb%d" % i, [128, 512]) for i in range(8)]
    PSb6 = view(PS[6], PS[6].t[:, :].bitcast(BF16))

    kT_mem1 = k.sb("kTm", [128, 4, MEM], BF16)
    v_mem1 = k.sb("vm", [128, 2, 512], BF16)
    kT_mem = [kT_mem1] * 4
    v_mem = [v_mem1] * 4
    rs_mem = k.sb("rs_mem", [128, MEM])
    gmem = k.sb("gmem_sb", [128, 4, 8])
    k.dma(gmem[:], gmem_d[:, :, :])
    gpre = k.sb("gpre_sb", [128, 4, 8])
    gpost = k.sb("gpost_sb", [128, 4, 8])
    k.dma(gpre[:], gpre_d[:, :, :])
    k.dma(gpost[:], gpost_d[:, :, :])
    hT = [k.sb("hT%d" % c, [128, ST], BF16) for c in range(8)]
    mixT = [k.sb("mixT%d" % c, [128, ST], BF16) for c in range(12)]
    wst = [k.sb("wst%d" % i, [128, 8 * 128]) for i in range(2)]
    wbf = [k.sb("wbf%d" % i, [128, 8 * 128], BF16) for i in range(3)]
    F = [k.sb("F%d" % i, [128, ST + 3]) for i in range(5)]
    FA, FB, FC, FD, FE = F
    hbig = k.es.enter_context(nc.sbuf_tensor("hbig", [128, 12288], BF16))
    HA = Tl(hbig[:, 0:4096], "HA")
    HB = Tl(hbig[:, 4096:8192], "HB")
    HC = Tl(hbig[:, 8192:10240], "HC")
    HDt = Tl(hbig[:, 10240:12288], "HD")
    WOUT = V(hbig[:, :].rearrange("p (k c) -> p k c", k=12), [HA.b, HB.b, HC.b, HDt.b])
    Pt = [k.sb("Pt%d" % i, [128, 512], BF16) for i in range(2)]
    rs = k.sb("rs", [128, 512])
    rs2 = k.sb("rs2", [128, 512])

    plan = []
    for l in range(NL):
        j = l // 2
        for s in range(NST):
            if l % 2 == 0:
                for c in range(64):
                    plan.append((weven_d.v(weven_d.t[j, c]), 8))
            else:
                for c in range(24):
                    plan.append((wodd_d.v(wodd_d.t[j, c]), 8))
    W = WStream(k, plan, wst, wbf)

    def mem_phase(l):
        wbf0 = [view(HA, HA.t[:, 0:4096].rearrange("p (c m) -> p c m", c=8)),
                view(HB, HB.t[:, 0:4096].rearrange("p (c m) -> p c m", c=8))]
        k.dma(view(FA, FA.t[:, 0:2048].rearrange("p (c m) -> p c m", c=8)), memT_d[:, :, :])
        if l == 0:
            k.actf(FB[:, 0:2048], FA[:, 0:2048], AF.Square)
            for kk in range(8):
                k.mm(PS[6][:, 0:MEM], ones_f, FB[:, kk * MEM:(kk + 1) * MEM], start=(kk == 0), stop=(kk == 7))
            k.actf(rs_mem[:], PS[6][:, 0:MEM], AF.Sqrt, bias=EPS, scale=1.0 / D)
            k.recip(rs_mem[:], rs_mem[:])
        for c8 in range(8):
            stg = (FD, FE)[c8 % 2]
            sv = stg.v(stg.t[:, 0:1024].rearrange("p (k m) -> p k m", k=8))
            k.dma(sv, wkv_d[l, :, :, c8 * 128:(c8 + 1) * 128])
            dst = wbf0[c8 // 4]
            k.cp(V(dst.ap[:, :, (c8 % 4) * 128:(c8 % 4 + 1) * 128], dst.bufs), sv)
        for kk in range(8):
            k.stt(HC[:, kk * MEM:(kk + 1) * MEM], FA[:, kk * MEM:(kk + 1) * MEM],
                  gmem[:, l, kk:kk + 1], rs_mem[:], ALU.mult, ALU.mult)
        for half in range(2):
            wb = [HA, HB][half]
            for c in range(4):
                pb = PS[c % 2]
                for kk in range(8):
                    k.mm(pb[:, 0:MEM], wb[:, kk * 512 + c * 128:kk * 512 + (c + 1) * 128],
                         HC[:, kk * MEM:(kk + 1) * MEM], start=(kk == 0), stop=(kk == 7))
                cc = half * 4 + c
                k.cp(FC[:, cc * MEM:(cc + 1) * MEM], pb[:, 0:MEM])
                if half == 0:
                    k.acp(kT_mem1[:, c, :], pb[:, 0:MEM])
            if half == 1:
                for jb in range(2):
                    pb = PS[2 + jb]
                    for kk in range(8):
                        k.mm(pb[:, :], HC[:, kk * MEM + jb * 128:kk * MEM + (jb + 1) * 128],
                             wb[:, kk * 512:(kk + 1) * 512], start=(kk == 0), stop=(kk == 7))
                    k.acp(v_mem1[:, jb, :], pb[:, :])
        k.dma(memo_d.v(memo_d.t[l].rearrange("c p m -> p c m")),
              view(FC, FC.t[:, 0:2048].rearrange("p (c m) -> p c m", c=8)), q=k.pool, is_output=True)

    hs = k.sb("hs_sb", [128, 8, 4], BF16)
    sproj = k.sb("sproj", [128, 64, 4])
    sctx = {"s": 0, "slot": 0, "on": cfg.get("sample", True)}

    def getw():
        wv = W.get()
        if sctx["on"] and sctx["s"] == 0:
            sl = sctx["slot"]
            sctx["slot"] += 1
            for kk in range(8):
                k.mm(PS[7][:, 0:4], view_w(wv, kk), hs[:, kk, :], start=(kk == 0), stop=(kk == 7))
            k.cp(sproj[:, sl, :], PS[7][:, 0:4])
        return wv

    proj_rot = [0]

    def proj(wv, tb):
        pb = PS[proj_rot[0] % 2]
        proj_rot[0] += 1
        for kk in range(8):
            k.mm(pb[:, :], view_w(wv, kk), hT[kk][:, tb * 512:(tb + 1) * 512], start=(kk == 0), stop=(kk == 7))
        return pb

    def view_w(wv, kk):
        return V(wv.ap[:, kk, :], wv.bufs)

    def rstd_act(out, ps, scale):
        k.actf(out, ps, AF.Ln, bias=EPS, scale=scale)
        k.actf(out, out, AF.Exp, scale=-0.5)

    def blk(tl, tb, off=0, n=512):
        return tl[:, off + tb * n: off + (tb + 1) * n]

    def norm_phase(l, s, xsrc):
        xbufs = ((FA, FB), (FC, FD))

        def xload(tb_):
            t0_ = s * ST + tb_ * 512
            for hf, Fx in enumerate(xbufs[tb_ % 2]):
                k.dma(view(Fx, Fx.t[:, 0:2048].rearrange("p (c t) -> p c t", c=4)),
                      xsrc.v(xsrc.t[hf * 4:(hf + 1) * 4, :, t0_:t0_ + 512].rearrange("c p t -> p c t")))
        xload(0)
        for tb in range(4):
            if tb + 1 < 4:
                xload(tb + 1)
            XA, XB = xbufs[tb % 2]
            k.actf(HC[:, 0:2048], XA[:, 0:2048], AF.Square)
            k.actf(HDt[:, 0:2048], XB[:, 0:2048], AF.Square)
            for c in range(8):
                Hx = (HC, HDt)[c // 4]
                k.mm(PS[6][:, :], ones_b[:], blk(Hx, c % 4), start=(c == 0), stop=(c == 7))
            rstd_act(rs[:], PS[6][:, :], 1.0 / D)
            for c in range(8):
                Fx = (XA, XB)[c // 4]
                k.stt(blk(hT[c], tb), blk(Fx, c % 4), gpre[:, l, c:c + 1], rs[:], ALU.mult, ALU.mult)

    def mem_item(l, h, mix_idx):
        wq = getw()
        for tb in range(4):
            pb = proj(wq, tb)
            k.acp(blk(HC, tb), pb[:, :])
        wg = getw()
        for tb in range(4):
            pb = proj(wg, tb)
            k.actf(blk(FA, tb), pb[:, :], AF.Silu)
        def mscores(tb_):
            banks = ((PS[2], PS[3]), (PS[6], PS[7]))[tb_ % 2]
            for c in range(2):
                k.mm(banks[c][:, :], kT_mem[l][:, h, c * 128:(c + 1) * 128], blk(HC, tb_))
        def mepi(tb_):
            nb_, db_ = ((PS[4], PS[5]), (PS[0], PS[1]))[tb_ % 2]
            k.actf(rs2[:], db_[:, :], AF.Ln)
            k.actf(rs2[:], rs2[:], AF.Exp, scale=-1.0)
            k.tt(rs2[:], rs2[:], nb_[:, :], ALU.mult)
            k.tt(blk(mixT[mix_idx], tb_), rs2[:], blk(FA, tb_), ALU.mult)

        mscores(0)
        for tb in range(4):
            banks = ((PS[2], PS[3]), (PS[6], PS[7]))[tb % 2]
            nb_, db_ = ((PS[4], PS[5]), (PS[0], PS[1]))[tb % 2]
            if tb + 1 < 4:
                mscores(tb + 1)
            for c in range(2):
                k.actf(Pt[c][:], banks[c][:, :], AF.Exp, scale=SCALE)
            for c in range(2):
                k.mm(nb_[:, :], v_mem[l][:, c, h * 128:(h + 1) * 128], Pt[c][:], start=(c == 0), stop=(c == 1))
            for c in range(2):
                k.mm(db_[:, :], ones_b[:], Pt[c][:], start=(c == 0), stop=(c == 1))
            if tb >= 1:
                mepi(tb - 1)
        mepi(3)

    def swa_item(l, s, h):
        j = l // 2
        acc_n, acc_d, gate = FB, FC, FD
        for g in range(3):
            win, d = GROUPS[g]
            span = win

            def rm_out(tl, off, tb):
                if d == 1:
                    return tl[:, off + tb * 512: off + (tb + 1) * 512]
                if d == 4:
                    return tl.v(tl.t[:, off + tb * 512: off + (tb + 1) * 512].rearrange("p (r i) -> p r i", r=4))
                return tl.v(tl.t[:, off:off + 2048].rearrange("p (r i) -> p r i", r=16)[:, :, tb * 32:(tb + 1) * 32])

            def rm_in(pb):
                if d == 1:
                    return pb[:, :]
                return pb.v(pb.t[:, :].rearrange("p (i r) -> p r i", r=d))

            wq = getw()
            for tb in range(4):
                pb = proj(wq, tb)
                k.acp(rm_out(HC, 0, tb), rm_in(pb))
            if s == 0:
                k.memset(HA[:, 0:span], 0.0)
            else:
                k.dma(HA[:, 0:span], histk_d[g][h][:, :])
            wk = getw()
            for tb in range(4):
                pb = proj(wk, tb)
                k.acp(rm_out(HA, span, tb), rm_in(pb))
                if s == NST - 1:
                    k.cp(blk(FA, tb), pb[:, :])
            if s == NST - 1:
                k.dma(oswa_d[g][j, 0, h, :, :], FA[:, ST - win:ST], q=k.pool, is_output=True)
            if s == 0:
                k.dma(histk_d[g][h][:, :], HA[:, ST:ST + span], q=k.pool)
            if s == 0:
                k.memset(HB[:, 0:d * 128], 0.0)
            else:
                k.dma(HB.v(HB.t[:, 0:d * 128].rearrange("p (b c) -> p b c", b=d)), histv_d[g][h][:, :, :])
            wv = getw()
            for tb in range(4):
                pb = proj(wv, tb)
                k.acp(rm_out(HDt, 0, tb), rm_in(pb))
                if s == NST - 1:
                    k.cp(blk(FE, tb), pb[:, :])
            if s == NST - 1:
                k.dma(oswa_d[g][j, 1, h, :, :], FE[:, ST - win:ST], q=k.pool, is_output=True)
            for b4 in range(4):
                for q in range(4):
                    bi = b4 * 4 + q
                    k.tr(view(PS[6], PSb6.ap[:, q * 128:(q + 1) * 128]), HDt[:, bi * 128:(bi + 1) * 128], ident_b[:])
                k.cp(HB[:, (d + b4 * 4) * 128:(d + b4 * 4 + 4) * 128], view(PS[6], PSb6.ap[:, 0:512]))
            if s == 0:
                k.dma(histv_d[g][h][:, :, :], HB.v(HB.t[:, 16 * 128:(16 + d) * 128].rearrange("p (b c) -> p b c", b=d)),
                      q=k.pool)
            def scores(rd_):
                pa, pb_ = ((PS[2], PS[3]), (PS[6], PS[7]))[rd_ % 2]
                for q in range(4):
                    bi = rd_ * 4 + q
                    qa = HC[:, bi * 128:(bi + 1) * 128]
                    kprev = HA[:, bi * 128:(bi + 1) * 128]
                    kcur = HA[:, span + bi * 128: span + (bi + 1) * 128]
                    k.mm(pa[:, q * 128:(q + 1) * 128], kprev, qa)
                    k.mm(pb_[:, q * 128:(q + 1) * 128], kcur, qa)
            scores(0)
            for rd in range(4):
                pa, pb_ = ((PS[2], PS[3]), (PS[6], PS[7]))[rd % 2]
                if rd + 1 < 4:
                    scores(rd + 1)
                k.actf(Pt[0][:], pa[:, :], AF.Exp, scale=SCALE)
                k.actf(Pt[1][:], pb_[:, :], AF.Exp, scale=SCALE)
                if s == 0 and ((d == 1 and rd == 0)):
                    mp = mP_f1
                elif s == 0 and ((d == 4 and rd == 0) or d == 16):
                    mp = mP_zero
                else:
                    mp = mP_gen
                k.tt(Pt[0][:], Pt[0][:], mp.v(mp.t[:, :, :].rearrange("p a b -> p (a b)")), ALU.mult)
                k.tt(Pt[1][:], Pt[1][:], mC_gen.v(mC_gen.t[:, :, :].rearrange("p a b -> p (a b)")), ALU.mult)
                for q in range(4):
                    bi = rd * 4 + q
                    vprev = HB[:, bi * 128:(bi + 1) * 128]
                    vcur = HB[:, (d + bi) * 128:(d + bi + 1) * 128]
                    k.mm(PS[4][:, q * 128:(q + 1) * 128], vprev, Pt[0][:, q * 128:(q + 1) * 128], start=True, stop=False)
                    k.mm(PS[4][:, q * 128:(q + 1) * 128], vcur, Pt[1][:, q * 128:(q + 1) * 128], start=False, stop=True)
                k.mm(PS[5][:, :], ones_b[:], Pt[0][:], start=True, stop=False)
                k.mm(PS[5][:, :], ones_b[:], Pt[1][:], start=False, stop=True)
                for accb, pbank in ((acc_n, PS[4]), (acc_d, PS[5])):
                    if d == 1:
                        dst = accb[:, rd * 512:(rd + 1) * 512]
                        src = pbank[:, :]
                    elif d == 4:
                        dst = accb.v(accb.t[:, rd * 512:(rd + 1) * 512].rearrange("p (i r) -> p r i", r=4))
                        src = pbank.v(pbank.t[:, :].rearrange("p (r i) -> p r i", r=4))
                    else:
                        dst = accb.v(accb.t[:, 0:2048].rearrange("p (i r) -> p r i", r=16)[:, rd * 4:(rd + 1) * 4, :])
                        src = pbank.v(pbank.t[:, :].rearrange("p (r i) -> p r i", r=4))
                    if g == 0:
                        k.cp(dst, src)
                    else:
                        k.tt(dst, dst, src, ALU.add)
        wg = getw()
        for tb in range(4):
            pb = proj(wg, tb)
            k.actf(blk(gate, tb), pb[:, :], AF.Silu)
        k.actf(acc_d[:, 0:ST], acc_d[:, 0:ST], AF.Ln)
        k.actf(acc_d[:, 0:ST], acc_d[:, 0:ST], AF.Exp, scale=-1.0)
        k.tt(acc_n[:, 0:ST], acc_n[:, 0:ST], acc_d[:, 0:ST], ALU.mult)
        k.tt(mixT[4 + h][:, :], acc_n[:, 0:ST], gate[:, 0:ST], ALU.mult)


    gs = {nm: k.sb("g_" + nm, [128, 64]) for nm in ("beta", "g", "gc", "ngc", "bg", "egl", "glb", "egs", "tmp")}
    wba_f = k.sb("wba_sf", [128, 8, 8])
    wba_b = k.sb("wba_b", [128, 8, 8], BF16)
    gconvw = k.sb("gconvw_sb", [128, 12, 4])
    alog = k.sb("alog_sb", [128, 64])
    dtb = k.sb("dtb_sb", [128, 64])
    negA = k.sb("negA", [128, 64])
    gnorm = k.sb("gnorm_sb", [128, 1])
    halo = k.sb("halo", [128, 12, 3])
    S = k.sb("S", [128, 4, 128])
    Sb = k.sb("Sb", [128, 4, 128], BF16)
    alt = k.es.enter_context(nc.sbuf_tensor("alt", [128, 3072], F32))
    qt = {}
    for i, nm in enumerate(("A", "Al", "T", "M", "X", "Kbg", "Vb", "EGB", "WT0", "WT1", "qg0", "qg1")):
        qt[nm] = Tl(alt[:, i * 256:(i + 1) * 256].bitcast(BF16), "q_" + nm)
    allb = [t.b for t in qt.values()]
    fa_names, fe_names = [], []
    for nm, lo, hi, dt_ in (("gRep", 0, 512, F32), ("D1", 512, 1024, F32), ("D2", 1024, 1536, F32),
                            ("attnT0", 1536, 1792, BF16), ("attnT1", 1792, 2048, BF16)):
        ap_ = FA.t[:, lo:hi]
        qt[nm] = Tl(ap_.bitcast(BF16) if dt_ == BF16 else ap_, "q_" + nm)
        fa_names.append(nm)
    for nm, lo, hi, dt_ in (("U0", 0, 512, F32), ("U1", 512, 1024, F32), ("kdec0", 1024, 1280, BF16),
                            ("kdec1", 1280, 1536, BF16), ("vnew0", 1536, 1600, BF16), ("vnew1", 1600, 1664, BF16)):
        ap_ = FE.t[:, lo:hi]
        qt[nm] = Tl(ap_.bitcast(BF16) if dt_ == BF16 else ap_, "q_" + nm)
        fe_names.append(nm)
    fa_bufs = [qt[n].b for n in fa_names]
    fe_bufs = [qt[n].b for n in fe_names]
    ALV = [Tl(hbig[:, i * 512:(i + 1) * 512], "alv%d" % i) for i in range(6)]
    alv_bufs = [t.b for t in ALV]
    cTRIU = cst[:, C_TRIU, :]

    def even_init(l):
        j = l // 2
        k.dma(wba_f[:], wba_d[j, :, :, :])
        k.cp(wba_b[:], wba_f[:])
        k.dma(gconvw[:], gconvw_d[j, :, :, :])
        k.dma(alog[:], alog_d[j, :, :])
        k.dma(dtb[:], dtb_d[j, :, :])
        k.dma(gnorm[:], gnorm_d[j, :, :])
        k.actf(negA[:], alog[:], AF.Exp)
        k.ts(negA[:], negA[:], -1.0, ALU.mult)
        k.memset(halo[:], 0.0)
        k.memset(S[:], 0.0)
        k.memset(Sb[:], 0.0)

    def gdn_prep():
        for bl in range(16):
            for kk in range(8):
                k.mm(PS[7][:, bl * 8:(bl + 1) * 8], hT[kk][:, bl * 128:(bl + 1) * 128], wba_b[:, kk, :],
                     start=(kk == 0), stop=(kk == 7))
        ba = PS[7].t[:, 0:128].rearrange("p (b c) -> p b c", c=8)

        def g3(tl):
            return tl.v(tl.t[:, :].rearrange("p (b c) -> p b c", c=4))
        k.actf(g3(gs["beta"]), PS[7].v(ba[:, :, 0:4]), AF.Sigmoid)
        k.tt(g3(gs["tmp"]), PS[7].v(ba[:, :, 4:8]), g3(dtb), ALU.add)
        k.actf(gs["tmp"][:], gs["tmp"][:], AF.Exp)
        k.actf(gs["tmp"][:], gs["tmp"][:], AF.Ln, bias=1.0)
        k.tt(gs["g"][:], gs["tmp"][:], negA[:], ALU.mult)
        k.mm(PS[7][:, 128:192], cTRIU, gs["g"][:])
        k.cp(gs["gc"][:], PS[7][:, 128:192])
        k.ts(gs["ngc"][:], gs["gc"][:], -1.0, ALU.mult)
        k.mm(PS[7][:, 192:256], cst[:, C_SEL, :], gs["gc"][:])
        k.cp(gs["glb"][:], PS[7][:, 192:256])
        k.actf(gs["egs"][:], gs["glb"][:], AF.Exp)
        k.tt(gs["tmp"][:], gs["glb"][:], gs["gc"][:], ALU.subtract)
        k.actf(gs["egl"][:], gs["tmp"][:], AF.Exp)
        k.actf(gs["tmp"][:], gs["gc"][:], AF.Exp)
        k.tt(gs["bg"][:], gs["tmp"][:], gs["beta"][:], ALU.mult)

    rs4 = [Tl(hbig[:, i * 1024:(i + 1) * 1024].bitcast(F32), "rs4_%d" % i) for i in range(4)]
    for t_ in rs4:
        t_.b = HA.b

    def gdn_item(l, s, h):
        j = l // 2
        dsts = (FB, FC, FD)
        hdst = (HC, HDt, None)
        for t in range(3):
            ci = t * 4 + h
            wv = getw()
            Fp = (FA, FE, FA)[t]
            k.cp(Fp[:, 0:3], halo[:, ci, :])
            for tb in range(4):
                pb = proj(wv, tb)
                k.acp(blk(Fp, tb, off=3), pb[:, :])
            k.cp(halo[:, ci, :], Fp[:, ST:ST + 3])
            Fd = dsts[t]
            k.ts(Fd[:, 0:ST], Fp[:, 0:ST], gconvw[:, ci, 0:1], ALU.mult)
            for jj in range(1, 4):
                k.stt(Fd[:, 0:ST], Fp[:, jj:jj + ST], gconvw[:, ci, jj:jj + 1], Fd[:, 0:ST], ALU.mult, ALU.add)
            k.actf(Fd[:, 0:ST], Fd[:, 0:ST], AF.Silu)
            if t < 2:
                k.actf(HB[:, 0:ST], Fd[:, 0:ST], AF.Square)
                for tb in range(4):
                    k.mm(PS[2 + tb][:, :], ones_b[:], blk(HB, tb))
                for tb in range(4):
                    rstd_act(rs4[tb][:], PS[2 + tb][:, :], 1.0)
                for tb in range(4):
                    k.stt(blk(hdst[t], tb), blk(Fd, tb), (SCALE if t == 0 else 1.0), rs4[tb][:], ALU.mult, ALU.mult)

        def q3(tl):
            return tl.v(tl.t[:, :].rearrange("p (q c) -> p q c", q=4))

        def bc_in(nm, Q):
            t = gs[nm]
            ap_ = t.t[:, 16 * Q:16 * Q + 16].rearrange("p (q h) -> p q h", h=4)[:, :, h]
            return t.v(ap_.unsqueeze(2).to_broadcast([128, 4, 128]))

        def bc_mid(slot):
            return cst.v(cst.t[:, slot, :].unsqueeze(1).to_broadcast([128, 4, 128]))

        def prep(Q):
            par = Q % 2
            WT, qg, attnT, kdec, U = (qt["WT%d" % par], qt["qg%d" % par], qt["attnT%d" % par], qt["kdec%d" % par],
                                      qt["U%d" % par])
            A, Al, T, M, X, Kbg, Vb, EGB = (qt[n] for n in ("A", "Al", "T", "M", "X", "Kbg", "Vb", "EGB"))
            gRep, D1, D2 = qt["gRep"], qt["D1"], qt["D2"]
            c0 = Q * 512
            PSb4 = PS[4].v(PS[4].t[:, :].bitcast(BF16)[:, 0:512])
            blks = [(q_, c0 + q_ * 128, c0 + (q_ + 1) * 128) for q_ in range(4)]
            sl = lambda tl, q_: tl[:, q_ * 128:(q_ + 1) * 128]
            for q_, a0, a1 in blks:
                k.tr(V(PSb4.ap[:, q_ * 128:(q_ + 1) * 128], PSb4.bufs), HDt[:, a0:a1], ident_b[:])
            for q_, a0, a1 in blks:
                k.tr(sl(PS[5], q_), FD[:, a0:a1], ident_f)
            k.tt(q3(gRep), bc_mid(C_ONE), bc_in("g", Q), ALU.mult)
            yield
            k.tt(q3(Kbg), V(PSb4.ap.rearrange("p (q c) -> p q c", q=4), PSb4.bufs), bc_in("bg", Q), ALU.mult)
            k.tt(q3(kdec), V(PSb4.ap.rearrange("p (q c) -> p q c", q=4), PSb4.bufs), bc_in("egl", Q), ALU.mult)
            k.tt(q3(Vb), q3(PS[5]), bc_in("beta", Q), ALU.mult)
            for q_, a0, a1 in blks:
                k.mm(sl(PS[1], q_), sl(gRep, q_), cTRIU)
            yield
            k.actf(EGB[:, :], PS[1][:, :], AF.Exp)
            k.stt(q3(D1), q3(PS[1]), -1.0, bc_mid(C_NSL), ALU.mult, ALU.add)
            k.tt(q3(D2), q3(PS[1]), bc_mid(C_NIU), ALU.add)
            for q_, a0, a1 in blks:
                k.mm(sl(PS[7], q_), HDt[:, a0:a1], HDt[:, a0:a1])
            for q_, a0, a1 in blks:
                k.mm(sl(PS[5], q_), HDt[:, a0:a1], HC[:, a0:a1])
            yield
            k.tt(q3(D1), q3(D1), bc_in("gc", Q), ALU.add)
            k.tt(q3(D2), q3(D2), bc_in("ngc", Q), ALU.add)
            k.tt(qg[:, :], HC[:, c0:c0 + 512], EGB[:, :], ALU.mult)
            yield
            k.actf(D1[:, :], D1[:, :], AF.Exp)
            k.actf(D2[:, :], D2[:, :], AF.Exp)
            yield
            k.tt(D1[:, :], D1[:, :], PS[7][:, :], ALU.mult)
            k.tt(q3(A), q3(D1), bc_in("beta", Q), ALU.mult)
            k.tt(attnT[:, :], PS[5][:, :], D2[:, :], ALU.mult)
            yield
            k.tt(q3(Al), q3(A), bc_mid(C_NM1), ALU.mult)
            k.tt(q3(T), q3(Al), bc_mid(C_ID), ALU.add)
            for lv in range(2, 8):
                k.tt(q3(ALV[lv - 2]), q3(A), bc_mid(C_LV + lv - 2), ALU.mult)
            yield
            for q_, a0, a1 in blks:
                k.tr(V(PSb4.ap[:, q_ * 128:(q_ + 1) * 128], PSb4.bufs), sl(T, q_), ident_b[:])
            yield
            k.acp(M[:, :], PSb4)
            for lv in range(2, 8):
                for q_, a0, a1 in blks:
                    k.mm(sl(PS[2], q_), sl(ALV[lv - 2], q_), sl(M, q_))
                yield
                k.acp(X[:, :], PS[2][:, :])
                yield
                for q_, a0, a1 in blks:
                    k.mm(sl(PS[3], q_), sl(T, q_), sl(X, q_))
                yield
                k.tt(M[:, :], M[:, :], PS[3][:, :], ALU.subtract)
                yield
                if lv < 7:
                    for q_, a0, a1 in blks:
                        k.tr(V(PSb4.ap[:, q_ * 128:(q_ + 1) * 128], PSb4.bufs), sl(M, q_), ident_b[:])
                    yield
                    k.acp(T[:, :], PSb4)
                    yield
            for q_, a0, a1 in blks:
                k.mm(sl(PS[2], q_), sl(M, q_), sl(Vb, q_))
            for q_, a0, a1 in blks:
                k.mm(sl(PS[3], q_), sl(Kbg, q_), sl(M, q_))
            yield
            k.acp(U[:, :], PS[2][:, :])
            k.acp(WT[:, :], PS[3][:, :])
            yield

        def seq(Q):
            par = Q % 2
            WT, qg, attnT, kdec, U = (qt["WT%d" % par], qt["qg%d" % par], qt["attnT%d" % par], qt["kdec%d" % par],
                                      qt["U%d" % par])
            sl = lambda tl, q_: tl[:, q_ * 128:(q_ + 1) * 128]
            for q_ in range(4):
                b_ = Q * 4 + q_
                col = b_ * 4 + h
                c0, c1 = b_ * 128, (b_ + 1) * 128
                vnew = qt["vnew%d" % (q_ % 2)]
                k.mm(PS[0][:, 0:128], sl(WT, q_), Sb[:, h, :])
                yield
                k.tt(vnew[:, :], sl(U, q_), PS[0][:, 0:128], ALU.subtract)
                yield
                k.mm(PS[0][:, 128:256], Sb[:, h, :], sl(qg, q_), start=True, stop=False)
                k.mm(PS[0][:, 128:256], vnew[:, :], sl(attnT, q_), start=False, stop=True)
                k.mm(PS[6][:, 0:128], sl(kdec, q_), vnew[:, :])
                yield
                k.stt(Sb[:, h, :], S[:, h, :], gs["egs"][:, col:col + 1], PS[6][:, 0:128], ALU.mult, ALU.add)
                k.stt(S[:, h, :], S[:, h, :], gs["egs"][:, col:col + 1], PS[6][:, 0:128], ALU.mult, ALU.add)
                k.acp(FB[:, c0:c1], PS[0][:, 128:256])
                yield

        def run_interleaved(gens):
            gens = [g for g in gens if g is not None]
            while gens:
                for g in list(gens):
                    try:
                        next(g)
                    except StopIteration:
                        gens.remove(g)

        inherit(fa_bufs, [FA.b])
        inherit(fe_bufs, [FE.b])
        inherit(alv_bufs, [HA.b])
        run_interleaved([prep(0)])
        for Q in range(4):
            run_interleaved([prep(Q + 1) if Q + 1 < 4 else None, seq(Q)])
        inherit([FA.b], fa_bufs)
        inherit([FE.b], fe_bufs)
        inherit([HA.b], alv_bufs)
        wz = getw()
        for tb in range(4):
            pb = proj(wz, tb)
            k.actf(blk(FC, tb), pb[:, :], AF.Silu)
        k.actf(HB[:, 0:ST], FB[:, 0:ST], AF.Square)
        for tb in range(4):
            k.mm(PS[2 + tb][:, :], ones_b[:], blk(HB, tb))
        for tb in range(4):
            rstd_act(rs4[tb][:], PS[2 + tb][:, :], 1.0 / 128)
        for tb in range(4):
            k.stt(blk(FB, tb), blk(FB, tb), gnorm[:, 0:1], rs4[tb][:], ALU.mult, ALU.mult)
            k.tt(blk(mixT[h], tb), blk(FB, tb), blk(FC, tb), ALU.mult)
        if s == NST - 1:
            k.dma(ogdn_d[j, h, :, :], S[:, h, :], q=k.pool, is_output=True)
            if h == 3:
                k.dma(ogconv_d[j, :, :, :], halo[:], q=k.pool, is_output=True)

    lconvw = k.sb("lconvw_sb", [128, 8, 4])
    lvec = k.sb("lvec_sb", [128, 4, 8])
    nc8 = k.sb("nc8", [128, 8])
    class _AliasTl:
        def __init__(self, ap, bufs):
            self.t = ap
            self.bufs = bufs

        def __getitem__(self, key):
            return V(self.t[key], self.bufs)

        def v(self, ap):
            return V(ap, self.bufs)
    lwa_b = _AliasTl(alt[:, 0:1024].bitcast(BF16).rearrange("p (a b c) -> p a b c", a=4, b=2), allb)
    lwx_b = _AliasTl(alt[:, 1024:2048].bitcast(BF16).rearrange("p (a b c) -> p a b c", a=4, b=2), allb)
    hstate = k.sb("hstate", [128, 8])
    lhalo = k.sb("lhalo", [128, 8, 3])

    def odd_init(l):
        j = l // 2
        k.dma(lconvw[:], lconvw_d[j, :, :, :])
        k.dma(lvec[:], lvec_d[j, :, :, :])
        k.dma(view(FA, FA.t[:, 0:2048].rearrange("p (a b c) -> p a b c", a=4, b=2)), lwa_d[j, :, :, :, :])
        k.dma(view(FB, FB.t[:, 0:2048].rearrange("p (a b c) -> p a b c", a=4, b=2)), lwx_d[j, :, :, :, :])
        k.cp(view(lwa_b, lwa_b.t[:, :, :, :].rearrange("p a b c -> p (a b c)")), FA[:, 0:2048])
        k.cp(view(lwx_b, lwx_b.t[:, :, :, :].rearrange("p a b c -> p (a b c)")), FB[:, 0:2048])
        k.actf(nc8[:], lvec[:, 3, :], AF.Exp, scale=-1.0)
        k.actf(nc8[:], nc8[:], AF.Ln, bias=1.0)
        k.ts(nc8[:], nc8[:], -8.0, ALU.mult)
        k.memset(hstate[:], 0.0)
        k.memset(lhalo[:], 0.0)

    def lru_item(l, s, n):
        j = l // 2
        xcs = (FB, FC)
        xbs = (HC, HDt)
        for cc in range(2):
            c = 2 * n + cc
            wv = getw()
            k.cp(FA[:, 0:3], lhalo[:, c, :])
            for tb in range(4):
                pb = proj(wv, tb)
                k.acp(blk(FA, tb, off=3), pb[:, :])
            k.cp(lhalo[:, c, :], FA[:, ST:ST + 3])
            Fx = xcs[cc]
            k.ts(Fx[:, 0:ST], FA[:, 0:ST], lconvw[:, c, 0:1], ALU.mult, lvec[:, 0, c:c + 1], ALU.add)
            for jj in range(1, 4):
                k.stt(Fx[:, 0:ST], FA[:, jj:jj + ST], lconvw[:, c, jj:jj + 1], Fx[:, 0:ST], ALU.mult, ALU.add)
            k.acp(xbs[cc][:, 0:ST], Fx[:, 0:ST])
        for oc in range(2):
            c = 2 * n + oc
            for tb in range(4):
                for kc in range(2):
                    k.mm(PS[2][:, :], lwa_b[:, n, kc, oc * 128:(oc + 1) * 128], blk(xbs[kc], tb),
                         start=(kc == 0), stop=(kc == 1))
                k.actf(blk(FD, tb), PS[2][:, :], AF.Sigmoid, bias=lvec[:, 1, c:c + 1])
                for kc in range(2):
                    k.mm(PS[3][:, :], lwx_b[:, n, kc, oc * 128:(oc + 1) * 128], blk(xbs[kc], tb),
                         start=(kc == 0), stop=(kc == 1))
                k.actf(blk(FE, tb), PS[3][:, :], AF.Sigmoid, bias=lvec[:, 2, c:c + 1])
            k.actf(FD[:, 0:ST], FD[:, 0:ST], AF.Exp, scale=nc8[:, c:c + 1])
            k.tt(FA[:, 0:ST], FD[:, 0:ST], FD[:, 0:ST], ALU.mult)
            k.actf(FA[:, 0:ST], FA[:, 0:ST], AF.Sqrt, bias=1.0, scale=-1.0)
            k.tt(FE[:, 0:ST], FE[:, 0:ST], xcs[oc][:, 0:ST], ALU.mult)
            k.tt(FE[:, 0:ST], FE[:, 0:ST], FA[:, 0:ST], ALU.mult)
            k.scan(FA[:, 0:ST], FD[:, 0:ST], FE[:, 0:ST], (hstate[:, c:c + 1] if s > 0 else 0.0))
            k.cp(hstate[:, c:c + 1], FA[:, ST - 1:ST])
            wg = getw()
            for tb in range(4):
                pb = proj(wg, tb)
                k.actf(blk(FD, tb), pb[:, :], AF.Silu)
            k.tt(mixT[c][:, :], FA[:, 0:ST], FD[:, 0:ST], ALU.mult)

    dbg_d = k.dout("dbg_mix", [12, 128, SEQ]) if cfg.get("debug") else None

    def out_phase(l, s, xsrc, xdst, last):
        if dbg_d is not None and l == cfg.get("debug_layer", 0):
            for kk in range(12):
                k.acp(FA[:, 0:ST], mixT[kk][:, :])
                k.dma(dbg_d[kk, :, s * ST:(s + 1) * ST], FA[:, 0:ST], q=k.pool, is_output=True)
        for c in range(8):
            for hf in range(2):
                stg = (FA, FB, FC)[(2 * c + hf) % 3]
                sv = stg.v(stg.t[:, 0:768].rearrange("p (k m) -> p k m", k=6))
                k.dma(sv, wout_d[l, c, :, hf * 6:(hf + 1) * 6, :])
                k.cp(V(WOUT.ap[:, hf * 6:(hf + 1) * 6, c * 128:(c + 1) * 128], WOUT.bufs), sv)
        if sctx["on"] and s == 0:
            sample_out(l, last)
        for tb in range(4):
            t0 = s * ST + tb * 512
            for hf, Fx in enumerate((FD, FE)):
                k.dma(view(Fx, Fx.t[:, 0:2048].rearrange("p (c t) -> p c t", c=4)),
                      xsrc.v(xsrc.t[hf * 4:(hf + 1) * 4, :, t0:t0 + 512].rearrange("c p t -> p c t")))
            for c in range(8):
                pb = PS[c % 2]
                for kk in range(12):
                    k.mm(pb[:, :], V(WOUT.ap[:, kk, c * 128:(c + 1) * 128], WOUT.bufs), blk(mixT[kk], tb),
                         start=(kk == 0), stop=(kk == 11))
                Fy = (FA, FB)[c // 4]
                k.acp(blk(Fy, c % 4), pb[:, :])
                k.tt(Pt[c % 2][:], blk(Fy, c % 4), blk(Fy, c % 4), ALU.mult)
                k.mm(PS[6][:, :], ones_b[:], Pt[c % 2][:], start=(c == 0), stop=(c == 7))
            rstd_act(rs[:], PS[6][:, :], 1.0 / D)
            for c in range(8):
                Fy = (FA, FB)[c // 4]
                Fx = (FD, FE)[c // 4]
                k.stt(blk(Fy, c % 4), blk(Fy, c % 4), gpost[:, l, c:c + 1], rs[:], ALU.mult, ALU.mult)
                k.tt(blk(Fx, c % 4), blk(Fx, c % 4), blk(Fy, c % 4), ALU.add)
            for hf, Fx in enumerate((FD, FE)):
                k.dma(xdst.v(xdst.t[hf * 4:(hf + 1) * 4, :, t0:t0 + 512].rearrange("c p t -> p c t")),
                      view(Fx, Fx.t[:, 0:2048].rearrange("p (c t) -> p c t", c=4)), q=k.pool, is_output=last)


    def sv(tl, ap):
        return tl.v(ap)

    xs = k.sb("xs_sb", [128, 8, 4])
    mix_s = k.sb("mix_s", [128, 12, 4], BF16)
    sgconv_sb = k.sb("sgconv_sb", [128, 12, 4, 3])
    slru_sb = k.sb("slru_sb", [128, 8, 4])
    slconv_sb = k.sb("slconv_sb", [128, 8, 4, 3])
    sT = [k.sb("sT%d" % i, [128, 64]) for i in range(6)]
    srow = k.sb("srow", [128, 128])
    sBeta = k.sb("sBeta", [128, 16])
    sNBeta = k.sb("sNBeta", [128, 16])
    sEg = k.sb("sEg", [128, 16])
    k.dma(xs[:], xsT_d[:, :, :])

    def bulk_copies():
        for g in range(3):
            L = GROUPS[g][0]
            for j in range(2):
                for tok in range(4):
                    r0 = 1
                    while r0 < L:
                        r1 = min(L, r0 + 512)
                        k.bulk_copy(oswas_d[g].t[j, tok, r0 - 1:r1 - 1].rearrange("r a h d -> r (a h d)"),
                                    cswa_d[g].t[j, tok, r0:r1].rearrange("r a h d -> r (a h d)"))
                        r0 = r1
        for j in range(2):
            k.bulk_copy(ogconvs_d.t[j, :, 0:2, :], sgconv_nat.t[j, :, 1:3, :])
            k.bulk_copy(olconvs_d.t[j, :, 0:2, :], slconv_nat.t[j, :, 1:3, :])

    def emit_rows(src, n, dst, is_out=True):
        k.tr(PS[7][0:n, 0:128], src, ident_f)
        k.cp(srow[0:n, :], PS[7][0:n, 0:128])
        k.dma(dst, srow[0:n, :], q=k.pool, is_output=is_out)

    def sample_norm(l):
        k.tt(sT[0][:, 0:32], xs.v(xs.t[:, :, :].rearrange("p c t -> p (c t)")),
             xs.v(xs.t[:, :, :].rearrange("p c t -> p (c t)")), ALU.mult)
        for c in range(8):
            k.mm(PS[7][:, 0:4], ones_f, sT[0][:, c * 4:(c + 1) * 4], start=(c == 0), stop=(c == 7))
        k.actf(sT[1][:, 0:4], PS[7][:, 0:4], AF.Sqrt, bias=EPS, scale=1.0 / D)
        k.recip(sT[1][:, 0:4], sT[1][:, 0:4])
        for c in range(8):
            k.stt(hs[:, c, :], xs[:, c, :], gpre[:, l, c:c + 1], sT[1][:, 0:4], ALU.mult, ALU.mult)

    def zero_bank(pb, n):
        k.mm(pb[:, 0:n], mP_zero[:, 0, :], mP_zero.v(mP_zero.t[:, :, :].rearrange("p a b -> p (a b)")[:, 0:n]),
             start=True, stop=True, sgc=True)

    def s_qB(qcol):
        for h in range(4):
            k.ts(FA[:, h * 128:(h + 1) * 128], ident_f, qcol(h), ALU.mult)
        k.mm(PS[6][:, :], ones_f, FA[:, 0:512])
        k.acp(FA[:, 512:1024], PS[6][:, :])

    kvbuf = ((FB, FC), (FE, FE))

    def s_kvload(i, Kd, Vd):
        kb_, vb_ = kvbuf[i % 2]
        if i % 2 == 0:
            k.dma(kb_[:, 0:512], Kd)
            k.dma(vb_[:, 0:512], Vd)
        else:
            k.dma(kb_[:, 0:512], Kd)
            k.dma(vb_[:, 512:1024], Vd)

    def s_keyblock(i, tok):
        kb_, vb_ = kvbuf[i % 2]
        Kt = kb_[:, 0:512]
        voff = 0 if i % 2 == 0 else 512
        k.tt(FD[:, 0:512], Kt, FA[:, 512:1024], ALU.mult)
        k.reduce_x(sT[2][:, 0:4], sv(FD, FD.t[:, 0:512].rearrange("p (h d) -> p h d", h=4)))
        k.actf(sT[3][:, 0:4], sT[2][:, 0:4], AF.Exp, scale=SCALE)
        for h in range(4):
            k.mm(PS[4][:, tok * 4 + h:tok * 4 + h + 1], vb_[:, voff + h * 128:voff + (h + 1) * 128], sT[3][:, h:h + 1],
                 start=False, stop=True, sgc=True)
        k.mm(PS[5][:, tok * 4:tok * 4 + 4], ones_f, sT[3][:, 0:4], start=False, stop=True, sgc=True)

    def s_run_blocks(blocks):
        if blocks:
            s_kvload(0, blocks[0][2], blocks[0][3])
        for i, (tok, qfn, Kd, Vd) in enumerate(blocks):
            if i + 1 < len(blocks):
                s_kvload(i + 1, blocks[i + 1][2], blocks[i + 1][3])
            if qfn is not None:
                s_qB(qfn)
            s_keyblock(i, tok)

    def s_mem(l):
        even = (l % 2 == 0)
        qs = (lambda h: 2 * h) if even else (lambda h: 16 + 2 * h)
        zero_bank(PS[4], 16)
        zero_bank(PS[5], 16)
        blocks = []
        for tok in range(4):
            for kb in range(2):
                blocks.append((tok, (lambda h, tok=tok: sproj[:, qs(h), tok:tok + 1]) if kb == 0 else None,
                               cmem_d.v(cmem_d.t[l, tok, kb * 128:(kb + 1) * 128, 0].rearrange("r h d -> r (h d)")),
                               cmem_d.v(cmem_d.t[l, tok, kb * 128:(kb + 1) * 128, 1].rearrange("r h d -> r (h d)"))))
        s_run_blocks(blocks)
        k.recip(sT[2][:, 0:16], PS[5][:, 0:16])
        k.tt(sT[2][:, 0:16], sT[2][:, 0:16], PS[4][:, 0:16], ALU.mult)
        for h in range(4):
            k.actf(sT[3][:, 0:4], sproj[:, qs(h) + 1, :], AF.Silu)
            k.tt(mix_s[:, 8 + h, :], sv(sT[2], sT[2].t[:, 0:16].rearrange("p (t h) -> p h t", h=4)[:, h, :]),
                 sT[3][:, 0:4], ALU.mult)

    def s_swa(l):
        j = l // 2
        zero_bank(PS[4], 16)
        zero_bank(PS[5], 16)
        qslot = lambda g, h: 8 + 10 * h + 3 * g
        blocks = []
        for tok in range(4):
            for g in range(3):
                L, d = GROUPS[g]
                cd = cswa_d[g]
                blocks.append((tok, (lambda h, tok=tok, g=g: sproj[:, qslot(g, h), tok:tok + 1]),
                               cd.v(cd.t[j, tok, :, 0].rearrange("(i s) h d -> i s (h d)", s=d)[:, 0, :]),
                               cd.v(cd.t[j, tok, :, 1].rearrange("(i s) h d -> i s (h d)", s=d)[:, 0, :])))
        s_run_blocks(blocks)
        for tok in range(4):
            for g in range(3):
                for h in range(4):
                    c = g * 4 + h
                    k.tt(sT[0][:, c:c + 1], sproj[:, qslot(g, h), tok:tok + 1], sproj[:, qslot(g, h) + 1, tok:tok + 1],
                         ALU.mult)
            k.mm(PS[6][:, 0:12], ones_f, sT[0][:, 0:12])
            k.actf(sT[1][:, 0:12], PS[6][:, 0:12], AF.Exp, scale=SCALE)
            for h in range(4):
                col = tok * 4 + h
                k.cp(sT[4][:, col:col + 1], PS[4][:, col:col + 1])
                k.cp(sT[5][:, col:col + 1], PS[5][:, col:col + 1])
                for g in range(3):
                    c = g * 4 + h
                    k.stt(sT[4][:, col:col + 1], sproj[:, qslot(g, h) + 2, tok:tok + 1], sT[1][:, c:c + 1],
                          sT[4][:, col:col + 1], ALU.mult, ALU.add)
                    k.tt(sT[5][:, col:col + 1], sT[5][:, col:col + 1], sT[1][:, c:c + 1], ALU.add)
            for g in range(3):
                for kv in range(2):
                    for h in range(4):
                        c = g * 8 + kv * 4 + h
                        k.cp(sT[2][:, c:c + 1], sproj[:, qslot(g, h) + 1 + kv, tok:tok + 1])
            k.tr(PS[7][0:24, 0:128], sT[2][:, 0:24], ident_f)
            k.cp(srow[0:24, :], PS[7][0:24, 0:128])
            for g in range(3):
                L = GROUPS[g][0]
                k.dma(oswas_d[g].v(oswas_d[g].t[j, tok, L - 1].rearrange("a h d -> (a h) d")),
                      srow[g * 8:(g + 1) * 8, :], q=k.pool, is_output=True)
        k.recip(sT[5][:, 0:16], sT[5][:, 0:16])
        k.tt(sT[4][:, 0:16], sT[4][:, 0:16], sT[5][:, 0:16], ALU.mult)
        for h in range(4):
            k.actf(sT[3][:, 0:4], sproj[:, 8 + 10 * h + 9, :], AF.Silu)
            k.tt(mix_s[:, 4 + h, :], sv(sT[4], sT[4].t[:, 0:16].rearrange("p (t h) -> p h t", h=4)[:, h, :]),
                 sT[3][:, 0:4], ALU.mult)

    def s_gdn(l):
        j = l // 2
        k.dma(sgconv_sb[:], sgconv_d[j, :, :, :, :])
        for tok in range(4):
            for kk in range(8):
                k.ts(HC[:, kk * 128:(kk + 1) * 128], ones_b[:], hs[:, kk, tok:tok + 1], ALU.mult)
            for kk in range(8):
                k.mm(PS[6][:, tok * 8:(tok + 1) * 8], HC[:, kk * 128:(kk + 1) * 128], wba_b[:, kk, :],
                     start=(kk == 0), stop=(kk == 7))
        ba = PS[6].t[:, 0:32].rearrange("p (t c) -> p t c", c=8)
        g3 = lambda tl: tl.v(tl.t[:, 0:16].rearrange("p (t c) -> p t c", c=4))
        k.actf(g3(sBeta), PS[6].v(ba[:, :, 0:4]), AF.Sigmoid)
        k.ts(sNBeta[:], sBeta[:], -1.0, ALU.mult)
        k.tt(g3(sEg), PS[6].v(ba[:, :, 4:8]), g3(dtb), ALU.add)
        k.actf(sEg[:], sEg[:], AF.Exp)
        k.actf(sEg[:], sEg[:], AF.Ln, bias=1.0)
        k.tt(sEg[:], sEg[:], negA[:, 0:16], ALU.mult)
        k.actf(sEg[:], sEg[:], AF.Exp)
        cq = sv(FD, FD.t[:, 0:48].rearrange("p (c t) -> p c t", t=4))
        for ci in range(12):
            t_, h_ = ci // 4, ci % 4
            xsl = sproj[:, 48 + 4 * h_ + t_, :]
            dst = sv(FD, FD.t[:, ci * 4:(ci + 1) * 4])
            k.ts(dst, sgconv_sb[:, ci, :, 0], gconvw[:, ci, 0:1], ALU.mult)
            for r in (1, 2):
                k.stt(dst, sgconv_sb[:, ci, :, r], gconvw[:, ci, r:r + 1], dst, ALU.mult, ALU.add)
            k.stt(dst, xsl, gconvw[:, ci, 3:4], dst, ALU.mult, ALU.add)
            k.cp(sT[0][:, ci * 4:(ci + 1) * 4], xsl)
        for tok in range(4):
            emit_rows(sv(sT[0], sT[0].t[:, 0:48].rearrange("p (c t) -> p c t", t=4)[:, :, tok]), 12,
                      ogconvs_d.v(ogconvs_d.t[j, tok, 2].rearrange("(c p) -> c p", p=128)))
        k.actf(FD[:, 0:48], FD[:, 0:48], AF.Silu)
        k.tt(sT[1][:, 0:32], FD[:, 0:32], FD[:, 0:32], ALU.mult)
        k.mm(PS[6][:, 64:96], ones_f, sT[1][:, 0:32])
        k.actf(sT[1][:, 0:32], PS[6][:, 64:96], AF.Sqrt, bias=EPS)
        k.recip(sT[1][:, 0:32], sT[1][:, 0:32])
        k.stt(FD[:, 0:16], FD[:, 0:16], SCALE, sT[1][:, 0:16], ALU.mult, ALU.mult)
        k.tt(FD[:, 16:32], FD[:, 16:32], sT[1][:, 16:32], ALU.mult)
        for tok in range(4):
            for h in range(4):
                col = tok * 4 + h
                qc = FD[:, (0 + h) * 4 + tok:(0 + h) * 4 + tok + 1]
                kc = FD[:, (4 + h) * 4 + tok:(4 + h) * 4 + tok + 1]
                vc = FD[:, (8 + h) * 4 + tok:(8 + h) * 4 + tok + 1]
                k.dma(FE[:, 0:128], sgdn_d[j, tok, h, :, :])
                k.mm(PS[6][:, 128:129], FE[:, 0:128], kc)
                k.stt(sT[2][:, 0:1], PS[6][:, 128:129], sEg[:, col:col + 1], vc, ALU.mult, ALU.subtract)
                k.ts(sT[2][:, 0:1], sT[2][:, 0:1], sNBeta[:, col:col + 1], ALU.mult)
                k.ts(FB[:, 0:128], ident_f, sT[2][:, 0:1], ALU.mult)
                k.mm(PS[5][:, 256:384], ones_f, FB[:, 0:128])
                k.ts(FE[:, 0:128], FE[:, 0:128], sEg[:, col:col + 1], ALU.mult)
                k.stt(FE[:, 128:256], PS[5][:, 256:384], kc, FE[:, 0:128], ALU.mult, ALU.add)
                k.dma(ogdns_d[j, tok, h, :, :], FE[:, 128:256], q=k.pool, is_output=True)
                k.mm(PS[6][:, 129:130], FE[:, 128:256], qc)
                k.cp(sT[3][:, col:col + 1], PS[6][:, 129:130])
        k.tt(sT[1][:, 0:16], sT[3][:, 0:16], sT[3][:, 0:16], ALU.mult)
        k.mm(PS[6][:, 64:80], ones_f, sT[1][:, 0:16])
        k.actf(sT[1][:, 0:16], PS[6][:, 64:80], AF.Sqrt, bias=EPS, scale=1.0 / 128)
        k.recip(sT[1][:, 0:16], sT[1][:, 0:16])
        k.stt(sT[3][:, 0:16], sT[3][:, 0:16], gnorm[:, 0:1], sT[1][:, 0:16], ALU.mult, ALU.mult)
        for h in range(4):
            k.actf(sT[2][:, 0:4], sproj[:, 48 + 4 * h + 3, :], AF.Silu)
            k.tt(mix_s[:, h, :], sv(sT[3], sT[3].t[:, 0:16].rearrange("p (t h) -> p h t", h=4)[:, h, :]),
                 sT[2][:, 0:4], ALU.mult)

    def s_lru(l):
        j = l // 2
        k.dma(slru_sb[:], slru_d[j, :, :, :])
        k.dma(slconv_sb[:], slconv_d[j, :, :, :, :])
        xc = lambda c: FD[:, c * 4:(c + 1) * 4]
        xcb = lambda c: HC[:, c * 4:(c + 1) * 4]
        for c in range(8):
            n, cc = c // 2, c % 2
            xsl = sproj[:, 4 * n + cc, :]
            k.ts(xc(c), slconv_sb[:, c, :, 0], lconvw[:, c, 0:1], ALU.mult, lvec[:, 0, c:c + 1], ALU.add)
            for r in (1, 2):
                k.stt(xc(c), slconv_sb[:, c, :, r], lconvw[:, c, r:r + 1], xc(c), ALU.mult, ALU.add)
            k.stt(xc(c), xsl, lconvw[:, c, 3:4], xc(c), ALU.mult, ALU.add)
            k.cp(sT[0][:, c * 4:(c + 1) * 4], xsl)
        for tok in range(4):
            emit_rows(sv(sT[0], sT[0].t[:, 0:32].rearrange("p (c t) -> p c t", t=4)[:, :, tok]), 8,
                      olconvs_d.v(olconvs_d.t[j, tok, 2].rearrange("(c p) -> c p", p=128)))
        k.cp(HC[:, 0:32], FD[:, 0:32])
        for c in range(8):
            n, oc = c // 2, c % 2
            for gi, (wb_, bi) in enumerate(((lwa_b, 1), (lwx_b, 2))):
                for kc in range(2):
                    k.mm(PS[6][:, gi * 32 + c * 4:gi * 32 + (c + 1) * 4], wb_[:, n, kc, oc * 128:(oc + 1) * 128],
                         xcb(2 * n + kc), start=(kc == 0), stop=(kc == 1))
                k.actf(sT[1 + gi][:, c * 4:(c + 1) * 4], PS[6][:, gi * 32 + c * 4:gi * 32 + (c + 1) * 4], AF.Sigmoid,
                       bias=lvec[:, bi, c:c + 1])
            k.actf(sT[1][:, c * 4:(c + 1) * 4], sT[1][:, c * 4:(c + 1) * 4], AF.Exp, scale=nc8[:, c:c + 1])
        k.tt(sT[3][:, 0:32], sT[1][:, 0:32], sT[1][:, 0:32], ALU.mult)
        k.actf(sT[3][:, 0:32], sT[3][:, 0:32], AF.Sqrt, bias=1.0, scale=-1.0)
        k.tt(sT[2][:, 0:32], sT[2][:, 0:32], FD[:, 0:32], ALU.mult)
        k.tt(sT[2][:, 0:32], sT[2][:, 0:32], sT[3][:, 0:32], ALU.mult)
        k.tt(sT[1][:, 0:32], sT[1][:, 0:32], slru_sb.v(slru_sb.t[:, :, :].rearrange("p c t -> p (c t)")), ALU.mult)
        k.tt(sT[1][:, 0:32], sT[1][:, 0:32], sT[2][:, 0:32], ALU.add)
        k.cp(sv(sT[5], sT[5].t[:, 0:32].rearrange("p (t c) -> p t c", c=8)),
             sv(sT[1], sT[1].t[:, 0:32].rearrange("p (c t) -> p t c", t=4)))
        emit_rows(sT[5][:, 0:32], 32, olrus_d.v(olrus_d.t[j].rearrange("t (c p) -> (t c) p", p=128)))
        for c in range(8):
            n, cc = c // 2, c % 2
            k.actf(sT[3][:, 0:4], sproj[:, 4 * n + 2 + cc, :], AF.Silu)
            k.tt(mix_s[:, c, :], sT[1][:, c * 4:(c + 1) * 4], sT[3][:, 0:4], ALU.mult)

    def sample_mixers(l):
        if l % 2 == 0:
            s_mem(l)
            s_swa(l)
            s_gdn(l)
        else:
            s_lru(l)
            s_mem(l)

    def sample_out(l, last):
        for c in range(8):
            for kk in range(12):
                k.mm(PS[7][:, c * 4:(c + 1) * 4], V(WOUT.ap[:, kk, c * 128:(c + 1) * 128], WOUT.bufs), mix_s[:, kk, :],
                     start=(kk == 0), stop=(kk == 11))
        k.cp(sT[0][:, 0:32], PS[7][:, 0:32])
        k.tt(sT[1][:, 0:32], sT[0][:, 0:32], sT[0][:, 0:32], ALU.mult)
        for c in range(8):
            k.mm(PS[7][:, 32:36], ones_f, sT[1][:, c * 4:(c + 1) * 4], start=(c == 0), stop=(c == 7))
        k.actf(sT[2][:, 0:4], PS[7][:, 32:36], AF.Sqrt, bias=EPS, scale=1.0 / D)
        k.recip(sT[2][:, 0:4], sT[2][:, 0:4])
        for c in range(8):
            k.stt(sT[0][:, c * 4:(c + 1) * 4], sT[0][:, c * 4:(c + 1) * 4], gpost[:, l, c:c + 1], sT[2][:, 0:4],
                  ALU.mult, ALU.mult)
            k.tt(xs[:, c, :], xs[:, c, :], sT[0][:, c * 4:(c + 1) * 4], ALU.add)
        if last:
            k.cp(sv(sT[5], sT[5].t[:, 0:32].rearrange("p (t c) -> p t c", c=8)),
                 xs.v(xs.t[:, :, :].rearrange("p c t -> p t c")))
            emit_rows(sT[5][:, 0:32], 32, oys_d.v(oys_d.t[:, :].rearrange("t (c p) -> (t c) p", p=128)))
    SAMPLE = sctx["on"]
    if SAMPLE:
        bulk_copies()
    for l in range(NL):
        last = (l == NL - 1)
        xsrc = xin_d if l == 0 else xs_d[(l - 1) % 2]
        xdst = yT_d if last else xs_d[l % 2]
        mem_phase(l)
        if l % 2 == 0:
            even_init(l)
        else:
            odd_init(l)
        if SAMPLE:
            sample_norm(l)
        for s in range(NST):
            sctx["s"] = s
            if s == 0:
                sctx["slot"] = 0
            norm_phase(l, s, xsrc)
            if l % 2 == 0:
                gdn_prep()
                for kind, h, _ in even_order():
                    if kind == "mem":
                        mem_item(l, h, 8 + h)
                    elif kind == "swa":
                        swa_item(l, s, h)
                    else:
                        gdn_item(l, s, h)
            else:
                for kind, h, _ in odd_order():
                    if kind == "lru":
                        lru_item(l, s, h)
                    else:
                        mem_item(l, h, 8 + h)
                if s == NST - 1:
                    j = l // 2
                    k.dma(olru_d[j, :, :], hstate[:], q=k.pool, is_output=True)
                    k.dma(olconv_d[j, :, :, :], lhalo[:], q=k.pool, is_output=True)
            if SAMPLE and s == 0:
                sample_mixers(l)
            out_phase(l, s, xsrc, xdst, last)
    assert W.n_use == len(plan), (W.n_use, len(plan))
    k.finish()
    return k


def _consts():
    c = np.zeros((128, NCST, 128), np.float32)
    p = np.arange(128)[:, None]
    f = np.arange(128)[None, :]
    c[:, C_ID] = (p == f)
    c[:, C_ONE] = 1.0
    c[:, C_NSL] = np.where(p > f, 0.0, NEG)
    c[:, C_NIU] = np.where(f >= p, 0.0, NEG)
    c[:, C_TRIU] = (p <= f)
    c[:, C_SEL] = (p == 127) * np.ones((1, 128))
    c[:, C_MP] = (p >= f)
    c[:, C_MC] = (p <= f)
    for lv in range(1, 8):
        m = (p > f) & ((p >> lv) == (f >> lv)) & ((p >> (lv - 1)) != (f >> (lv - 1)))
        if lv == 1:
            c[:, C_NM1] = -1.0 * m
        else:
            c[:, C_LV + lv - 2] = m
    return np.ascontiguousarray(c)


def _f(a):
    return np.ascontiguousarray(np.asarray(a, dtype=np.float32))


def _fm(v, n):
    v = _f(v)
    lead = v.shape[:-1]
    v = v.reshape(lead + (n, 128))
    return _f(np.moveaxis(v, -1, 0))


def _tile_cols(Wm, cols, nk):
    out = np.empty((len(cols), 128, nk, 128), np.float32)
    for i, c0 in enumerate(cols):
        out[i] = Wm[:, c0:c0 + 128].reshape(nk, 128, 128).transpose(1, 0, 2)
    return out


def kernel(**inp):
    return kernel_impl(inp)


def kernel_impl(inp, cfg=None):
    cfg = cfg or {}
    NL = cfg.get("nlayers", 4)
    k = build_program(cfg)
    shared = {"cst": _consts()}
    shared["wkv"] = _f(_f(inp["w_mem_kv"]).reshape(4, 8, 128, 1024).transpose(0, 2, 1, 3))
    shared["gmem"] = _fm(inp["mem_norm"], 8)
    shared["gpre"] = _fm(inp["norm_pre"], 8)
    shared["gpost"] = _fm(inp["norm_post"], 8)
    we = _f(inp["w_in_even"])
    wo = _f(inp["w_in_odd"])
    ecols = [c for _, _, cs in even_order() for c in cs]
    ocols = [c for _, _, cs in odd_order() for c in cs]
    shared["w_even"] = np.stack([_tile_cols(we[j], ecols, 8) for j in range(2)])
    shared["w_odd"] = np.stack([_tile_cols(wo[j], ocols, 8) for j in range(2)])
    woe = _f(inp["w_out_even"])
    woo = _f(inp["w_out_odd"])
    shared["w_out"] = np.stack([_tile_cols((woe if l % 2 == 0 else woo)[l // 2], [c * 128 for c in range(8)], 12)
                                for l in range(4)])
    shared["wba"] = _f(we[:, :, 2048:2056].reshape(2, 8, 128, 8).transpose(0, 2, 1, 3))
    shared["gconvw"] = _f(_f(inp["gdn_conv_w"]).transpose(0, 2, 1).reshape(2, 12, 128, 4).transpose(0, 2, 1, 3))
    shared["alog"] = _f(np.broadcast_to(np.tile(_f(inp["gdn_a_log"]), (1, 16))[:, None, :], (2, 128, 64)))
    shared["dtb"] = _f(np.broadcast_to(np.tile(_f(inp["gdn_dt_bias"]), (1, 16))[:, None, :], (2, 128, 64)))
    shared["gnorm"] = _f(_f(inp["gdn_norm"]).reshape(2, 128, 1))
    shared["lconvw"] = _f(_f(inp["lru_conv_w"]).transpose(0, 2, 1).reshape(2, 8, 128, 4).transpose(0, 2, 1, 3))
    lv = np.stack([_f(inp["lru_conv_b"]).reshape(2, 1024), _f(inp["lru_ba"]).reshape(2, 1024),
                   _f(inp["lru_bx"]).reshape(2, 1024), _f(inp["lru_lambda"]).reshape(2, 1024)], axis=1)
    shared["lvec"] = _f(lv.reshape(2, 4, 8, 128).transpose(0, 3, 1, 2))
    shared["lwa"] = _f(_f(inp["lru_wa"]).reshape(2, 4, 2, 128, 256).transpose(0, 3, 1, 2, 4))
    shared["lwx"] = _f(_f(inp["lru_wx"]).reshape(2, 4, 2, 128, 256).transpose(0, 3, 1, 2, 4))
    xp = _f(inp["x_prompt"])
    mp = _f(inp["mem_prompt"])
    SAMPLE = cfg.get("sample", True)
    if SAMPLE:
        xsm = _f(inp["x_sample"]).reshape(32, 8, 128)
        sg = _f(inp["state_gdn"])
        sgc = _f(inp["state_gdn_conv"])
        csw = [_f(inp["cache_swa1"]), _f(inp["cache_swa2"]), _f(inp["cache_swa3"])]
        cme = _f(inp["cache_mem"])
        sl = _f(inp["state_lru"])
        slc = _f(inp["state_lru_conv"])
    in_maps = []
    for c in range(NCORES):
        b = c % 4
        m = dict(shared)
        m["xT"] = _f(xp[b].T.reshape(8, 128, SEQ))
        m["memT"] = _f(mp[b].T.reshape(8, 128, MEM).transpose(1, 0, 2))
        if SAMPLE:
            b0 = c * SPC
            sl_ = slice(b0, b0 + SPC)
            m["xsT"] = _f(xsm[sl_].transpose(2, 1, 0))
            m["sgdn"] = _f(sg[:, sl_])
            m["sgconv"] = _f(sgc[:, sl_].reshape(2, SPC, 3, 12, 128).transpose(0, 4, 3, 1, 2))
            m["sgconv_nat"] = _f(sgc[:, sl_])
            for g in range(3):
                m["cswa%d" % (g + 1)] = _f(csw[g][:, sl_])
            m["cmem"] = _f(cme[:, sl_])
            m["slru"] = _f(sl[:, sl_].reshape(2, SPC, 8, 128).transpose(0, 3, 2, 1))
            m["slconv"] = _f(slc[:, sl_].reshape(2, SPC, 3, 8, 128).transpose(0, 4, 3, 1, 2))
            m["slconv_nat"] = _f(slc[:, sl_])
        in_maps.append(m)
    res = run_bass_kernel_spmd(k.nc, in_maps, core_ids=list(range(NCORES)))
    R = res.results
    B = 4
    y_prompt = np.zeros((B, SEQ, D), np.float32)
    mem_p = np.zeros((4, B, MEM, 2, 4, 128), np.float32)
    gdn_p = np.zeros((2, B, 4, 128, 128), np.float32)
    gconv_p = np.zeros((2, B, 3, 1536), np.float32)
    swa_p = [np.zeros((2, B, GROUPS[g][0], 2, 4, 128), np.float32) for g in range(3)]
    lru_p = np.zeros((2, B, 1024), np.float32)
    lconv_p = np.zeros((2, B, 3, 1024), np.float32)
    for b in range(B):
        r = R[b]
        y_prompt[b] = r["o_yT"].reshape(1024, SEQ).T
        mem_p[:, b] = r["o_memT"].reshape(4, 2, 4, 128, MEM).transpose(0, 4, 1, 2, 3)
        gdn_p[:, b] = r["o_gdn"]
        gconv_p[:, b] = r["o_gconv"].transpose(0, 3, 2, 1).reshape(2, 3, 1536)
        for g in range(3):
            swa_p[g][:, b] = r["o_swa%d" % (g + 1)].transpose(0, 4, 1, 2, 3)
        lru_p[:, b] = r["o_lru"].transpose(0, 2, 1).reshape(2, 1024)
        lconv_p[:, b] = r["o_lconv"].transpose(0, 3, 2, 1).reshape(2, 3, 1024)
    SB = 32
    outs = {
        "y_prompt": y_prompt,
        "gdn_p": gdn_p, "gdn_conv_p": gconv_p,
        "swa1_p": swa_p[0], "swa2_p": swa_p[1], "swa3_p": swa_p[2],
        "lru_p": lru_p, "lru_conv_p": lconv_p, "mem_p": mem_p,
    }
    if cfg.get("debug"):
        outs["dbg_mix"] = R[0]["dbg_mix"].reshape(1536, SEQ).T
    y_sample = np.zeros((SB, 1, D), np.float32)
    gdn_s = np.zeros((2, SB, 4, 128, 128), np.float32)
    gconv_s = np.zeros((2, SB, 3, 1536), np.float32)
    swa_s = [np.zeros((2, SB, GROUPS[g][0], 2, 4, 128), np.float32) for g in range(3)]
    lru_s = np.zeros((2, SB, 1024), np.float32)
    lconv_s = np.zeros((2, SB, 3, 1024), np.float32)
    if SAMPLE:
        for c in range(NCORES):
            r = R[c]
            sl_ = slice(c * SPC, (c + 1) * SPC)
            y_sample[sl_, 0] = r["o_ys"]
            gdn_s[:, sl_] = r["o_gdn_s"]
            gconv_s[:, sl_] = r["o_gconv_s"]
            for g in range(3):
                swa_s[g][:, sl_] = r["o_swa%d_s" % (g + 1)]
            lru_s[:, sl_] = r["o_lru_s"]
            lconv_s[:, sl_] = r["o_lconv_s"]
    if cfg.get("as_dict"):
        outs.update({"y_sample": y_sample, "gdn_s": gdn_s, "gdn_conv_s": gconv_s, "swa1_s": swa_s[0],
                     "swa2_s": swa_s[1], "swa3_s": swa_s[2], "lru_s": lru_s, "lru_conv_s": lconv_s})
        return outs
    return (y_prompt, y_sample, gdn_p, gdn_s, gconv_p, gconv_s, swa_p[0], swa_s[0], swa_p[1], swa_s[1],
            swa_p[2], swa_s[2], lru_p, lru_s, lconv_p, lconv_s, mem_p)
```

```python
import numpy as np
from contextlib import ExitStack
import concourse.bass as bass
import concourse.mybir as mybir
from concourse.bass_utils import run_bass_kernel_spmd

F32 = mybir.dt.float32
BF16 = mybir.dt.bfloat16
AF = mybir.ActivationFunctionType
ALU = mybir.AluOpType
AX = mybir.AxisListType

D = 1024
SEQ = 4096
ST = 2048
NST = SEQ // ST
EPS = 1e-6
MEM = 256
NCORES = 8
SPC = 4


class Buf:
    __slots__ = ("w", "r", "name", "excl")

    def __init__(self, name="", excl=False):
        self.w = {}
        self.r = {}
        self.name = name
        self.excl = excl


def inherit(new_bufs, old_bufs):
    merged = {}
    for ob in old_bufs:
        for d in (ob.w, ob.r):
            for key, (sem, val) in d.items():
                if merged.get(key, (None, 0))[1] < val:
                    merged[key] = (sem, val)
    for nb in new_bufs:
        for key, (sem, val) in merged.items():
            if nb.w.get(key, (None, 0))[1] < val:
                nb.w[key] = (sem, val)


class V:
    __slots__ = ("ap", "bufs")

    def __init__(self, ap, bufs):
        self.ap = ap
        self.bufs = bufs


class Tl:
    def __init__(self, t, name, excl=False):
        self.t = t
        self.b = Buf(name, excl)

    def __getitem__(self, key):
        return V(self.t[key], [self.b])

    def v(self, ap):
        return V(ap, [self.b])


class Eng:
    def __init__(self, k, name, eng, is_pe=False):
        self.k = k
        self.name = name
        self.eng = eng
        self.is_pe = is_pe
        self.sem = k.new_sem("c_" + name)
        self.cnt = 0
        self.seen = {}
        self.nsem = 1

    def wait(self, ev):
        if ev is None:
            return
        sem, val = ev
        if self.seen.get(id(sem), 0) >= val:
            return
        self.eng.wait_ge(sem, val)
        self.seen[id(sem)] = val

    def collect(self, R, W):
        need = {}

        def add(ev, own_ok):
            if ev is None:
                return
            sem, val = ev
            if sem is self.sem and own_ok:
                return
            if need.get(id(sem), (None, 0))[1] < val:
                need[id(sem)] = (sem, val)

        for v in R:
            for b in v.bufs:
                for ev in b.w.values():
                    add(ev, self.is_pe)
                if b.excl:
                    for sem, val in b.r.values():
                        add((sem, val), True)
        for v in W:
            for b in v.bufs:
                for ev in b.w.values():
                    add(ev, self.is_pe)
                for sem, val in b.r.values():
                    add((sem, val), self.is_pe)
        for ev in need.values():
            self.wait(ev)

    def issue(self, fn, R, W):
        self.collect(R, W)
        ins = fn()
        if self.cnt >= 30000:
            self.nsem += 1
            self.sem = self.k.new_sem("c_%s%d" % (self.name, self.nsem))
            self.cnt = 0
        self.cnt += 1
        ins.then_inc(self.sem, 1)
        ev = (self.sem, self.cnt)
        self.k.ninstr += 1
        for v in R:
            for b in v.bufs:
                b.r[id(self.sem)] = ev
        for v in W:
            for b in v.bufs:
                b.w = {id(self.sem): ev}
                b.r = {}
        return ins


class K:
    def __init__(self):
        self.nc = bass.Bass("TRN2", target_bir_lowering=False)
        self.es = ExitStack()
        self.ninstr = 0
        nc = self.nc
        self.pe = Eng(self, "pe", nc.tensor, is_pe=True)
        self.act = Eng(self, "act", nc.scalar)
        self.dve = Eng(self, "dve", nc.vector)
        self.pool = Eng(self, "pool", nc.gpsimd)
        self.sp = Eng(self, "sp", nc.sync)
        self.dsem = {"sp": [[self.new_sem("d%d" % i), 0] for i in range(16)],
                     "pool": [[self.new_sem("e%d" % i), 0] for i in range(12)]}
        self.dnext = {"sp": 0, "pool": 0}
        self.out_events = []
        self.bulk_sem = None
        self.bulk_n = 0
        self.in_names = []
        self.out_names = []

    def new_sem(self, name):
        return self.es.enter_context(self.nc.semaphore(name))

    def sb(self, name, shape, dt=F32):
        return Tl(self.es.enter_context(self.nc.sbuf_tensor(name, list(shape), dt)), name)

    def ps(self, name, shape, dt=F32):
        return Tl(self.es.enter_context(self.nc.psum_tensor(name, list(shape), dt)), name, excl=True)

    def din(self, name, shape, dt=F32):
        self.in_names.append(name)
        return Tl(self.nc.dram_tensor(name, list(shape), dt, kind="ExternalInput"), name)

    def dout(self, name, shape, dt=F32):
        self.out_names.append(name)
        return Tl(self.nc.dram_tensor(name, list(shape), dt, kind="ExternalOutput"), name)

    def dtmp(self, name, shape, dt=F32):
        return Tl(self.nc.dram_tensor(name, list(shape), dt, kind="Internal"), name)

    def dma(self, out, in_, q=None, is_output=False):
        q = q or self.sp
        pool = self.dsem[q.name]
        slot = pool[self.dnext[q.name]]
        self.dnext[q.name] = (self.dnext[q.name] + 1) % len(pool)
        if slot[1]:
            q.wait((slot[0], slot[1]))
        q.collect([in_], [out])
        ins = q.eng.dma_start(out=out.ap, in_=in_.ap)
        slot[1] += 16
        ins.then_inc(slot[0], 16)
        ev = (slot[0], slot[1])
        for b in in_.bufs:
            b.r[id(slot[0])] = ev
        for b in out.bufs:
            b.w = {id(slot[0]): ev}
            b.r = {}
        if is_output:
            self.out_events.append(ev)
        self.ninstr += 1

    def mm(self, out, lhsT, rhs, start=True, stop=True, sgc=False):
        if sgc:
            self.pe.issue(lambda: self.nc.tensor.matmul(out.ap, lhsT=lhsT.ap, rhs=rhs.ap, start=start, stop=stop,
                                                        skip_group_check=True), [lhsT, rhs], [out])
        else:
            self.pe.issue(lambda: self.nc.tensor.matmul(out.ap, lhsT=lhsT.ap, rhs=rhs.ap, start=start, stop=stop),
                          [lhsT, rhs], [out])

    def reduce_x(self, out, in_, op=ALU.add):
        self.dve.issue(lambda: self.nc.vector.tensor_reduce(out=out.ap, in_=in_.ap, axis=AX.X, op=op), [in_], [out])

    def bulk_copy(self, out_ap, in_ap):
        if self.bulk_sem is None:
            self.bulk_sem = self.new_sem("bulk")
        ins = self.nc.scalar.dma_start(out=out_ap, in_=in_ap)
        self.bulk_n += 16
        ins.then_inc(self.bulk_sem, 16)

    def tr(self, out, in_, ident):
        self.pe.issue(lambda: self.nc.tensor.transpose(out.ap, in_.ap, ident.ap), [in_, ident], [out])

    def actf(self, out, in_, func, bias=None, scale=None, eng=None):
        R = [in_]
        kw = {}
        if bias is not None:
            if isinstance(bias, V):
                R.append(bias)
                kw["bias"] = bias.ap
            else:
                kw["bias"] = bias
        if scale is not None:
            if isinstance(scale, V):
                R.append(scale)
                kw["scale"] = scale.ap
            else:
                kw["scale"] = scale
        self.act.issue(lambda: self.nc.scalar.activation(out=out.ap, in_=in_.ap, func=func, **kw), R, [out])

    def _ve(self, eng):
        return eng or self.dve

    def tt(self, out, a, b, op, eng=None):
        e = self._ve(eng)
        e.issue(lambda: e.eng.tensor_tensor(out=out.ap, in0=a.ap, in1=b.ap, op=op), [a, b], [out])

    def ts(self, out, a, s1, op0, s2=None, op1=None, eng=None):
        e = self._ve(eng)
        R = [a]
        s1a = s1
        s2a = s2
        if isinstance(s1, V):
            R.append(s1)
            s1a = s1.ap
        if isinstance(s2, V):
            R.append(s2)
            s2a = s2.ap
        if op1 is None:
            e.issue(lambda: e.eng.tensor_scalar(out=out.ap, in0=a.ap, scalar1=s1a, scalar2=None, op0=op0), R, [out])
        else:
            e.issue(lambda: e.eng.tensor_scalar(out=out.ap, in0=a.ap, scalar1=s1a, scalar2=s2a, op0=op0, op1=op1),
                    R, [out])

    def stt(self, out, a, s, b, op0, op1, eng=None):
        e = self._ve(eng)
        R = [a, b]
        sa = s
        if isinstance(s, V):
            R.append(s)
            sa = s.ap
        e.issue(lambda: e.eng.scalar_tensor_tensor(out=out.ap, in0=a.ap, scalar=sa, in1=b.ap, op0=op0, op1=op1),
                R, [out])

    def cp(self, out, in_, eng=None):
        e = self._ve(eng)
        e.issue(lambda: e.eng.tensor_copy(out=out.ap, in_=in_.ap), [in_], [out])

    def acp(self, out, in_):
        self.actf(out, in_, AF.Copy)

    def recip(self, out, in_):
        self.dve.issue(lambda: self.nc.vector.reciprocal(out=out.ap, in_=in_.ap), [in_], [out])

    def memset(self, out, val, eng=None):
        e = self._ve(eng)
        e.issue(lambda: e.eng.memset(out.ap, val), [], [out])

    def scan(self, out, d0, d1, init, op0=ALU.mult, op1=ALU.add):
        R = [d0, d1]
        ia = init
        if isinstance(init, V):
            R.append(init)
            ia = init.ap
        self.dve.issue(lambda: self.nc.vector.tensor_tensor_scan(out=out.ap, data0=d0.ap, data1=d1.ap, initial=ia,
                                                                 op0=op0, op1=op1), R, [out])

    def finish(self):
        best = {}
        for sem, val in self.out_events:
            if best.get(id(sem), (None, 0))[1] < val:
                best[id(sem)] = (sem, val)
        for ev in best.values():
            self.sp.wait(ev)
        if self.bulk_sem is not None:
            self.sp.wait((self.bulk_sem, self.bulk_n))
        self.es.close()


HD = 128
SCALE = float(128 ** -0.5)
GROUPS = ((128, 1), (512, 4), (2048, 16))
NEG = -30000.0

C_ID, C_ONE, C_NSL, C_NIU, C_TRIU, C_SEL, C_MP, C_MC, C_NM1 = range(9)
C_LV = 9
NCST = 15


def even_order():
    items = []
    for h in range(4):
        items.append(("mem", h, [7176 + h * 128, 7688 + h * 128]))
    for h in range(4):
        cols = []
        for g in range(3):
            for t in range(3):
                cols.append(2056 + (t * 12 + g * 4 + h) * 128)
        cols.append(6664 + h * 128)
        items.append(("swa", h, cols))
    for h in range(4):
        items.append(("gdn", h, [h * 128, 512 + h * 128, 1024 + h * 128, 1536 + h * 128]))
    return items


def odd_order():
    items = []
    for n in range(4):
        items.append(("lru", n, [(2 * n) * 128, (2 * n + 1) * 128, 1024 + (2 * n) * 128, 1024 + (2 * n + 1) * 128]))
    for h in range(4):
        items.append(("mem", h, [2048 + h * 128, 2560 + h * 128]))
    return items


class WStream:
    def __init__(self, k, plan, wst, wbf):
        self.k = k
        self.plan = plan
        self.wst = wst
        self.wbf = wbf
        self.n_dma = 0
        self.n_cast = 0
        self.n_use = 0

    def _dma(self):
        i = self.n_dma
        src, nk = self.plan[i]
        t = self.wst[i % len(self.wst)]
        self.k.dma(t.v(t.t[:, 0:nk * 128].rearrange("p (k m) -> p k m", k=nk)), src)
        self.n_dma += 1

    def _cast(self):
        i = self.n_cast
        src, nk = self.plan[i]
        a = self.wst[i % len(self.wst)]
        b = self.wbf[i % len(self.wbf)]
        self.k.cp(b[:, 0:nk * 128], a[:, 0:nk * 128], eng=self.k.pool)
        self.n_cast += 1

    def get(self):
        i = self.n_use
        n = len(self.plan)
        while self.n_cast < min(n, i + 2):
            while self.n_dma <= self.n_cast:
                self._dma()
            self._cast()
        while self.n_dma < min(n, i + 3):
            self._dma()
        self.n_use += 1
        nk = self.plan[i][1]
        b = self.wbf[i % len(self.wbf)]
        return b.v(b.t[:, 0:nk * 128].rearrange("p (k m) -> p k m", k=nk))


def build_program(cfg):
    NL = cfg.get("nlayers", 4)
    k = K()
    nc = k.nc

    def view(tl, ap):
        return tl.v(ap)

    cst_d = k.din("cst", [128, NCST, 128])
    xin_d = k.din("xT", [8, 128, SEQ])
    memT_d = k.din("memT", [128, 8, MEM])
    wkv_d = k.din("wkv", [4, 128, 8, 1024])
    gmem_d = k.din("gmem", [128, 4, 8])
    gpre_d = k.din("gpre", [128, 4, 8])
    gpost_d = k.din("gpost", [128, 4, 8])
    weven_d = k.din("w_even", [2, 64, 128, 8, 128])
    wodd_d = k.din("w_odd", [2, 24, 128, 8, 128])
    wout_d = k.din("w_out", [4, 8, 128, 12, 128])
    wba_d = k.din("wba", [2, 128, 8, 8])
    gconvw_d = k.din("gconvw", [2, 128, 12, 4])
    alog_d = k.din("alog", [2, 128, 64])
    dtb_d = k.din("dtb", [2, 128, 64])
    gnorm_d = k.din("gnorm", [2, 128, 1])
    lconvw_d = k.din("lconvw", [2, 128, 8, 4])
    lvec_d = k.din("lvec", [2, 128, 4, 8])
    lwa_d = k.din("lwa", [2, 128, 4, 2, 256])
    lwx_d = k.din("lwx", [2, 128, 4, 2, 256])

    memo_d = k.dout("o_memT", [4, 8, 128, MEM])
    yT_d = k.dout("o_yT", [8, 128, SEQ])
    ogdn_d = k.dout("o_gdn", [2, 4, 128, 128])
    ogconv_d = k.dout("o_gconv", [2, 128, 12, 3])
    oswa_d = [k.dout("o_swa%d" % (g + 1), [2, 2, 4, 128, GROUPS[g][0]]) for g in range(3)]
    olru_d = k.dout("o_lru", [2, 128, 8])
    olconv_d = k.dout("o_lconv", [2, 128, 8, 3])

    xsT_d = k.din("xsT", [128, 8, 4])
    sgdn_d = k.din("sgdn", [2, 4, 4, 128, 128])
    sgconv_d = k.din("sgconv", [2, 128, 12, 4, 3])
    sgconv_nat = k.din("sgconv_nat", [2, 4, 3, 1536])
    cswa_d = [k.din("cswa%d" % (g + 1), [2, 4, GROUPS[g][0], 2, 4, 128]) for g in range(3)]
    cmem_d = k.din("cmem", [4, 4, MEM, 2, 4, 128])
    slru_d = k.din("slru", [2, 128, 8, 4])
    slconv_d = k.din("slconv", [2, 128, 8, 4, 3])
    slconv_nat = k.din("slconv_nat", [2, 4, 3, 1024])
    oys_d = k.dout("o_ys", [4, 1024])
    ogdns_d = k.dout("o_gdn_s", [2, 4, 4, 128, 128])
    ogconvs_d = k.dout("o_gconv_s", [2, 4, 3, 1536])
    oswas_d = [k.dout("o_swa%d_s" % (g + 1), [2, 4, GROUPS[g][0], 2, 4, 128]) for g in range(3)]
    olrus_d = k.dout("o_lru_s", [2, 4, 1024])
    olconvs_d = k.dout("o_lconv_s", [2, 4, 3, 1024])

    xs_d = [k.dtmp("xs%d" % i, [8, 128, SEQ]) for i in range(2)]
    histk_d = [[k.dtmp("hk%d_%d" % (g, h), [128, GROUPS[g][0]], BF16) for h in range(4)] for g in range(3)]
    histv_d = [[k.dtmp("hv%d_%d" % (g, h), [128, GROUPS[g][1], 128], BF16) for h in range(4)] for g in range(3)]

    cst = k.sb("cst_sb", [128, NCST, 128])
    k.dma(cst[:], cst_d[:, :, :])
    ident_f = view(cst, cst.t[:, C_ID, :])
    ones_f = view(cst, cst.t[:, C_ONE, :])
    ident_b = k.sb("ident_b", [128, 128], BF16)
    ones_b = k.sb("ones_b", [128, 128], BF16)
    k.cp(ident_b[:], ident_f)
    k.cp(ones_b[:], ones_f)
    mP_gen = k.sb("mP_gen", [128, 4, 128], BF16)
    mP_f1 = k.sb("mP_f1", [128, 4, 128], BF16)
    mP_zero = k.sb("mP_zero", [128, 4, 128], BF16)
    mC_gen = k.sb("mC_gen", [128, 4, 128], BF16)
    k.memset(mP_zero[:], 0.0)
    k.memset(mP_f1[:, 0, :], 0.0)
    for q in range(4):
        k.cp(mP_gen[:, q, :], cst[:, C_MP, :])
        k.cp(mC_gen[:, q, :], cst[:, C_MC, :])
        if q > 0:
            k.cp(mP_f1[:, q, :], cst[:, C_MP, :])

    PS = [k.ps("psb%d" % i, [128, 512]) for i in range(8)]
    PSb6 = view(PS[6], PS[6].t[:, :].bitcast(BF16))

    kT_mem1 = k.sb("kTm", [128, 4, MEM], BF16)
    v_mem1 = k.sb("vm", [128, 2, 512], BF16)
    kT_mem = [kT_mem1] * 4
    v_mem = [v_mem1] * 4
    rs_mem = k.sb("rs_mem", [128, MEM])
    gmem = k.sb("gmem_sb", [128, 4, 8])
    k.dma(gmem[:], gmem_d[:, :, :])
    gpre = k.sb("gpre_sb", [128, 4, 8])
    gpost = k.sb("gpost_sb", [128, 4, 8])
    k.dma(gpre[:], gpre_d[:, :, :])
    k.dma(gpost[:], gpost_d[:, :, :])
    hT = [k.sb("hT%d" % c, [128, ST], BF16) for c in range(8)]
    mixT = [k.sb("mixT%d" % c, [128, ST], BF16) for c in range(12)]
    wst = [k.sb("wst%d" % i, [128, 8 * 128]) for i in range(2)]
    wbf = [k.sb("wbf%d" % i, [128, 8 * 128], BF16) for i in range(3)]
    F = [k.sb("F%d" % i, [128, ST + 3]) for i in range(5)]
    FA, FB, FC, FD, FE = F
    hbig = k.es.enter_context(nc.sbuf_tensor("hbig", [128, 12288], BF16))
    HA = Tl(hbig[:, 0:4096], "HA")
    HB = Tl(hbig[:, 4096:8192], "HB")
    HC = Tl(hbig[:, 8192:10240], "HC")
    HDt = Tl(hbig[:, 10240:12288], "HD")
    WOUT = V(hbig[:, :].rearrange("p (k c) -> p k c", k=12), [HA.b, HB.b, HC.b, HDt.b])
    Pt = [k.sb("Pt%d" % i, [128, 512], BF16) for i in range(2)]
    rs = k.sb("rs", [128, 512])
    rs2 = k.sb("rs2", [128, 512])

    plan = []
    for l in range(NL):
        j = l // 2
        for s in range(NST):
            if l % 2 == 0:
                for c in range(64):
                    plan.append((weven_d.v(weven_d.t[j, c]), 8))
            else:
                for c in range(24):
                    plan.append((wodd_d.v(wodd_d.t[j, c]), 8))
    W = WStream(k, plan, wst, wbf)

    def mem_phase(l):
        wbf0 = [view(HA, HA.t[:, 0:4096].rearrange("p (c m) -> p c m", c=8)),
                view(HB, HB.t[:, 0:4096].rearrange("p (c m) -> p c m", c=8))]
        k.dma(view(FA, FA.t[:, 0:2048].rearrange("p (c m) -> p c m", c=8)), memT_d[:, :, :])
        if l == 0:
            k.actf(FB[:, 0:2048], FA[:, 0:2048], AF.Square)
            for kk in range(8):
                k.mm(PS[6][:, 0:MEM], ones_f, FB[:, kk * MEM:(kk + 1) * MEM], start=(kk == 0), stop=(kk == 7))
            k.actf(rs_mem[:], PS[6][:, 0:MEM], AF.Sqrt, bias=EPS, scale=1.0 / D)
            k.recip(rs_mem[:], rs_mem[:])
        for c8 in range(8):
            stg = (FD, FE)[c8 % 2]
            sv = stg.v(stg.t[:, 0:1024].rearrange("p (k m) -> p k m", k=8))
            k.dma(sv, wkv_d[l, :, :, c8 * 128:(c8 + 1) * 128])
            dst = wbf0[c8 // 4]
            k.cp(V(dst.ap[:, :, (c8 % 4) * 128:(c8 % 4 + 1) * 128], dst.bufs), sv)
        for kk in range(8):
            k.stt(HC[:, kk * MEM:(kk + 1) * MEM], FA[:, kk * MEM:(kk + 1) * MEM],
                  gmem[:, l, kk:kk + 1], rs_mem[:], ALU.mult, ALU.mult)
        for half in range(2):
            wb = [HA, HB][half]
            for c in range(4):
                pb = PS[c % 2]
                for kk in range(8):
                    k.mm(pb[:, 0:MEM], wb[:, kk * 512 + c * 128:kk * 512 + (c + 1) * 128],
                         HC[:, kk * MEM:(kk + 1) * MEM], start=(kk == 0), stop=(kk == 7))
                cc = half * 4 + c
                k.cp(FC[:, cc * MEM:(cc + 1) * MEM], pb[:, 0:MEM])
                if half == 0:
                    k.acp(kT_mem1[:, c, :], pb[:, 0:MEM])
            if half == 1:
                for jb in range(2):
                    pb = PS[2 + jb]
                    for kk in range(8):
                        k.mm(pb[:, :], HC[:, kk * MEM + jb * 128:kk * MEM + (jb + 1) * 128],
                             wb[:, kk * 512:(kk + 1) * 512], start=(kk == 0), stop=(kk == 7))
                    k.acp(v_mem1[:, jb, :], pb[:, :])
        k.dma(memo_d.v(memo_d.t[l].rearrange("c p m -> p c m")),
              view(FC, FC.t[:, 0:2048].rearrange("p (c m) -> p c m", c=8)), q=k.pool, is_output=True)

    hs = k.sb("hs_sb", [128, 8, 4], BF16)
    sproj = k.sb("sproj", [128, 64, 4])
    sctx = {"s": 0, "slot": 0, "on": cfg.get("sample", True)}

    def getw():
        wv = W.get()
        if sctx["on"] and sctx["s"] == 0:
            sl = sctx["slot"]
            sctx["slot"] += 1
            for kk in range(8):
                k.mm(PS[7][:, 0:4], view_w(wv, kk), hs[:, kk, :], start=(kk == 0), stop=(kk == 7))
            k.cp(sproj[:, sl, :], PS[7][:, 0:4])
        return wv

    proj_rot = [0]

    def proj(wv, tb):
        pb = PS[proj_rot[0] % 2]
        proj_rot[0] += 1
        for kk in range(8):
            k.mm(pb[:, :], view_w(wv, kk), hT[kk][:, tb * 512:(tb + 1) * 512], start=(kk == 0), stop=(kk == 7))
        return pb

    def view_w(wv, kk):
        return V(wv.ap[:, kk, :], wv.bufs)

    def rstd_act(out, ps, scale):
        k.actf(out, ps, AF.Ln, bias=EPS, scale=scale)
        k.actf(out, out, AF.Exp, scale=-0.5)

    def blk(tl, tb, off=0, n=512):
        return tl[:, off + tb * n: off + (tb + 1) * n]

    def norm_phase(l, s, xsrc):
        xbufs = ((FA, FB), (FC, FD))

        def xload(tb_):
            t0_ = s * ST + tb_ * 512
            for hf, Fx in enumerate(xbufs[tb_ % 2]):
                k.dma(view(Fx, Fx.t[:, 0:2048].rearrange("p (c t) -> p c t", c=4)),
                      xsrc.v(xsrc.t[hf * 4:(hf + 1) * 4, :, t0_:t0_ + 512].rearrange("c p t -> p c t")))
        xload(0)
        for tb in range(4):
            if tb + 1 < 4:
                xload(tb + 1)
            XA, XB = xbufs[tb % 2]
            k.actf(HC[:, 0:2048], XA[:, 0:2048], AF.Square)
            k.actf(HDt[:, 0:2048], XB[:, 0:2048], AF.Square)
            for c in range(8):
                Hx = (HC, HDt)[c // 4]
                k.mm(PS[6][:, :], ones_b[:], blk(Hx, c % 4), start=(c == 0), stop=(c == 7))
            rstd_act(rs[:], PS[6][:, :], 1.0 / D)
            for c in range(8):
                Fx = (XA, XB)[c // 4]
                k.stt(blk(hT[c], tb), blk(Fx, c % 4), gpre[:, l, c:c + 1], rs[:], ALU.mult, ALU.mult)

    def mem_item(l, h, mix_idx):
        wq = getw()
        for tb in range(4):
            pb = proj(wq, tb)
            k.acp(blk(HC, tb), pb[:, :])
        wg = getw()
        for tb in range(4):
            pb = proj(wg, tb)
            k.actf(blk(FA, tb), pb[:, :], AF.Silu)
        def mscores(tb_):
            banks = ((PS[2], PS[3]), (PS[6], PS[7]))[tb_ % 2]
            for c in range(2):
                k.mm(banks[c][:, :], kT_mem[l][:, h, c * 128:(c + 1) * 128], blk(HC, tb_))
        def mepi(tb_):
            nb_, db_ = ((PS[4], PS[5]), (PS[0], PS[1]))[tb_ % 2]
            k.actf(rs2[:], db_[:, :], AF.Ln)
            k.actf(rs2[:], rs2[:], AF.Exp, scale=-1.0)
            k.tt(rs2[:], rs2[:], nb_[:, :], ALU.mult)
            k.tt(blk(mixT[mix_idx], tb_), rs2[:], blk(FA, tb_), ALU.mult)

        mscores(0)
        for tb in range(4):
            banks = ((PS[2], PS[3]), (PS[6], PS[7]))[tb % 2]
            nb_, db_ = ((PS[4], PS[5]), (PS[0], PS[1]))[tb % 2]
            if tb + 1 < 4:
                mscores(tb + 1)
            for c in range(2):
                k.actf(Pt[c][:], banks[c][:, :], AF.Exp, scale=SCALE)
            for c in range(2):
                k.mm(nb_[:, :], v_mem[l][:, c, h * 128:(h + 1) * 128], Pt[c][:], start=(c == 0), stop=(c == 1))
            for c in range(2):
                k.mm(db_[:, :], ones_b[:], Pt[c][:], start=(c == 0), stop=(c == 1))
            if tb >= 1:
                mepi(tb - 1)
        mepi(3)

    def swa_item(l, s, h):
        j = l // 2
        acc_n, acc_d, gate = FB, FC, FD
        for g in range(3):
            win, d = GROUPS[g]
            span = win

            def rm_out(tl, off, tb):
                if d == 1:
                    return tl[:, off + tb * 512: off + (tb + 1) * 512]
                if d == 4:
                    return tl.v(tl.t[:, off + tb * 512: off + (tb + 1) * 512].rearrange("p (r i) -> p r i", r=4))
                return tl.v(tl.t[:, off:off + 2048].rearrange("p (r i) -> p r i", r=16)[:, :, tb * 32:(tb + 1) * 32])

            def rm_in(pb):
                if d == 1:
                    return pb[:, :]
                return pb.v(pb.t[:, :].rearrange("p (i r) -> p r i", r=d))

            wq = getw()
            for tb in range(4):
                pb = proj(wq, tb)
                k.acp(rm_out(HC, 0, tb), rm_in(pb))
            if s == 0:
                k.memset(HA[:, 0:span], 0.0)
            else:
                k.dma(HA[:, 0:span], histk_d[g][h][:, :])
            wk = getw()
            for tb in range(4):
                pb = proj(wk, tb)
                k.acp(rm_out(HA, span, tb), rm_in(pb))
                if s == NST - 1:
                    k.cp(blk(FA, tb), pb[:, :])
            if s == NST - 1:
                k.dma(oswa_d[g][j, 0, h, :, :], FA[:, ST - win:ST], q=k.pool, is_output=True)
            if s == 0:
                k.dma(histk_d[g][h][:, :], HA[:, ST:ST + span], q=k.pool)
            if s == 0:
                k.memset(HB[:, 0:d * 128], 0.0)
            else:
                k.dma(HB.v(HB.t[:, 0:d * 128].rearrange("p (b c) -> p b c", b=d)), histv_d[g][h][:, :, :])
            wv = getw()
            for tb in range(4):
                pb = proj(wv, tb)
                k.acp(rm_out(HDt, 0, tb), rm_in(pb))
                if s == NST - 1:
                    k.cp(blk(FE, tb), pb[:, :])
            if s == NST - 1:
                k.dma(oswa_d[g][j, 1, h, :, :], FE[:, ST - win:ST], q=k.pool, is_output=True)
            for b4 in range(4):
                for q in range(4):
                    bi = b4 * 4 + q
                    k.tr(view(PS[6], PSb6.ap[:, q * 128:(q + 1) * 128]), HDt[:, bi * 128:(bi + 1) * 128], ident_b[:])
                k.cp(HB[:, (d + b4 * 4) * 128:(d + b4 * 4 + 4) * 128], view(PS[6], PSb6.ap[:, 0:512]))
            if s == 0:
                k.dma(histv_d[g][h][:, :, :], HB.v(HB.t[:, 16 * 128:(16 + d) * 128].rearrange("p (b c) -> p b c", b=d)),
                      q=k.pool)
            def scores(rd_):
                pa, pb_ = ((PS[2], PS[3]), (PS[6], PS[7]))[rd_ % 2]
                for q in range(4):
                    bi = rd_ * 4 + q
                    qa = HC[:, bi * 128:(bi + 1) * 128]
                    kprev = HA[:, bi * 128:(bi + 1) * 128]
                    kcur = HA[:, span + bi * 128: span + (bi + 1) * 128]
                    k.mm(pa[:, q * 128:(q + 1) * 128], kprev, qa)
                    k.mm(pb_[:, q * 128:(q + 1) * 128], kcur, qa)
            def accum(rd_):
                nbk, dbk = ((PS[4], PS[5]), (PS[0], PS[1]))[rd_ % 2]
                for accb, pbank in ((acc_n, nbk), (acc_d, dbk)):
                    if d == 1:
                        dst = accb[:, rd_ * 512:(rd_ + 1) * 512]
                        src = pbank[:, :]
                    elif d == 4:
                        dst = accb.v(accb.t[:, rd_ * 512:(rd_ + 1) * 512].rearrange("p (i r) -> p r i", r=4))
                        src = pbank.v(pbank.t[:, :].rearrange("p (r i) -> p r i", r=4))
                    else:
                        dst = accb.v(accb.t[:, 0:2048].rearrange("p (i r) -> p r i", r=16)[:, rd_ * 4:(rd_ + 1) * 4, :])
                        src = pbank.v(pbank.t[:, :].rearrange("p (r i) -> p r i", r=4))
                    if g == 0:
                        k.cp(dst, src)
                    else:
                        k.tt(dst, dst, src, ALU.add)

            scores(0)
            for rd in range(4):
                pa, pb_ = ((PS[2], PS[3]), (PS[6], PS[7]))[rd % 2]
                nbk, dbk = ((PS[4], PS[5]), (PS[0], PS[1]))[rd % 2]
                if rd + 1 < 4:
                    scores(rd + 1)
                k.actf(Pt[0][:], pa[:, :], AF.Exp, scale=SCALE)
                k.actf(Pt[1][:], pb_[:, :], AF.Exp, scale=SCALE)
                if s == 0 and ((d == 1 and rd == 0)):
                    mp = mP_f1
                elif s == 0 and ((d == 4 and rd == 0) or d == 16):
                    mp = mP_zero
                else:
                    mp = mP_gen
                k.tt(Pt[0][:], Pt[0][:], mp.v(mp.t[:, :, :].rearrange("p a b -> p (a b)")), ALU.mult)
                k.tt(Pt[1][:], Pt[1][:], mC_gen.v(mC_gen.t[:, :, :].rearrange("p a b -> p (a b)")), ALU.mult)
                for q in range(4):
                    bi = rd * 4 + q
                    vprev = HB[:, bi * 128:(bi + 1) * 128]
                    vcur = HB[:, (d + bi) * 128:(d + bi + 1) * 128]
                    k.mm(nbk[:, q * 128:(q + 1) * 128], vprev, Pt[0][:, q * 128:(q + 1) * 128], start=True, stop=False)
                    k.mm(nbk[:, q * 128:(q + 1) * 128], vcur, Pt[1][:, q * 128:(q + 1) * 128], start=False, stop=True)
                k.mm(dbk[:, :], ones_b[:], Pt[0][:], start=True, stop=False)
                k.mm(dbk[:, :], ones_b[:], Pt[1][:], start=False, stop=True)
                if rd >= 1:
                    accum(rd - 1)
            accum(3)
        wg = getw()
        for tb in range(4):
            pb = proj(wg, tb)
            k.actf(blk(gate, tb), pb[:, :], AF.Silu)
        k.actf(acc_d[:, 0:ST], acc_d[:, 0:ST], AF.Ln)
        k.actf(acc_d[:, 0:ST], acc_d[:, 0:ST], AF.Exp, scale=-1.0)
        k.tt(acc_n[:, 0:ST], acc_n[:, 0:ST], acc_d[:, 0:ST], ALU.mult)
        k.tt(mixT[4 + h][:, :], acc_n[:, 0:ST], gate[:, 0:ST], ALU.mult)


    gs = {nm: k.sb("g_" + nm, [128, 64]) for nm in ("beta", "g", "gc", "ngc", "bg", "egl", "glb", "egs", "tmp")}
    wba_f = k.sb("wba_sf", [128, 8, 8])
    wba_b = k.sb("wba_b", [128, 8, 8], BF16)
    gconvw = k.sb("gconvw_sb", [128, 12, 4])
    alog = k.sb("alog_sb", [128, 64])
    dtb = k.sb("dtb_sb", [128, 64])
    negA = k.sb("negA", [128, 64])
    gnorm = k.sb("gnorm_sb", [128, 1])
    halo = k.sb("halo", [128, 12, 3])
    S = k.sb("S", [128, 4, 128])
    Sb = k.sb("Sb", [128, 4, 128], BF16)
    alt = k.es.enter_context(nc.sbuf_tensor("alt", [128, 3072], F32))
    qt = {}
    for i, nm in enumerate(("A", "Al", "T", "M", "X", "Kbg", "Vb", "EGB", "WT0", "WT1", "qg0", "qg1")):
        qt[nm] = Tl(alt[:, i * 256:(i + 1) * 256].bitcast(BF16), "q_" + nm)
    allb = [t.b for t in qt.values()]
    fa_names, fe_names = [], []
    for nm, lo, hi, dt_ in (("gRep", 0, 512, F32), ("D1", 512, 1024, F32), ("D2", 1024, 1536, F32),
                            ("attnT0", 1536, 1792, BF16), ("attnT1", 1792, 2048, BF16)):
        ap_ = FA.t[:, lo:hi]
        qt[nm] = Tl(ap_.bitcast(BF16) if dt_ == BF16 else ap_, "q_" + nm)
        fa_names.append(nm)
    for nm, lo, hi, dt_ in (("U0", 0, 512, F32), ("U1", 512, 1024, F32), ("kdec0", 1024, 1280, BF16),
                            ("kdec1", 1280, 1536, BF16), ("vnew0", 1536, 1600, BF16), ("vnew1", 1600, 1664, BF16)):
        ap_ = FE.t[:, lo:hi]
        qt[nm] = Tl(ap_.bitcast(BF16) if dt_ == BF16 else ap_, "q_" + nm)
        fe_names.append(nm)
    fa_bufs = [qt[n].b for n in fa_names]
    fe_bufs = [qt[n].b for n in fe_names]
    ALV = [Tl(hbig[:, i * 512:(i + 1) * 512], "alv%d" % i) for i in range(6)]
    alv_bufs = [t.b for t in ALV]
    cTRIU = cst[:, C_TRIU, :]

    def even_init(l):
        j = l // 2
        k.dma(wba_f[:], wba_d[j, :, :, :])
        k.cp(wba_b[:], wba_f[:])
        k.dma(gconvw[:], gconvw_d[j, :, :, :])
        k.dma(alog[:], alog_d[j, :, :])
        k.dma(dtb[:], dtb_d[j, :, :])
        k.dma(gnorm[:], gnorm_d[j, :, :])
        k.actf(negA[:], alog[:], AF.Exp)
        k.ts(negA[:], negA[:], -1.0, ALU.mult)
        k.memset(halo[:], 0.0)
        k.memset(S[:], 0.0)
        k.memset(Sb[:], 0.0)

    def gdn_prep():
        for bl in range(16):
            for kk in range(8):
                k.mm(PS[7][:, bl * 8:(bl + 1) * 8], hT[kk][:, bl * 128:(bl + 1) * 128], wba_b[:, kk, :],
                     start=(kk == 0), stop=(kk == 7))
        ba = PS[7].t[:, 0:128].rearrange("p (b c) -> p b c", c=8)

        def g3(tl):
            return tl.v(tl.t[:, :].rearrange("p (b c) -> p b c", c=4))
        k.actf(g3(gs["beta"]), PS[7].v(ba[:, :, 0:4]), AF.Sigmoid)
        k.tt(g3(gs["tmp"]), PS[7].v(ba[:, :, 4:8]), g3(dtb), ALU.add)
        k.actf(gs["tmp"][:], gs["tmp"][:], AF.Exp)
        k.actf(gs["tmp"][:], gs["tmp"][:], AF.Ln, bias=1.0)
        k.tt(gs["g"][:], gs["tmp"][:], negA[:], ALU.mult)
        k.mm(PS[7][:, 128:192], cTRIU, gs["g"][:])
        k.cp(gs["gc"][:], PS[7][:, 128:192])
        k.ts(gs["ngc"][:], gs["gc"][:], -1.0, ALU.mult)
        k.mm(PS[7][:, 192:256], cst[:, C_SEL, :], gs["gc"][:])
        k.cp(gs["glb"][:], PS[7][:, 192:256])
        k.actf(gs["egs"][:], gs["glb"][:], AF.Exp)
        k.tt(gs["tmp"][:], gs["glb"][:], gs["gc"][:], ALU.subtract)
        k.actf(gs["egl"][:], gs["tmp"][:], AF.Exp)
        k.actf(gs["tmp"][:], gs["gc"][:], AF.Exp)
        k.tt(gs["bg"][:], gs["tmp"][:], gs["beta"][:], ALU.mult)

    rs4 = [Tl(hbig[:, i * 1024:(i + 1) * 1024].bitcast(F32), "rs4_%d" % i) for i in range(4)]
    for t_ in rs4:
        t_.b = HA.b

    def gdn_item(l, s, h):
        j = l // 2
        dsts = (FB, FC, FD)
        hdst = (HC, HDt, None)
        for t in range(3):
            ci = t * 4 + h
            wv = getw()
            Fp = (FA, FE, FA)[t]
            k.cp(Fp[:, 0:3], halo[:, ci, :])
            for tb in range(4):
                pb = proj(wv, tb)
                k.acp(blk(Fp, tb, off=3), pb[:, :])
            k.cp(halo[:, ci, :], Fp[:, ST:ST + 3])
            Fd = dsts[t]
            k.ts(Fd[:, 0:ST], Fp[:, 0:ST], gconvw[:, ci, 0:1], ALU.mult)
            for jj in range(1, 4):
                k.stt(Fd[:, 0:ST], Fp[:, jj:jj + ST], gconvw[:, ci, jj:jj + 1], Fd[:, 0:ST], ALU.mult, ALU.add)
            k.actf(Fd[:, 0:ST], Fd[:, 0:ST], AF.Silu)
            if t < 2:
                k.actf(HB[:, 0:ST], Fd[:, 0:ST], AF.Square)
                for tb in range(4):
                    k.mm(PS[2 + tb][:, :], ones_b[:], blk(HB, tb))
                for tb in range(4):
                    rstd_act(rs4[tb][:], PS[2 + tb][:, :], 1.0)
                for tb in range(4):
                    k.stt(blk(hdst[t], tb), blk(Fd, tb), (SCALE if t == 0 else 1.0), rs4[tb][:], ALU.mult, ALU.mult)

        def q3(tl):
            return tl.v(tl.t[:, :].rearrange("p (q c) -> p q c", q=4))

        def bc_in(nm, Q):
            t = gs[nm]
            ap_ = t.t[:, 16 * Q:16 * Q + 16].rearrange("p (q h) -> p q h", h=4)[:, :, h]
            return t.v(ap_.unsqueeze(2).to_broadcast([128, 4, 128]))

        def bc_mid(slot):
            return cst.v(cst.t[:, slot, :].unsqueeze(1).to_broadcast([128, 4, 128]))

        def prep(Q):
            par = Q % 2
            WT, qg, attnT, kdec, U = (qt["WT%d" % par], qt["qg%d" % par], qt["attnT%d" % par], qt["kdec%d" % par],
                                      qt["U%d" % par])
            A, Al, T, M, X, Kbg, Vb, EGB = (qt[n] for n in ("A", "Al", "T", "M", "X", "Kbg", "Vb", "EGB"))
            gRep, D1, D2 = qt["gRep"], qt["D1"], qt["D2"]
            c0 = Q * 512
            PSb4 = PS[4].v(PS[4].t[:, :].bitcast(BF16)[:, 0:512])
            blks = [(q_, c0 + q_ * 128, c0 + (q_ + 1) * 128) for q_ in range(4)]
            sl = lambda tl, q_: tl[:, q_ * 128:(q_ + 1) * 128]
            for q_, a0, a1 in blks:
                k.tr(V(PSb4.ap[:, q_ * 128:(q_ + 1) * 128], PSb4.bufs), HDt[:, a0:a1], ident_b[:])
            for q_, a0, a1 in blks:
                k.tr(sl(PS[5], q_), FD[:, a0:a1], ident_f)
            k.tt(q3(gRep), bc_mid(C_ONE), bc_in("g", Q), ALU.mult)
            yield
            k.tt(q3(Kbg), V(PSb4.ap.rearrange("p (q c) -> p q c", q=4), PSb4.bufs), bc_in("bg", Q), ALU.mult)
            k.tt(q3(kdec), V(PSb4.ap.rearrange("p (q c) -> p q c", q=4), PSb4.bufs), bc_in("egl", Q), ALU.mult)
            k.tt(q3(Vb), q3(PS[5]), bc_in("beta", Q), ALU.mult)
            for q_, a0, a1 in blks:
                k.mm(sl(PS[1], q_), sl(gRep, q_), cTRIU)
            yield
            k.actf(EGB[:, :], PS[1][:, :], AF.Exp)
            k.stt(q3(D1), q3(PS[1]), -1.0, bc_mid(C_NSL), ALU.mult, ALU.add)
            k.tt(q3(D2), q3(PS[1]), bc_mid(C_NIU), ALU.add)
            for q_, a0, a1 in blks:
                k.mm(sl(PS[7], q_), HDt[:, a0:a1], HDt[:, a0:a1])
            for q_, a0, a1 in blks:
                k.mm(sl(PS[5], q_), HDt[:, a0:a1], HC[:, a0:a1])
            yield
            k.tt(q3(D1), q3(D1), bc_in("gc", Q), ALU.add)
            k.tt(q3(D2), q3(D2), bc_in("ngc", Q), ALU.add)
            k.tt(qg[:, :], HC[:, c0:c0 + 512], EGB[:, :], ALU.mult)
            yield
            k.actf(D1[:, :], D1[:, :], AF.Exp)
            k.actf(D2[:, :], D2[:, :], AF.Exp)
            yield
            k.tt(D1[:, :], D1[:, :], PS[7][:, :], ALU.mult)
            k.tt(q3(A), q3(D1), bc_in("beta", Q), ALU.mult)
            k.tt(attnT[:, :], PS[5][:, :], D2[:, :], ALU.mult)
            yield
            k.tt(q3(Al), q3(A), bc_mid(C_NM1), ALU.mult)
            k.tt(q3(T), q3(Al), bc_mid(C_ID), ALU.add)
            for lv in range(2, 8):
                k.tt(q3(ALV[lv - 2]), q3(A), bc_mid(C_LV + lv - 2), ALU.mult)
            yield
            for q_, a0, a1 in blks:
                k.tr(V(PSb4.ap[:, q_ * 128:(q_ + 1) * 128], PSb4.bufs), sl(T, q_), ident_b[:])
            yield
            k.acp(M[:, :], PSb4)
            for lv in range(2, 8):
                for q_, a0, a1 in blks:
                    k.mm(sl(PS[2], q_), sl(ALV[lv - 2], q_), sl(M, q_))
                yield
                k.acp(X[:, :], PS[2][:, :])
                yield
                for q_, a0, a1 in blks:
                    k.mm(sl(PS[3], q_), sl(T, q_), sl(X, q_))
                yield
                k.tt(M[:, :], M[:, :], PS[3][:, :], ALU.subtract)
                yield
                if lv < 7:
                    for q_, a0, a1 in blks:
                        k.tr(V(PSb4.ap[:, q_ * 128:(q_ + 1) * 128], PSb4.bufs), sl(M, q_), ident_b[:])
                    yield
                    k.acp(T[:, :], PSb4)
                    yield
            for q_, a0, a1 in blks:
                k.mm(sl(PS[2], q_), sl(M, q_), sl(Vb, q_))
            for q_, a0, a1 in blks:
                k.mm(sl(PS[3], q_), sl(Kbg, q_), sl(M, q_))
            yield
            k.acp(U[:, :], PS[2][:, :])
            k.acp(WT[:, :], PS[3][:, :])
            yield

        def seq(Q):
            par = Q % 2
            WT, qg, attnT, kdec, U = (qt["WT%d" % par], qt["qg%d" % par], qt["attnT%d" % par], qt["kdec%d" % par],
                                      qt["U%d" % par])
            sl = lambda tl, q_: tl[:, q_ * 128:(q_ + 1) * 128]
            for q_ in range(4):
                b_ = Q * 4 + q_
                col = b_ * 4 + h
                c0, c1 = b_ * 128, (b_ + 1) * 128
                vnew = qt["vnew%d" % (q_ % 2)]
                k.mm(PS[0][:, 0:128], sl(WT, q_), Sb[:, h, :])
                yield
                k.tt(vnew[:, :], sl(U, q_), PS[0][:, 0:128], ALU.subtract)
                yield
                k.mm(PS[0][:, 128:256], Sb[:, h, :], sl(qg, q_), start=True, stop=False)
                k.mm(PS[0][:, 128:256], vnew[:, :], sl(attnT, q_), start=False, stop=True)
                k.mm(PS[6][:, 0:128], sl(kdec, q_), vnew[:, :])
                yield
                k.stt(Sb[:, h, :], S[:, h, :], gs["egs"][:, col:col + 1], PS[6][:, 0:128], ALU.mult, ALU.add)
                k.stt(S[:, h, :], S[:, h, :], gs["egs"][:, col:col + 1], PS[6][:, 0:128], ALU.mult, ALU.add)
                k.acp(FB[:, c0:c1], PS[0][:, 128:256])
                yield

        def run_interleaved(gens):
            gens = [g for g in gens if g is not None]
            while gens:
                for g in list(gens):
                    try:
                        next(g)
                    except StopIteration:
                        gens.remove(g)

        inherit(fa_bufs, [FA.b])
        inherit(fe_bufs, [FE.b])
        inherit(alv_bufs, [HA.b])
        run_interleaved([prep(0)])
        for Q in range(4):
            run_interleaved([prep(Q + 1) if Q + 1 < 4 else None, seq(Q)])
        inherit([FA.b], fa_bufs)
        inherit([FE.b], fe_bufs)
        inherit([HA.b], alv_bufs)
        wz = getw()
        for tb in range(4):
            pb = proj(wz, tb)
            k.actf(blk(FC, tb), pb[:, :], AF.Silu)
        k.actf(HB[:, 0:ST], FB[:, 0:ST], AF.Square)
        for tb in range(4):
            k.mm(PS[2 + tb][:, :], ones_b[:], blk(HB, tb))
        for tb in range(4):
            rstd_act(rs4[tb][:], PS[2 + tb][:, :], 1.0 / 128)
        for tb in range(4):
            k.stt(blk(FB, tb), blk(FB, tb), gnorm[:, 0:1], rs4[tb][:], ALU.mult, ALU.mult)
            k.tt(blk(mixT[h], tb), blk(FB, tb), blk(FC, tb), ALU.mult)
        if s == NST - 1:
            k.dma(ogdn_d[j, h, :, :], S[:, h, :], q=k.pool, is_output=True)
            if h == 3:
                k.dma(ogconv_d[j, :, :, :], halo[:], q=k.pool, is_output=True)

    lconvw = k.sb("lconvw_sb", [128, 8, 4])
    lvec = k.sb("lvec_sb", [128, 4, 8])
    nc8 = k.sb("nc8", [128, 8])
    class _AliasTl:
        def __init__(self, ap, bufs):
            self.t = ap
            self.bufs = bufs

        def __getitem__(self, key):
            return V(self.t[key], self.bufs)

        def v(self, ap):
            return V(ap, self.bufs)
    lwa_b = _AliasTl(alt[:, 0:1024].bitcast(BF16).rearrange("p (a b c) -> p a b c", a=4, b=2), allb)
    lwx_b = _AliasTl(alt[:, 1024:2048].bitcast(BF16).rearrange("p (a b c) -> p a b c", a=4, b=2), allb)
    hstate = k.sb("hstate", [128, 8])
    lhalo = k.sb("lhalo", [128, 8, 3])

    def odd_init(l):
        j = l // 2
        k.dma(lconvw[:], lconvw_d[j, :, :, :])
        k.dma(lvec[:], lvec_d[j, :, :, :])
        k.dma(view(FA, FA.t[:, 0:2048].rearrange("p (a b c) -> p a b c", a=4, b=2)), lwa_d[j, :, :, :, :])
        k.dma(view(FB, FB.t[:, 0:2048].rearrange("p (a b c) -> p a b c", a=4, b=2)), lwx_d[j, :, :, :, :])
        k.cp(view(lwa_b, lwa_b.t[:, :, :, :].rearrange("p a b c -> p (a b c)")), FA[:, 0:2048])
        k.cp(view(lwx_b, lwx_b.t[:, :, :, :].rearrange("p a b c -> p (a b c)")), FB[:, 0:2048])
        k.actf(nc8[:], lvec[:, 3, :], AF.Exp, scale=-1.0)
        k.actf(nc8[:], nc8[:], AF.Ln, bias=1.0)
        k.ts(nc8[:], nc8[:], -8.0, ALU.mult)
        k.memset(hstate[:], 0.0)
        k.memset(lhalo[:], 0.0)

    def lru_item(l, s, n):
        j = l // 2
        xcs = (FB, FC)
        xbs = (HC, HDt)
        for cc in range(2):
            c = 2 * n + cc
            wv = getw()
            k.cp(FA[:, 0:3], lhalo[:, c, :])
            for tb in range(4):
                pb = proj(wv, tb)
                k.acp(blk(FA, tb, off=3), pb[:, :])
            k.cp(lhalo[:, c, :], FA[:, ST:ST + 3])
            Fx = xcs[cc]
            k.ts(Fx[:, 0:ST], FA[:, 0:ST], lconvw[:, c, 0:1], ALU.mult, lvec[:, 0, c:c + 1], ALU.add)
            for jj in range(1, 4):
                k.stt(Fx[:, 0:ST], FA[:, jj:jj + ST], lconvw[:, c, jj:jj + 1], Fx[:, 0:ST], ALU.mult, ALU.add)
            k.acp(xbs[cc][:, 0:ST], Fx[:, 0:ST])
        for oc in range(2):
            c = 2 * n + oc
            for tb in range(4):
                for kc in range(2):
                    k.mm(PS[2][:, :], lwa_b[:, n, kc, oc * 128:(oc + 1) * 128], blk(xbs[kc], tb),
                         start=(kc == 0), stop=(kc == 1))
                k.actf(blk(FD, tb), PS[2][:, :], AF.Sigmoid, bias=lvec[:, 1, c:c + 1])
                for kc in range(2):
                    k.mm(PS[3][:, :], lwx_b[:, n, kc, oc * 128:(oc + 1) * 128], blk(xbs[kc], tb),
                         start=(kc == 0), stop=(kc == 1))
                k.actf(blk(FE, tb), PS[3][:, :], AF.Sigmoid, bias=lvec[:, 2, c:c + 1])
            k.actf(FD[:, 0:ST], FD[:, 0:ST], AF.Exp, scale=nc8[:, c:c + 1])
            k.tt(FA[:, 0:ST], FD[:, 0:ST], FD[:, 0:ST], ALU.mult)
            k.actf(FA[:, 0:ST], FA[:, 0:ST], AF.Sqrt, bias=1.0, scale=-1.0)
            k.tt(FE[:, 0:ST], FE[:, 0:ST], xcs[oc][:, 0:ST], ALU.mult)
            k.tt(FE[:, 0:ST], FE[:, 0:ST], FA[:, 0:ST], ALU.mult)
            k.scan(FA[:, 0:ST], FD[:, 0:ST], FE[:, 0:ST], (hstate[:, c:c + 1] if s > 0 else 0.0))
            k.cp(hstate[:, c:c + 1], FA[:, ST - 1:ST])
            wg = getw()
            for tb in range(4):
                pb = proj(wg, tb)
                k.actf(blk(FD, tb), pb[:, :], AF.Silu)
            k.tt(mixT[c][:, :], FA[:, 0:ST], FD[:, 0:ST], ALU.mult)

    dbg_d = k.dout("dbg_mix", [12, 128, SEQ]) if cfg.get("debug") else None

    def out_phase(l, s, xsrc, xdst, last):
        if dbg_d is not None and l == cfg.get("debug_layer", 0):
            for kk in range(12):
                k.acp(FA[:, 0:ST], mixT[kk][:, :])
                k.dma(dbg_d[kk, :, s * ST:(s + 1) * ST], FA[:, 0:ST], q=k.pool, is_output=True)
        for c in range(8):
            for hf in range(2):
                stg = (FA, FB, FC)[(2 * c + hf) % 3]
                sv = stg.v(stg.t[:, 0:768].rearrange("p (k m) -> p k m", k=6))
                k.dma(sv, wout_d[l, c, :, hf * 6:(hf + 1) * 6, :])
                k.cp(V(WOUT.ap[:, hf * 6:(hf + 1) * 6, c * 128:(c + 1) * 128], WOUT.bufs), sv)
        if sctx["on"] and s == 0:
            sample_out(l, last)
        for tb in range(4):
            t0 = s * ST + tb * 512
            for hf, Fx in enumerate((FD, FE)):
                k.dma(view(Fx, Fx.t[:, 0:2048].rearrange("p (c t) -> p c t", c=4)),
                      xsrc.v(xsrc.t[hf * 4:(hf + 1) * 4, :, t0:t0 + 512].rearrange("c p t -> p c t")))
            for c in range(8):
                pb = PS[c % 2]
                for kk in range(12):
                    k.mm(pb[:, :], V(WOUT.ap[:, kk, c * 128:(c + 1) * 128], WOUT.bufs), blk(mixT[kk], tb),
                         start=(kk == 0), stop=(kk == 11))
                Fy = (FA, FB)[c // 4]
                k.acp(blk(Fy, c % 4), pb[:, :])
                k.tt(Pt[c % 2][:], blk(Fy, c % 4), blk(Fy, c % 4), ALU.mult)
                k.mm(PS[6][:, :], ones_b[:], Pt[c % 2][:], start=(c == 0), stop=(c == 7))
            rstd_act(rs[:], PS[6][:, :], 1.0 / D)
            for c in range(8):
                Fy = (FA, FB)[c // 4]
                Fx = (FD, FE)[c // 4]
                k.stt(blk(Fy, c % 4), blk(Fy, c % 4), gpost[:, l, c:c + 1], rs[:], ALU.mult, ALU.mult)
                k.tt(blk(Fx, c % 4), blk(Fx, c % 4), blk(Fy, c % 4), ALU.add)
            for hf, Fx in enumerate((FD, FE)):
                k.dma(xdst.v(xdst.t[hf * 4:(hf + 1) * 4, :, t0:t0 + 512].rearrange("c p t -> p c t")),
                      view(Fx, Fx.t[:, 0:2048].rearrange("p (c t) -> p c t", c=4)), q=k.pool, is_output=last)


    def sv(tl, ap):
        return tl.v(ap)

    xs = k.sb("xs_sb", [128, 8, 4])
    mix_s = k.sb("mix_s", [128, 12, 4], BF16)
    sgconv_sb = k.sb("sgconv_sb", [128, 12, 4, 3])
    slru_sb = k.sb("slru_sb", [128, 8, 4])
    slconv_sb = k.sb("slconv_sb", [128, 8, 4, 3])
    sT = [k.sb("sT%d" % i, [128, 64]) for i in range(6)]
    srow = k.sb("srow", [128, 128])
    sBeta = k.sb("sBeta", [128, 16])
    sNBeta = k.sb("sNBeta", [128, 16])
    sEg = k.sb("sEg", [128, 16])
    k.dma(xs[:], xsT_d[:, :, :])

    def bulk_copies():
        for g in range(3):
            L = GROUPS[g][0]
            for j in range(2):
                for tok in range(4):
                    r0 = 1
                    while r0 < L:
                        r1 = min(L, r0 + 512)
                        k.bulk_copy(oswas_d[g].t[j, tok, r0 - 1:r1 - 1].rearrange("r a h d -> r (a h d)"),
                                    cswa_d[g].t[j, tok, r0:r1].rearrange("r a h d -> r (a h d)"))
                        r0 = r1
        for j in range(2):
            k.bulk_copy(ogconvs_d.t[j, :, 0:2, :], sgconv_nat.t[j, :, 1:3, :])
            k.bulk_copy(olconvs_d.t[j, :, 0:2, :], slconv_nat.t[j, :, 1:3, :])

    def emit_rows(src, n, dst, is_out=True):
        k.tr(PS[7][0:n, 0:128], src, ident_f)
        k.cp(srow[0:n, :], PS[7][0:n, 0:128])
        k.dma(dst, srow[0:n, :], q=k.pool, is_output=is_out)

    def sample_norm(l):
        k.tt(sT[0][:, 0:32], xs.v(xs.t[:, :, :].rearrange("p c t -> p (c t)")),
             xs.v(xs.t[:, :, :].rearrange("p c t -> p (c t)")), ALU.mult)
        for c in range(8):
            k.mm(PS[7][:, 0:4], ones_f, sT[0][:, c * 4:(c + 1) * 4], start=(c == 0), stop=(c == 7))
        k.actf(sT[1][:, 0:4], PS[7][:, 0:4], AF.Sqrt, bias=EPS, scale=1.0 / D)
        k.recip(sT[1][:, 0:4], sT[1][:, 0:4])
        for c in range(8):
            k.stt(hs[:, c, :], xs[:, c, :], gpre[:, l, c:c + 1], sT[1][:, 0:4], ALU.mult, ALU.mult)

    def zero_bank(pb, n):
        k.mm(pb[:, 0:n], mP_zero[:, 0, :], mP_zero.v(mP_zero.t[:, :, :].rearrange("p a b -> p (a b)")[:, 0:n]),
             start=True, stop=True, sgc=True)

    def s_qB(qcol):
        for h in range(4):
            k.ts(FA[:, h * 128:(h + 1) * 128], ident_f, qcol(h), ALU.mult)
        k.mm(PS[6][:, :], ones_f, FA[:, 0:512])
        k.acp(FA[:, 512:1024], PS[6][:, :])

    kvbuf = ((FB, FC), (FE, FE))

    def s_kvload(i, Kd, Vd):
        kb_, vb_ = kvbuf[i % 2]
        if i % 2 == 0:
            k.dma(kb_[:, 0:512], Kd)
            k.dma(vb_[:, 0:512], Vd)
        else:
            k.dma(kb_[:, 0:512], Kd)
            k.dma(vb_[:, 512:1024], Vd)

    def s_keyblock(i, tok):
        kb_, vb_ = kvbuf[i % 2]
        Kt = kb_[:, 0:512]
        voff = 0 if i % 2 == 0 else 512
        k.tt(FD[:, 0:512], Kt, FA[:, 512:1024], ALU.mult)
        k.reduce_x(sT[2][:, 0:4], sv(FD, FD.t[:, 0:512].rearrange("p (h d) -> p h d", h=4)))
        k.actf(sT[3][:, 0:4], sT[2][:, 0:4], AF.Exp, scale=SCALE)
        for h in range(4):
            k.mm(PS[4][:, tok * 4 + h:tok * 4 + h + 1], vb_[:, voff + h * 128:voff + (h + 1) * 128], sT[3][:, h:h + 1],
                 start=False, stop=True, sgc=True)
        k.mm(PS[5][:, tok * 4:tok * 4 + 4], ones_f, sT[3][:, 0:4], start=False, stop=True, sgc=True)

    def s_run_blocks(blocks):
        if blocks:
            s_kvload(0, blocks[0][2], blocks[0][3])
        for i, (tok, qfn, Kd, Vd) in enumerate(blocks):
            if i + 1 < len(blocks):
                s_kvload(i + 1, blocks[i + 1][2], blocks[i + 1][3])
            if qfn is not None:
                s_qB(qfn)
            s_keyblock(i, tok)

    def s_mem(l):
        even = (l % 2 == 0)
        qs = (lambda h: 2 * h) if even else (lambda h: 16 + 2 * h)
        zero_bank(PS[4], 16)
        zero_bank(PS[5], 16)
        blocks = []
        for tok in range(4):
            for kb in range(2):
                blocks.append((tok, (lambda h, tok=tok: sproj[:, qs(h), tok:tok + 1]) if kb == 0 else None,
                               cmem_d.v(cmem_d.t[l, tok, kb * 128:(kb + 1) * 128, 0].rearrange("r h d -> r (h d)")),
                               cmem_d.v(cmem_d.t[l, tok, kb * 128:(kb + 1) * 128, 1].rearrange("r h d -> r (h d)"))))
        s_run_blocks(blocks)
        k.recip(sT[2][:, 0:16], PS[5][:, 0:16])
        k.tt(sT[2][:, 0:16], sT[2][:, 0:16], PS[4][:, 0:16], ALU.mult)
        for h in range(4):
            k.actf(sT[3][:, 0:4], sproj[:, qs(h) + 1, :], AF.Silu)
            k.tt(mix_s[:, 8 + h, :], sv(sT[2], sT[2].t[:, 0:16].rearrange("p (t h) -> p h t", h=4)[:, h, :]),
                 sT[3][:, 0:4], ALU.mult)

    def s_swa(l):
        j = l // 2
        zero_bank(PS[4], 16)
        zero_bank(PS[5], 16)
        qslot = lambda g, h: 8 + 10 * h + 3 * g
        blocks = []
        for tok in range(4):
            for g in range(3):
                L, d = GROUPS[g]
                cd = cswa_d[g]
                blocks.append((tok, (lambda h, tok=tok, g=g: sproj[:, qslot(g, h), tok:tok + 1]),
                               cd.v(cd.t[j, tok, :, 0].rearrange("(i s) h d -> i s (h d)", s=d)[:, 0, :]),
                               cd.v(cd.t[j, tok, :, 1].rearrange("(i s) h d -> i s (h d)", s=d)[:, 0, :])))
        s_run_blocks(blocks)
        for tok in range(4):
            for g in range(3):
                for h in range(4):
                    c = g * 4 + h
                    k.tt(sT[0][:, c:c + 1], sproj[:, qslot(g, h), tok:tok + 1], sproj[:, qslot(g, h) + 1, tok:tok + 1],
                         ALU.mult)
            k.mm(PS[6][:, 0:12], ones_f, sT[0][:, 0:12])
            k.actf(sT[1][:, 0:12], PS[6][:, 0:12], AF.Exp, scale=SCALE)
            for h in range(4):
                col = tok * 4 + h
                k.cp(sT[4][:, col:col + 1], PS[4][:, col:col + 1])
                k.cp(sT[5][:, col:col + 1], PS[5][:, col:col + 1])
                for g in range(3):
                    c = g * 4 + h
                    k.stt(sT[4][:, col:col + 1], sproj[:, qslot(g, h) + 2, tok:tok + 1], sT[1][:, c:c + 1],
                          sT[4][:, col:col + 1], ALU.mult, ALU.add)
                    k.tt(sT[5][:, col:col + 1], sT[5][:, col:col + 1], sT[1][:, c:c + 1], ALU.add)
            for g in range(3):
                for kv in range(2):
                    for h in range(4):
                        c = g * 8 + kv * 4 + h
                        k.cp(sT[2][:, c:c + 1], sproj[:, qslot(g, h) + 1 + kv, tok:tok + 1])
            k.tr(PS[7][0:24, 0:128], sT[2][:, 0:24], ident_f)
            k.cp(srow[0:24, :], PS[7][0:24, 0:128])
            for g in range(3):
                L = GROUPS[g][0]
                k.dma(oswas_d[g].v(oswas_d[g].t[j, tok, L - 1].rearrange("a h d -> (a h) d")),
                      srow[g * 8:(g + 1) * 8, :], q=k.pool, is_output=True)
        k.recip(sT[5][:, 0:16], sT[5][:, 0:16])
        k.tt(sT[4][:, 0:16], sT[4][:, 0:16], sT[5][:, 0:16], ALU.mult)
        for h in range(4):
            k.actf(sT[3][:, 0:4], sproj[:, 8 + 10 * h + 9, :], AF.Silu)
            k.tt(mix_s[:, 4 + h, :], sv(sT[4], sT[4].t[:, 0:16].rearrange("p (t h) -> p h t", h=4)[:, h, :]),
                 sT[3][:, 0:4], ALU.mult)

    def s_gdn(l):
        j = l // 2
        k.dma(sgconv_sb[:], sgconv_d[j, :, :, :, :])
        for tok in range(4):
            for kk in range(8):
                k.ts(HC[:, kk * 128:(kk + 1) * 128], ones_b[:], hs[:, kk, tok:tok + 1], ALU.mult)
            for kk in range(8):
                k.mm(PS[6][:, tok * 8:(tok + 1) * 8], HC[:, kk * 128:(kk + 1) * 128], wba_b[:, kk, :],
                     start=(kk == 0), stop=(kk == 7))
        ba = PS[6].t[:, 0:32].rearrange("p (t c) -> p t c", c=8)
        g3 = lambda tl: tl.v(tl.t[:, 0:16].rearrange("p (t c) -> p t c", c=4))
        k.actf(g3(sBeta), PS[6].v(ba[:, :, 0:4]), AF.Sigmoid)
        k.ts(sNBeta[:], sBeta[:], -1.0, ALU.mult)
        k.tt(g3(sEg), PS[6].v(ba[:, :, 4:8]), g3(dtb), ALU.add)
        k.actf(sEg[:], sEg[:], AF.Exp)
        k.actf(sEg[:], sEg[:], AF.Ln, bias=1.0)
        k.tt(sEg[:], sEg[:], negA[:, 0:16], ALU.mult)
        k.actf(sEg[:], sEg[:], AF.Exp)
        cq = sv(FD, FD.t[:, 0:48].rearrange("p (c t) -> p c t", t=4))
        for ci in range(12):
            t_, h_ = ci // 4, ci % 4
            xsl = sproj[:, 48 + 4 * h_ + t_, :]
            dst = sv(FD, FD.t[:, ci * 4:(ci + 1) * 4])
            k.ts(dst, sgconv_sb[:, ci, :, 0], gconvw[:, ci, 0:1], ALU.mult)
            for r in (1, 2):
                k.stt(dst, sgconv_sb[:, ci, :, r], gconvw[:, ci, r:r + 1], dst, ALU.mult, ALU.add)
            k.stt(dst, xsl, gconvw[:, ci, 3:4], dst, ALU.mult, ALU.add)
            k.cp(sT[0][:, ci * 4:(ci + 1) * 4], xsl)
        for tok in range(4):
            emit_rows(sv(sT[0], sT[0].t[:, 0:48].rearrange("p (c t) -> p c t", t=4)[:, :, tok]), 12,
                      ogconvs_d.v(ogconvs_d.t[j, tok, 2].rearrange("(c p) -> c p", p=128)))
        k.actf(FD[:, 0:48], FD[:, 0:48], AF.Silu)
        k.tt(sT[1][:, 0:32], FD[:, 0:32], FD[:, 0:32], ALU.mult)
        k.mm(PS[6][:, 64:96], ones_f, sT[1][:, 0:32])
        k.actf(sT[1][:, 0:32], PS[6][:, 64:96], AF.Sqrt, bias=EPS)
        k.recip(sT[1][:, 0:32], sT[1][:, 0:32])
        k.stt(FD[:, 0:16], FD[:, 0:16], SCALE, sT[1][:, 0:16], ALU.mult, ALU.mult)
        k.tt(FD[:, 16:32], FD[:, 16:32], sT[1][:, 16:32], ALU.mult)
        for tok in range(4):
            for h in range(4):
                col = tok * 4 + h
                qc = FD[:, (0 + h) * 4 + tok:(0 + h) * 4 + tok + 1]
                kc = FD[:, (4 + h) * 4 + tok:(4 + h) * 4 + tok + 1]
                vc = FD[:, (8 + h) * 4 + tok:(8 + h) * 4 + tok + 1]
                k.dma(FE[:, 0:128], sgdn_d[j, tok, h, :, :])
                k.mm(PS[6][:, 128:129], FE[:, 0:128], kc)
                k.stt(sT[2][:, 0:1], PS[6][:, 128:129], sEg[:, col:col + 1], vc, ALU.mult, ALU.subtract)
                k.ts(sT[2][:, 0:1], sT[2][:, 0:1], sNBeta[:, col:col + 1], ALU.mult)
                k.ts(FB[:, 0:128], ident_f, sT[2][:, 0:1], ALU.mult)
                k.mm(PS[5][:, 256:384], ones_f, FB[:, 0:128])
                k.ts(FE[:, 0:128], FE[:, 0:128], sEg[:, col:col + 1], ALU.mult)
                k.stt(FE[:, 128:256], PS[5][:, 256:384], kc, FE[:, 0:128], ALU.mult, ALU.add)
                k.dma(ogdns_d[j, tok, h, :, :], FE[:, 128:256], q=k.pool, is_output=True)
                k.mm(PS[6][:, 129:130], FE[:, 128:256], qc)
                k.cp(sT[3][:, col:col + 1], PS[6][:, 129:130])
        k.tt(sT[1][:, 0:16], sT[3][:, 0:16], sT[3][:, 0:16], ALU.mult)
        k.mm(PS[6][:, 64:80], ones_f, sT[1][:, 0:16])
        k.actf(sT[1][:, 0:16], PS[6][:, 64:80], AF.Sqrt, bias=EPS, scale=1.0 / 128)
        k.recip(sT[1][:, 0:16], sT[1][:, 0:16])
        k.stt(sT[3][:, 0:16], sT[3][:, 0:16], gnorm[:, 0:1], sT[1][:, 0:16], ALU.mult, ALU.mult)
        for h in range(4):
            k.actf(sT[2][:, 0:4], sproj[:, 48 + 4 * h + 3, :], AF.Silu)
            k.tt(mix_s[:, h, :], sv(sT[3], sT[3].t[:, 0:16].rearrange("p (t h) -> p h t", h=4)[:, h, :]),
                 sT[2][:, 0:4], ALU.mult)

    def s_lru(l):
        j = l // 2
        k.dma(slru_sb[:], slru_d[j, :, :, :])
        k.dma(slconv_sb[:], slconv_d[j, :, :, :, :])
        xc = lambda c: FD[:, c * 4:(c + 1) * 4]
        xcb = lambda c: HC[:, c * 4:(c + 1) * 4]
        for c in range(8):
            n, cc = c // 2, c % 2
            xsl = sproj[:, 4 * n + cc, :]
            k.ts(xc(c), slconv_sb[:, c, :, 0], lconvw[:, c, 0:1], ALU.mult, lvec[:, 0, c:c + 1], ALU.add)
            for r in (1, 2):
                k.stt(xc(c), slconv_sb[:, c, :, r], lconvw[:, c, r:r + 1], xc(c), ALU.mult, ALU.add)
            k.stt(xc(c), xsl, lconvw[:, c, 3:4], xc(c), ALU.mult, ALU.add)
            k.cp(sT[0][:, c * 4:(c + 1) * 4], xsl)
        for tok in range(4):
            emit_rows(sv(sT[0], sT[0].t[:, 0:32].rearrange("p (c t) -> p c t", t=4)[:, :, tok]), 8,
                      olconvs_d.v(olconvs_d.t[j, tok, 2].rearrange("(c p) -> c p", p=128)))
        k.cp(HC[:, 0:32], FD[:, 0:32])
        for c in range(8):
            n, oc = c // 2, c % 2
            for gi, (wb_, bi) in enumerate(((lwa_b, 1), (lwx_b, 2))):
                for kc in range(2):
                    k.mm(PS[6][:, gi * 32 + c * 4:gi * 32 + (c + 1) * 4], wb_[:, n, kc, oc * 128:(oc + 1) * 128],
                         xcb(2 * n + kc), start=(kc == 0), stop=(kc == 1))
                k.actf(sT[1 + gi][:, c * 4:(c + 1) * 4], PS[6][:, gi * 32 + c * 4:gi * 32 + (c + 1) * 4], AF.Sigmoid,
                       bias=lvec[:, bi, c:c + 1])
            k.actf(sT[1][:, c * 4:(c + 1) * 4], sT[1][:, c * 4:(c + 1) * 4], AF.Exp, scale=nc8[:, c:c + 1])
        k.tt(sT[3][:, 0:32], sT[1][:, 0:32], sT[1][:, 0:32], ALU.mult)
        k.actf(sT[3][:, 0:32], sT[3][:, 0:32], AF.Sqrt, bias=1.0, scale=-1.0)
        k.tt(sT[2][:, 0:32], sT[2][:, 0:32], FD[:, 0:32], ALU.mult)
        k.tt(sT[2][:, 0:32], sT[2][:, 0:32], sT[3][:, 0:32], ALU.mult)
        k.tt(sT[1][:, 0:32], sT[1][:, 0:32], slru_sb.v(slru_sb.t[:, :, :].rearrange("p c t -> p (c t)")), ALU.mult)
        k.tt(sT[1][:, 0:32], sT[1][:, 0:32], sT[2][:, 0:32], ALU.add)
        k.cp(sv(sT[5], sT[5].t[:, 0:32].rearrange("p (t c) -> p t c", c=8)),
             sv(sT[1], sT[1].t[:, 0:32].rearrange("p (c t) -> p t c", t=4)))
        emit_rows(sT[5][:, 0:32], 32, olrus_d.v(olrus_d.t[j].rearrange("t (c p) -> (t c) p", p=128)))
        for c in range(8):
            n, cc = c // 2, c % 2
            k.actf(sT[3][:, 0:4], sproj[:, 4 * n + 2 + cc, :], AF.Silu)
            k.tt(mix_s[:, c, :], sT[1][:, c * 4:(c + 1) * 4], sT[3][:, 0:4], ALU.mult)

    def sample_mixers(l):
        if l % 2 == 0:
            s_mem(l)
            s_swa(l)
            s_gdn(l)
        else:
            s_lru(l)
            s_mem(l)

    def sample_out(l, last):
        for c in range(8):
            for kk in range(12):
                k.mm(PS[7][:, c * 4:(c + 1) * 4], V(WOUT.ap[:, kk, c * 128:(c + 1) * 128], WOUT.bufs), mix_s[:, kk, :],
                     start=(kk == 0), stop=(kk == 11))
        k.cp(sT[0][:, 0:32], PS[7][:, 0:32])
        k.tt(sT[1][:, 0:32], sT[0][:, 0:32], sT[0][:, 0:32], ALU.mult)
        for c in range(8):
            k.mm(PS[7][:, 32:36], ones_f, sT[1][:, c * 4:(c + 1) * 4], start=(c == 0), stop=(c == 7))
        k.actf(sT[2][:, 0:4], PS[7][:, 32:36], AF.Sqrt, bias=EPS, scale=1.0 / D)
        k.recip(sT[2][:, 0:4], sT[2][:, 0:4])
        for c in range(8):
            k.stt(sT[0][:, c * 4:(c + 1) * 4], sT[0][:, c * 4:(c + 1) * 4], gpost[:, l, c:c + 1], sT[2][:, 0:4],
                  ALU.mult, ALU.mult)
            k.tt(xs[:, c, :], xs[:, c, :], sT[0][:, c * 4:(c + 1) * 4], ALU.add)
        if last:
            k.cp(sv(sT[5], sT[5].t[:, 0:32].rearrange("p (t c) -> p t c", c=8)),
                 xs.v(xs.t[:, :, :].rearrange("p c t -> p t c")))
            emit_rows(sT[5][:, 0:32], 32, oys_d.v(oys_d.t[:, :].rearrange("t (c p) -> (t c) p", p=128)))
    SAMPLE = sctx["on"]
    if SAMPLE:
        bulk_copies()
    for l in range(NL):
        last = (l == NL - 1)
        xsrc = xin_d if l == 0 else xs_d[(l - 1) % 2]
        xdst = yT_d if last else xs_d[l % 2]
        mem_phase(l)
        if l % 2 == 0:
            even_init(l)
        else:
            odd_init(l)
        if SAMPLE:
            sample_norm(l)
        for s in range(NST):
            sctx["s"] = s
            if s == 0:
                sctx["slot"] = 0
            norm_phase(l, s, xsrc)
            if l % 2 == 0:
                gdn_prep()
                for kind, h, _ in even_order():
                    if kind == "mem":
                        mem_item(l, h, 8 + h)
                    elif kind == "swa":
                        swa_item(l, s, h)
                    else:
                        gdn_item(l, s, h)
            else:
                for kind, h, _ in odd_order():
                    if kind == "lru":
                        lru_item(l, s, h)
                    else:
                        mem_item(l, h, 8 + h)
                if s == NST - 1:
                    j = l // 2
                    k.dma(olru_d[j, :, :], hstate[:], q=k.pool, is_output=True)
                    k.dma(olconv_d[j, :, :, :], lhalo[:], q=k.pool, is_output=True)
            if SAMPLE and s == 0:
                sample_mixers(l)
            out_phase(l, s, xsrc, xdst, last)
    assert W.n_use == len(plan), (W.n_use, len(plan))
    k.finish()
    return k


def _consts():
    c = np.zeros((128, NCST, 128), np.float32)
    p = np.arange(128)[:, None]
    f = np.arange(128)[None, :]
    c[:, C_ID] = (p == f)
    c[:, C_ONE] = 1.0
    c[:, C_NSL] = np.where(p > f, 0.0, NEG)
    c[:, C_NIU] = np.where(f >= p, 0.0, NEG)
    c[:, C_TRIU] = (p <= f)
    c[:, C_SEL] = (p == 127) * np.ones((1, 128))
    c[:, C_MP] = (p >= f)
    c[:, C_MC] = (p <= f)
    for lv in range(1, 8):
        m = (p > f) & ((p >> lv) == (f >> lv)) & ((p >> (lv - 1)) != (f >> (lv - 1)))
        if lv == 1:
            c[:, C_NM1] = -1.0 * m
        else:
            c[:, C_LV + lv - 2] = m
    return np.ascontiguousarray(c)


def _f(a):
    return np.ascontiguousarray(np.asarray(a, dtype=np.float32))


def _fm(v, n):
    v = _f(v)
    lead = v.shape[:-1]
    v = v.reshape(lead + (n, 128))
    return _f(np.moveaxis(v, -1, 0))


def _tile_cols(Wm, cols, nk):
    out = np.empty((len(cols), 128, nk, 128), np.float32)
    for i, c0 in enumerate(cols):
        out[i] = Wm[:, c0:c0 + 128].reshape(nk, 128, 128).transpose(1, 0, 2)
    return out


def kernel(**inp):
    return kernel_impl(inp)


def kernel_impl(inp, cfg=None):
    cfg = cfg or {}
    NL = cfg.get("nlayers", 4)
    k = build_program(cfg)
    shared = {"cst": _consts()}
    shared["wkv"] = _f(_f(inp["w_mem_kv"]).reshape(4, 8, 128, 1024).transpose(0, 2, 1, 3))
    shared["gmem"] = _fm(inp["mem_norm"], 8)
    shared["gpre"] = _fm(inp["norm_pre"], 8)
    shared["gpost"] = _fm(inp["norm_post"], 8)
    we = _f(inp["w_in_even"])
    wo = _f(inp["w_in_odd"])
    ecols = [c for _, _, cs in even_order() for c in cs]
    ocols = [c for _, _, cs in odd_order() for c in cs]
    shared["w_even"] = np.stack([_tile_cols(we[j], ecols, 8) for j in range(2)])
    shared["w_odd"] = np.stack([_tile_cols(wo[j], ocols, 8) for j in range(2)])
    woe = _f(inp["w_out_even"])
    woo = _f(inp["w_out_odd"])
    shared["w_out"] = np.stack([_tile_cols((woe if l % 2 == 0 else woo)[l // 2], [c * 128 for c in range(8)], 12)
                                for l in range(4)])
    shared["wba"] = _f(we[:, :, 2048:2056].reshape(2, 8, 128, 8).transpose(0, 2, 1, 3))
    shared["gconvw"] = _f(_f(inp["gdn_conv_w"]).transpose(0, 2, 1).reshape(2, 12, 128, 4).transpose(0, 2, 1, 3))
    shared["alog"] = _f(np.broadcast_to(np.tile(_f(inp["gdn_a_log"]), (1, 16))[:, None, :], (2, 128, 64)))
    shared["dtb"] = _f(np.broadcast_to(np.tile(_f(inp["gdn_dt_bias"]), (1, 16))[:, None, :], (2, 128, 64)))
    shared["gnorm"] = _f(_f(inp["gdn_norm"]).reshape(2, 128, 1))
    shared["lconvw"] = _f(_f(inp["lru_conv_w"]).transpose(0, 2, 1).reshape(2, 8, 128, 4).transpose(0, 2, 1, 3))
    lv = np.stack([_f(inp["lru_conv_b"]).reshape(2, 1024), _f(inp["lru_ba"]).reshape(2, 1024),
                   _f(inp["lru_bx"]).reshape(2, 1024), _f(inp["lru_lambda"]).reshape(2, 1024)], axis=1)
    shared["lvec"] = _f(lv.reshape(2, 4, 8, 128).transpose(0, 3, 1, 2))
    shared["lwa"] = _f(_f(inp["lru_wa"]).reshape(2, 4, 2, 128, 256).transpose(0, 3, 1, 2, 4))
    shared["lwx"] = _f(_f(inp["lru_wx"]).reshape(2, 4, 2, 128, 256).transpose(0, 3, 1, 2, 4))
    xp = _f(inp["x_prompt"])
    mp = _f(inp["mem_prompt"])
    SAMPLE = cfg.get("sample", True)
    if SAMPLE:
        xsm = _f(inp["x_sample"]).reshape(32, 8, 128)
        sg = _f(inp["state_gdn"])
        sgc = _f(inp["state_gdn_conv"])
        csw = [_f(inp["cache_swa1"]), _f(inp["cache_swa2"]), _f(inp["cache_swa3"])]
        cme = _f(inp["cache_mem"])
        sl = _f(inp["state_lru"])
        slc = _f(inp["state_lru_conv"])
    in_maps = []
    for c in range(NCORES):
        b = c % 4
        m = dict(shared)
        m["xT"] = _f(xp[b].T.reshape(8, 128, SEQ))
        m["memT"] = _f(mp[b].T.reshape(8, 128, MEM).transpose(1, 0, 2))
        if SAMPLE:
            b0 = c * SPC
            sl_ = slice(b0, b0 + SPC)
            m["xsT"] = _f(xsm[sl_].transpose(2, 1, 0))
            m["sgdn"] = _f(sg[:, sl_])
            m["sgconv"] = _f(sgc[:, sl_].reshape(2, SPC, 3, 12, 128).transpose(0, 4, 3, 1, 2))
            m["sgconv_nat"] = _f(sgc[:, sl_])
            for g in range(3):
                m["cswa%d" % (g + 1)] = _f(csw[g][:, sl_])
            m["cmem"] = _f(cme[:, sl_])
            m["slru"] = _f(sl[:, sl_].reshape(2, SPC, 8, 128).transpose(0, 3, 2, 1))
            m["slconv"] = _f(slc[:, sl_].reshape(2, SPC, 3, 8, 128).transpose(0, 4, 3, 1, 2))
            m["slconv_nat"] = _f(slc[:, sl_])
        in_maps.append(m)
    res = run_bass_kernel_spmd(k.nc, in_maps, core_ids=list(range(NCORES)))
    R = res.results
    B = 4
    y_prompt = np.zeros((B, SEQ, D), np.float32)
    mem_p = np.zeros((4, B, MEM, 2, 4, 128), np.float32)
    gdn_p = np.zeros((2, B, 4, 128, 128), np.float32)
    gconv_p = np.zeros((2, B, 3, 1536), np.float32)
    swa_p = [np.zeros((2, B, GROUPS[g][0], 2, 4, 128), np.float32) for g in range(3)]
    lru_p = np.zeros((2, B, 1024), np.float32)
    lconv_p = np.zeros((2, B, 3, 1024), np.float32)
    for b in range(B):
        r = R[b]
        y_prompt[b] = r["o_yT"].reshape(1024, SEQ).T
        mem_p[:, b] = r["o_memT"].reshape(4, 2, 4, 128, MEM).transpose(0, 4, 1, 2, 3)
        gdn_p[:, b] = r["o_gdn"]
        gconv_p[:, b] = r["o_gconv"].transpose(0, 3, 2, 1).reshape(2, 3, 1536)
        for g in range(3):
            swa_p[g][:, b] = r["o_swa%d" % (g + 1)].transpose(0, 4, 1, 2, 3)
        lru_p[:, b] = r["o_lru"].transpose(0, 2, 1).reshape(2, 1024)
        lconv_p[:, b] = r["o_lconv"].transpose(0, 3, 2, 1).reshape(2, 3, 1024)
    SB = 32
    outs = {
        "y_prompt": y_prompt,
        "gdn_p": gdn_p, "gdn_conv_p": gconv_p,
        "swa1_p": swa_p[0], "swa2_p": swa_p[1], "swa3_p": swa_p[2],
        "lru_p": lru_p, "lru_conv_p": lconv_p, "mem_p": mem_p,
    }
    if cfg.get("debug"):
        outs["dbg_mix"] = R[0]["dbg_mix"].reshape(1536, SEQ).T
    y_sample = np.zeros((SB, 1, D), np.float32)
    gdn_s = np.zeros((2, SB, 4, 128, 128), np.float32)
    gconv_s = np.zeros((2, SB, 3, 1536), np.float32)
    swa_s = [np.zeros((2, SB, GROUPS[g][0], 2, 4, 128), np.float32) for g in range(3)]
    lru_s = np.zeros((2, SB, 1024), np.float32)
    lconv_s = np.zeros((2, SB, 3, 1024), np.float32)
    if SAMPLE:
        for c in range(NCORES):
            r = R[c]
            sl_ = slice(c * SPC, (c + 1) * SPC)
            y_sample[sl_, 0] = r["o_ys"]
            gdn_s[:, sl_] = r["o_gdn_s"]
            gconv_s[:, sl_] = r["o_gconv_s"]
            for g in range(3):
                swa_s[g][:, sl_] = r["o_swa%d_s" % (g + 1)]
            lru_s[:, sl_] = r["o_lru_s"]
            lconv_s[:, sl_] = r["o_lconv_s"]
    if cfg.get("as_dict"):
        outs.update({"y_sample": y_sample, "gdn_s": gdn_s, "gdn_conv_s": gconv_s, "swa1_s": swa_s[0],
                     "swa2_s": swa_s[1], "swa3_s": swa_s[2], "lru_s": lru_s, "lru_conv_s": lconv_s})
        return outs
    return (y_prompt, y_sample, gdn_p, gdn_s, gconv_p, gconv_s, swa_p[0], swa_s[0], swa_p[1], swa_s[1],
            swa_p[2], swa_s[2], lru_p, lru_s, lconv_p, lconv_s, mem_p)
```

```python
import numpy as np
from contextlib import ExitStack
import concourse.bass as bass
import concourse.mybir as mybir
from concourse.bass_utils import run_bass_kernel_spmd

F32 = mybir.dt.float32
BF16 = mybir.dt.bfloat16
AF = mybir.ActivationFunctionType
ALU = mybir.AluOpType
AX = mybir.AxisListType

D = 1024
SEQ = 4096
ST = 2048
NST = SEQ // ST
EPS = 1e-6
MEM = 256
NCORES = 8
SPC = 4


class Buf:
    __slots__ = ("w", "r", "name", "excl")

    def __init__(self, name="", excl=False):
        self.w = {}
        self.r = {}
        self.name = name
        self.excl = excl


def inherit(new_bufs, old_bufs):
    merged = {}
    for ob in old_bufs:
        for d in (ob.w, ob.r):
            for key, (sem, val) in d.items():
                if merged.get(key, (None, 0))[1] < val:
                    merged[key] = (sem, val)
    for nb in new_bufs:
        for key, (sem, val) in merged.items():
            if nb.w.get(key, (None, 0))[1] < val:
                nb.w[key] = (sem, val)


class V:
    __slots__ = ("ap", "bufs")

    def __init__(self, ap, bufs):
        self.ap = ap
        self.bufs = bufs


class Tl:
    def __init__(self, t, name, excl=False):
        self.t = t
        self.b = Buf(name, excl)

    def __getitem__(self, key):
        return V(self.t[key], [self.b])

    def v(self, ap):
        return V(ap, [self.b])


class Eng:
    def __init__(self, k, name, eng, is_pe=False):
        self.k = k
        self.name = name
        self.eng = eng
        self.is_pe = is_pe
        self.sem = k.new_sem("c_" + name)
        self.cnt = 0
        self.seen = {}
        self.nsem = 1

    def wait(self, ev):
        if ev is None:
            return
        sem, val = ev
        if self.seen.get(id(sem), 0) >= val:
            return
        self.eng.wait_ge(sem, val)
        self.seen[id(sem)] = val

    def collect(self, R, W):
        need = {}

        def add(ev, own_ok):
            if ev is None:
                return
            sem, val = ev
            if sem is self.sem and own_ok:
                return
            if need.get(id(sem), (None, 0))[1] < val:
                need[id(sem)] = (sem, val)

        for v in R:
            for b in v.bufs:
                for ev in b.w.values():
                    add(ev, self.is_pe)
                if b.excl:
                    for sem, val in b.r.values():
                        add((sem, val), True)
        for v in W:
            for b in v.bufs:
                for ev in b.w.values():
                    add(ev, self.is_pe)
                for sem, val in b.r.values():
                    add((sem, val), self.is_pe)
        for ev in need.values():
            self.wait(ev)

    def issue(self, fn, R, W):
        self.collect(R, W)
        ins = fn()
        if self.cnt >= 30000:
            self.nsem += 1
            self.sem = self.k.new_sem("c_%s%d" % (self.name, self.nsem))
            self.cnt = 0
        self.cnt += 1
        ins.then_inc(self.sem, 1)
        ev = (self.sem, self.cnt)
        self.k.ninstr += 1
        for v in R:
            for b in v.bufs:
                b.r[id(self.sem)] = ev
        for v in W:
            for b in v.bufs:
                b.w = {id(self.sem): ev}
                b.r = {}
        return ins


class K:
    def __init__(self):
        self.nc = bass.Bass("TRN2", target_bir_lowering=False)
        self.es = ExitStack()
        self.ninstr = 0
        nc = self.nc
        self.pe = Eng(self, "pe", nc.tensor, is_pe=True)
        self.act = Eng(self, "act", nc.scalar)
        self.dve = Eng(self, "dve", nc.vector)
        self.pool = Eng(self, "pool", nc.gpsimd)
        self.sp = Eng(self, "sp", nc.sync)
        self.dsem = {"sp": [[self.new_sem("d%d" % i), 0] for i in range(16)],
                     "pool": [[self.new_sem("e%d" % i), 0] for i in range(12)]}
        self.dnext = {"sp": 0, "pool": 0}
        self.out_events = []
        self.bulk_sem = None
        self.bulk_n = 0
        self.in_names = []
        self.out_names = []

    def new_sem(self, name):
        return self.es.enter_context(self.nc.semaphore(name))

    def sb(self, name, shape, dt=F32):
        return Tl(self.es.enter_context(self.nc.sbuf_tensor(name, list(shape), dt)), name)

    def ps(self, name, shape, dt=F32):
        return Tl(self.es.enter_context(self.nc.psum_tensor(name, list(shape), dt)), name, excl=True)

    def din(self, name, shape, dt=F32):
        self.in_names.append(name)
        return Tl(self.nc.dram_tensor(name, list(shape), dt, kind="ExternalInput"), name)

    def dout(self, name, shape, dt=F32):
        self.out_names.append(name)
        return Tl(self.nc.dram_tensor(name, list(shape), dt, kind="ExternalOutput"), name)

    def dtmp(self, name, shape, dt=F32):
        return Tl(self.nc.dram_tensor(name, list(shape), dt, kind="Internal"), name)

    def dma(self, out, in_, q=None, is_output=False):
        q = q or self.sp
        pool = self.dsem[q.name]
        slot = pool[self.dnext[q.name]]
        self.dnext[q.name] = (self.dnext[q.name] + 1) % len(pool)
        if slot[1]:
            q.wait((slot[0], slot[1]))
        q.collect([in_], [out])
        ins = q.eng.dma_start(out=out.ap, in_=in_.ap)
        slot[1] += 16
        ins.then_inc(slot[0], 16)
        ev = (slot[0], slot[1])
        for b in in_.bufs:
            b.r[id(slot[0])] = ev
        for b in out.bufs:
            b.w = {id(slot[0]): ev}
            b.r = {}
        if is_output:
            self.out_events.append(ev)
        self.ninstr += 1

    def mm(self, out, lhsT, rhs, start=True, stop=True, sgc=False):
        if sgc:
            self.pe.issue(lambda: self.nc.tensor.matmul(out.ap, lhsT=lhsT.ap, rhs=rhs.ap, start=start, stop=stop,
                                                        skip_group_check=True), [lhsT, rhs], [out])
        else:
            self.pe.issue(lambda: self.nc.tensor.matmul(out.ap, lhsT=lhsT.ap, rhs=rhs.ap, start=start, stop=stop),
                          [lhsT, rhs], [out])

    def reduce_x(self, out, in_, op=ALU.add):
        self.dve.issue(lambda: self.nc.vector.tensor_reduce(out=out.ap, in_=in_.ap, axis=AX.X, op=op), [in_], [out])

    def bulk_copy(self, out_ap, in_ap):
        if self.bulk_sem is None:
            self.bulk_sem = self.new_sem("bulk")
        ins = self.nc.scalar.dma_start(out=out_ap, in_=in_ap)
        self.bulk_n += 16
        ins.then_inc(self.bulk_sem, 16)

    def tr(self, out, in_, ident):
        self.pe.issue(lambda: self.nc.tensor.transpose(out.ap, in_.ap, ident.ap), [in_, ident], [out])

    def actf(self, out, in_, func, bias=None, scale=None, eng=None):
        R = [in_]
        kw = {}
        if bias is not None:
            if isinstance(bias, V):
                R.append(bias)
                kw["bias"] = bias.ap
            else:
                kw["bias"] = bias
        if scale is not None:
            if isinstance(scale, V):
                R.append(scale)
                kw["scale"] = scale.ap
            else:
                kw["scale"] = scale
        self.act.issue(lambda: self.nc.scalar.activation(out=out.ap, in_=in_.ap, func=func, **kw), R, [out])

    def _ve(self, eng):
        return eng or self.dve

    def tt(self, out, a, b, op, eng=None):
        e = self._ve(eng)
        e.issue(lambda: e.eng.tensor_tensor(out=out.ap, in0=a.ap, in1=b.ap, op=op), [a, b], [out])

    def ts(self, out, a, s1, op0, s2=None, op1=None, eng=None):
        e = self._ve(eng)
        R = [a]
        s1a = s1
        s2a = s2
        if isinstance(s1, V):
            R.append(s1)
            s1a = s1.ap
        if isinstance(s2, V):
            R.append(s2)
            s2a = s2.ap
        if op1 is None:
            e.issue(lambda: e.eng.tensor_scalar(out=out.ap, in0=a.ap, scalar1=s1a, scalar2=None, op0=op0), R, [out])
        else:
            e.issue(lambda: e.eng.tensor_scalar(out=out.ap, in0=a.ap, scalar1=s1a, scalar2=s2a, op0=op0, op1=op1),
                    R, [out])

    def stt(self, out, a, s, b, op0, op1, eng=None):
        e = self._ve(eng)
        R = [a, b]
        sa = s
        if isinstance(s, V):
            R.append(s)
            sa = s.ap
        e.issue(lambda: e.eng.scalar_tensor_tensor(out=out.ap, in0=a.ap, scalar=sa, in1=b.ap, op0=op0, op1=op1),
                R, [out])

    def cp(self, out, in_, eng=None):
        e = self._ve(eng)
        e.issue(lambda: e.eng.tensor_copy(out=out.ap, in_=in_.ap), [in_], [out])

    def acp(self, out, in_):
        self.actf(out, in_, AF.Copy)

    def recip(self, out, in_):
        self.dve.issue(lambda: self.nc.vector.reciprocal(out=out.ap, in_=in_.ap), [in_], [out])

    def memset(self, out, val, eng=None):
        e = self._ve(eng)
        e.issue(lambda: e.eng.memset(out.ap, val), [], [out])

    def scan(self, out, d0, d1, init, op0=ALU.mult, op1=ALU.add):
        R = [d0, d1]
        ia = init
        if isinstance(init, V):
            R.append(init)
            ia = init.ap
        self.dve.issue(lambda: self.nc.vector.tensor_tensor_scan(out=out.ap, data0=d0.ap, data1=d1.ap, initial=ia,
                                                                 op0=op0, op1=op1), R, [out])

    def finish(self):
        best = {}
        for sem, val in self.out_events:
            if best.get(id(sem), (None, 0))[1] < val:
                best[id(sem)] = (sem, val)
        for ev in best.values():
            self.sp.wait(ev)
        if self.bulk_sem is not None:
            self.sp.wait((self.bulk_sem, self.bulk_n))
        self.es.close()


HD = 128
SCALE = float(128 ** -0.5)
GROUPS = ((128, 1), (512, 4), (2048, 16))
NEG = -30000.0

C_ID, C_ONE, C_NSL, C_NIU, C_TRIU, C_SEL, C_MP, C_MC, C_NM1 = range(9)
C_LV = 9
NCST = 15


def even_order():
    items = []
    for h in range(4):
        items.append(("mem", h, [7176 + h * 128, 7688 + h * 128]))
    for h in range(4):
        cols = []
        for g in range(3):
            for t in range(3):
                cols.append(2056 + (t * 12 + g * 4 + h) * 128)
        cols.append(6664 + h * 128)
        items.append(("swa", h, cols))
    for h in range(4):
        items.append(("gdn", h, [h * 128, 512 + h * 128, 1024 + h * 128, 1536 + h * 128]))
    return items


def odd_order():
    items = []
    for n in range(4):
        items.append(("lru", n, [(2 * n) * 128, (2 * n + 1) * 128, 1024 + (2 * n) * 128, 1024 + (2 * n + 1) * 128]))
    for h in range(4):
        items.append(("mem", h, [2048 + h * 128, 2560 + h * 128]))
    return items


class WStream:
    def __init__(self, k, plan, wst, wbf):
        self.k = k
        self.plan = plan
        self.wst = wst
        self.wbf = wbf
        self.n_dma = 0
        self.n_cast = 0
        self.n_use = 0

    def _dma(self):
        i = self.n_dma
        src, nk = self.plan[i]
        t = self.wst[i % len(self.wst)]
        self.k.dma(t.v(t.t[:, 0:nk * 128].rearrange("p (k m) -> p k m", k=nk)), src)
        self.n_dma += 1

    def _cast(self):
        i = self.n_cast
        src, nk = self.plan[i]
        a = self.wst[i % len(self.wst)]
        b = self.wbf[i % len(self.wbf)]
        self.k.cp(b[:, 0:nk * 128], a[:, 0:nk * 128], eng=self.k.pool)
        self.n_cast += 1

    def get(self):
        i = self.n_use
        n = len(self.plan)
        while self.n_cast < min(n, i + 2):
            while self.n_dma <= self.n_cast:
                self._dma()
            self._cast()
        while self.n_dma < min(n, i + 3):
            self._dma()
        self.n_use += 1
        nk = self.plan[i][1]
        b = self.wbf[i % len(self.wbf)]
        return b.v(b.t[:, 0:nk * 128].rearrange("p (k m) -> p k m", k=nk))


def build_program(cfg):
    NL = cfg.get("nlayers", 4)
    k = K()
    nc = k.nc

    def view(tl, ap):
        return tl.v(ap)

    cst_d = k.din("cst", [128, NCST, 128])
    xin_d = k.din("xT", [8, 128, SEQ])
    memT_d = k.din("memT", [128, 8, MEM])
    wkv_d = k.din("wkv", [4, 128, 8, 1024])
    gmem_d = k.din("gmem", [128, 4, 8])
    gpre_d = k.din("gpre", [128, 4, 8])
    gpost_d = k.din("gpost", [128, 4, 8])
    weven_d = k.din("w_even", [2, 64, 128, 8, 128])
    wodd_d = k.din("w_odd", [2, 24, 128, 8, 128])
    wout_d = k.din("w_out", [4, 8, 128, 12, 128])
    wba_d = k.din("wba", [2, 128, 8, 8])
    gconvw_d = k.din("gconvw", [2, 128, 12, 4])
    alog_d = k.din("alog", [2, 128, 64])
    dtb_d = k.din("dtb", [2, 128, 64])
    gnorm_d = k.din("gnorm", [2, 128, 1])
    lconvw_d = k.din("lconvw", [2, 128, 8, 4])
    lvec_d = k.din("lvec", [2, 128, 4, 8])
    lwa_d = k.din("lwa", [2, 128, 4, 2, 256])
    lwx_d = k.din("lwx", [2, 128, 4, 2, 256])

    memo_d = k.dout("o_memT", [4, 8, 128, MEM])
    yT_d = k.dout("o_yT", [8, 128, SEQ])
    ogdn_d = k.dout("o_gdn", [2, 4, 128, 128])
    ogconv_d = k.dout("o_gconv", [2, 128, 12, 3])
    oswa_d = [k.dout("o_swa%d" % (g + 1), [2, 2, 4, 128, GROUPS[g][0]]) for g in range(3)]
    olru_d = k.dout("o_lru", [2, 128, 8])
    olconv_d = k.dout("o_lconv", [2, 128, 8, 3])

    xsT_d = k.din("xsT", [128, 8, 4])
    sgdn_d = k.din("sgdn", [2, 4, 4, 128, 128])
    sgconv_d = k.din("sgconv", [2, 128, 12, 4, 3])
    sgconv_nat = k.din("sgconv_nat", [2, 4, 3, 1536])
    cswa_d = [k.din("cswa%d" % (g + 1), [2, 4, GROUPS[g][0], 2, 4, 128]) for g in range(3)]
    cmem_d = k.din("cmem", [4, 4, MEM, 2, 4, 128])
    slru_d = k.din("slru", [2, 128, 8, 4])
    slconv_d = k.din("slconv", [2, 128, 8, 4, 3])
    slconv_nat = k.din("slconv_nat", [2, 4, 3, 1024])
    oys_d = k.dout("o_ys", [4, 1024])
    ogdns_d = k.dout("o_gdn_s", [2, 4, 4, 128, 128])
    ogconvs_d = k.dout("o_gconv_s", [2, 4, 3, 1536])
    oswas_d = [k.dout("o_swa%d_s" % (g + 1), [2, 4, GROUPS[g][0], 2, 4, 128]) for g in range(3)]
    olrus_d = k.dout("o_lru_s", [2, 4, 1024])
    olconvs_d = k.dout("o_lconv_s", [2, 4, 3, 1024])

    xs_d = [k.dtmp("xs%d" % i, [8, 128, SEQ]) for i in range(2)]
    histk_d = [[k.dtmp("hk%d_%d" % (g, h), [128, GROUPS[g][0]], BF16) for h in range(4)] for g in range(3)]
    histv_d = [[k.dtmp("hv%d_%d" % (g, h), [128, GROUPS[g][1], 128], BF16) for h in range(4)] for g in range(3)]

    cst = k.sb("cst_sb", [128, NCST, 128])
    k.dma(cst[:], cst_d[:, :, :])
    ident_f = view(cst, cst.t[:, C_ID, :])
    ones_f = view(cst, cst.t[:, C_ONE, :])
    ident_b = k.sb("ident_b", [128, 128], BF16)
    ones_b = k.sb("ones_b", [128, 128], BF16)
    k.cp(ident_b[:], ident_f)
    k.cp(ones_b[:], ones_f)
    mP_gen = k.sb("mP_gen", [128, 4, 128], BF16)
    mP_f1 = k.sb("mP_f1", [128, 4, 128], BF16)
    mP_zero = k.sb("mP_zero", [128, 4, 128], BF16)
    mC_gen = k.sb("mC_gen", [128, 4, 128], BF16)
    k.memset(mP_zero[:], 0.0)
    k.memset(mP_f1[:, 0, :], 0.0)
    for q in range(4):
        k.cp(mP_gen[:, q, :], cst[:, C_MP, :])
        k.cp(mC_gen[:, q, :], cst[:, C_MC, :])
        if q > 0:
            k.cp(mP_f1[:, q, :], cst[:, C_MP, :])

    PS = [k.ps("psb%d" % i, [128, 512]) for i in range(8)]
    PSb6 = view(PS[6], PS[6].t[:, :].bitcast(BF16))

    kT_mem1 = k.sb("kTm", [128, 4, MEM], BF16)
    v_mem1 = k.sb("vm", [128, 2, 512], BF16)
    kT_mem = [kT_mem1] * 4
    v_mem = [v_mem1] * 4
    rs_mem = k.sb("rs_mem", [128, MEM])
    gmem = k.sb("gmem_sb", [128, 4, 8])
    k.dma(gmem[:], gmem_d[:, :, :])
    gpre = k.sb("gpre_sb", [128, 4, 8])
    gpost = k.sb("gpost_sb", [128, 4, 8])
    k.dma(gpre[:], gpre_d[:, :, :])
    k.dma(gpost[:], gpost_d[:, :, :])
    hT = [k.sb("hT%d" % c, [128, ST], BF16) for c in range(8)]
    mixT = [k.sb("mixT%d" % c, [128, ST], BF16) for c in range(12)]
    wst = [k.sb("wst%d" % i, [128, 8 * 128]) for i in range(2)]
    wbf = [k.sb("wbf%d" % i, [128, 8 * 128], BF16) for i in range(3)]
    F = [k.sb("F%d" % i, [128, ST + 3]) for i in range(5)]
    FA, FB, FC, FD, FE = F
    hbig = k.es.enter_context(nc.sbuf_tensor("hbig", [128, 12288], BF16))
    HA = Tl(hbig[:, 0:4096], "HA")
    HB = Tl(hbig[:, 4096:8192], "HB")
    HC = Tl(hbig[:, 8192:10240], "HC")
    HDt = Tl(hbig[:, 10240:12288], "HD")
    WOUT = V(hbig[:, :].rearrange("p (k c) -> p k c", k=12), [HA.b, HB.b, HC.b, HDt.b])
    Pt = [k.sb("Pt%d" % i, [128, 512], BF16) for i in range(2)]
    rs = k.sb("rs", [128, 512])
    rs2 = k.sb("rs2", [128, 512])

    plan = []
    for l in range(NL):
        j = l // 2
        for s in range(NST):
            if l % 2 == 0:
                for c in range(64):
                    plan.append((weven_d.v(weven_d.t[j, c]), 8))
            else:
                for c in range(24):
                    plan.append((wodd_d.v(wodd_d.t[j, c]), 8))
    W = WStream(k, plan, wst, wbf)

    def mem_phase(l):
        wbf0 = [view(HA, HA.t[:, 0:4096].rearrange("p (c m) -> p c m", c=8)),
                view(HB, HB.t[:, 0:4096].rearrange("p (c m) -> p c m", c=8))]
        k.dma(view(FA, FA.t[:, 0:2048].rearrange("p (c m) -> p c m", c=8)), memT_d[:, :, :])
        if l == 0:
            k.actf(FB[:, 0:2048], FA[:, 0:2048], AF.Square)
            for kk in range(8):
                k.mm(PS[6][:, 0:MEM], ones_f, FB[:, kk * MEM:(kk + 1) * MEM], start=(kk == 0), stop=(kk == 7))
            k.actf(rs_mem[:], PS[6][:, 0:MEM], AF.Sqrt, bias=EPS, scale=1.0 / D)
            k.recip(rs_mem[:], rs_mem[:])
        for c8 in range(8):
            stg = (FD, FE)[c8 % 2]
            sv = stg.v(stg.t[:, 0:1024].rearrange("p (k m) -> p k m", k=8))
            k.dma(sv, wkv_d[l, :, :, c8 * 128:(c8 + 1) * 128])
            dst = wbf0[c8 // 4]
            k.cp(V(dst.ap[:, :, (c8 % 4) * 128:(c8 % 4 + 1) * 128], dst.bufs), sv)
        for kk in range(8):
            k.stt(HC[:, kk * MEM:(kk + 1) * MEM], FA[:, kk * MEM:(kk + 1) * MEM],
                  gmem[:, l, kk:kk + 1], rs_mem[:], ALU.mult, ALU.mult)
        for half in range(2):
            wb = [HA, HB][half]
            for c in range(4):
                pb = PS[c % 2]
                for kk in range(8):
                    k.mm(pb[:, 0:MEM], wb[:, kk * 512 + c * 128:kk * 512 + (c + 1) * 128],
                         HC[:, kk * MEM:(kk + 1) * MEM], start=(kk == 0), stop=(kk == 7))
                cc = half * 4 + c
                k.cp(FC[:, cc * MEM:(cc + 1) * MEM], pb[:, 0:MEM])
                if half == 0:
                    k.acp(kT_mem1[:, c, :], pb[:, 0:MEM])
            if half == 1:
                for jb in range(2):
                    pb = PS[2 + jb]
                    for kk in range(8):
                        k.mm(pb[:, :], HC[:, kk * MEM + jb * 128:kk * MEM + (jb + 1) * 128],
                             wb[:, kk * 512:(kk + 1) * 512], start=(kk == 0), stop=(kk == 7))
                    k.acp(v_mem1[:, jb, :], pb[:, :])
        k.dma(memo_d.v(memo_d.t[l].rearrange("c p m -> p c m")),
              view(FC, FC.t[:, 0:2048].rearrange("p (c m) -> p c m", c=8)), q=k.pool, is_output=True)

    hs = k.sb("hs_sb", [128, 8, 4], BF16)
    sproj = k.sb("sproj", [128, 64, 4])
    sctx = {"s": 0, "slot": 0, "on": cfg.get("sample", True)}

    def getw():
        wv = W.get()
        if sctx["on"] and sctx["s"] == 0:
            sl = sctx["slot"]
            sctx["slot"] += 1
            for kk in range(8):
                k.mm(PS[7][:, 0:4], view_w(wv, kk), hs[:, kk, :], start=(kk == 0), stop=(kk == 7))
            k.cp(sproj[:, sl, :], PS[7][:, 0:4])
        return wv

    proj_rot = [0]

    def proj(wv, tb):
        pb = PS[proj_rot[0] % 2]
        proj_rot[0] += 1
        for kk in range(8):
            k.mm(pb[:, :], view_w(wv, kk), hT[kk][:, tb * 512:(tb + 1) * 512], start=(kk == 0), stop=(kk == 7))
        return pb

    def view_w(wv, kk):
        return V(wv.ap[:, kk, :], wv.bufs)

    def rstd_act(out, ps, scale):
        k.actf(out, ps, AF.Ln, bias=EPS, scale=scale)
        k.actf(out, out, AF.Exp, scale=-0.5)

    def blk(tl, tb, off=0, n=512):
        return tl[:, off + tb * n: off + (tb + 1) * n]

    def norm_phase(l, s, xsrc):
        xbufs = ((FA, FB), (FC, FD))

        def xload(tb_):
            t0_ = s * ST + tb_ * 512
            for hf, Fx in enumerate(xbufs[tb_ % 2]):
                k.dma(view(Fx, Fx.t[:, 0:2048].rearrange("p (c t) -> p c t", c=4)),
                      xsrc.v(xsrc.t[hf * 4:(hf + 1) * 4, :, t0_:t0_ + 512].rearrange("c p t -> p c t")))
        xload(0)
        for tb in range(4):
            if tb + 1 < 4:
                xload(tb + 1)
            XA, XB = xbufs[tb % 2]
            k.actf(HC[:, 0:2048], XA[:, 0:2048], AF.Square)
            k.actf(HDt[:, 0:2048], XB[:, 0:2048], AF.Square)
            for c in range(8):
                Hx = (HC, HDt)[c // 4]
                k.mm(PS[6][:, :], ones_b[:], blk(Hx, c % 4), start=(c == 0), stop=(c == 7))
            rstd_act(rs[:], PS[6][:, :], 1.0 / D)
            for c in range(8):
                Fx = (XA, XB)[c // 4]
                k.stt(blk(hT[c], tb), blk(Fx, c % 4), gpre[:, l, c:c + 1], rs[:], ALU.mult, ALU.mult)

    def mem_item(l, h, mix_idx):
        wq = getw()
        for tb in range(4):
            pb = proj(wq, tb)
            k.acp(blk(HC, tb), pb[:, :])
        wg = getw()
        for tb in range(4):
            pb = proj(wg, tb)
            k.actf(blk(FA, tb), pb[:, :], AF.Silu)
        def mscores(tb_):
            banks = ((PS[2], PS[3]), (PS[6], PS[7]))[tb_ % 2]
            for c in range(2):
                k.mm(banks[c][:, :], kT_mem[l][:, h, c * 128:(c + 1) * 128], blk(HC, tb_))
        def mepi(tb_):
            nb_, db_ = ((PS[4], PS[5]), (PS[0], PS[1]))[tb_ % 2]
            k.actf(rs2[:], db_[:, :], AF.Ln)
            k.actf(rs2[:], rs2[:], AF.Exp, scale=-1.0)
            k.tt(rs2[:], rs2[:], nb_[:, :], ALU.mult)
            k.tt(blk(mixT[mix_idx], tb_), rs2[:], blk(FA, tb_), ALU.mult)

        mscores(0)
        for tb in range(4):
            banks = ((PS[2], PS[3]), (PS[6], PS[7]))[tb % 2]
            nb_, db_ = ((PS[4], PS[5]), (PS[0], PS[1]))[tb % 2]
            if tb + 1 < 4:
                mscores(tb + 1)
            for c in range(2):
                k.actf(Pt[c][:], banks[c][:, :], AF.Exp, scale=SCALE)
            for c in range(2):
                k.mm(nb_[:, :], v_mem[l][:, c, h * 128:(h + 1) * 128], Pt[c][:], start=(c == 0), stop=(c == 1))
            for c in range(2):
                k.mm(db_[:, :], ones_b[:], Pt[c][:], start=(c == 0), stop=(c == 1))
            if tb >= 1:
                mepi(tb - 1)
        mepi(3)

    def swa_item(l, s, h):
        j = l // 2
        acc_n, acc_d, gate = FB, FC, FD
        for g in range(3):
            win, d = GROUPS[g]
            span = win

            def rm_out(tl, off, tb):
                if d == 1:
                    return tl[:, off + tb * 512: off + (tb + 1) * 512]
                if d == 4:
                    return tl.v(tl.t[:, off + tb * 512: off + (tb + 1) * 512].rearrange("p (r i) -> p r i", r=4))
                return tl.v(tl.t[:, off:off + 2048].rearrange("p (r i) -> p r i", r=16)[:, :, tb * 32:(tb + 1) * 32])

            def rm_in(pb):
                if d == 1:
                    return pb[:, :]
                return pb.v(pb.t[:, :].rearrange("p (i r) -> p r i", r=d))

            wq = getw()
            for tb in range(4):
                pb = proj(wq, tb)
                k.acp(rm_out(HC, 0, tb), rm_in(pb))
            if s == 0:
                k.memset(HA[:, 0:span], 0.0)
            else:
                k.dma(HA[:, 0:span], histk_d[g][h][:, :])
            wk = getw()
            for tb in range(4):
                pb = proj(wk, tb)
                k.acp(rm_out(HA, span, tb), rm_in(pb))
                if s == NST - 1:
                    k.cp(blk(FA, tb), pb[:, :])
            if s == NST - 1:
                k.dma(oswa_d[g][j, 0, h, :, :], FA[:, ST - win:ST], q=k.pool, is_output=True)
            if s == 0:
                k.dma(histk_d[g][h][:, :], HA[:, ST:ST + span], q=k.pool)
            if s == 0:
                k.memset(HB[:, 0:d * 128], 0.0)
            else:
                k.dma(HB.v(HB.t[:, 0:d * 128].rearrange("p (b c) -> p b c", b=d)), histv_d[g][h][:, :, :])
            wv = getw()
            for tb in range(4):
                pb = proj(wv, tb)
                k.acp(rm_out(HDt, 0, tb), rm_in(pb))
                if s == NST - 1:
                    k.cp(blk(FE, tb), pb[:, :])
            if s == NST - 1:
                k.dma(oswa_d[g][j, 1, h, :, :], FE[:, ST - win:ST], q=k.pool, is_output=True)
            for b4 in range(4):
                for q in range(4):
                    bi = b4 * 4 + q
                    k.tr(view(PS[6], PSb6.ap[:, q * 128:(q + 1) * 128]), HDt[:, bi * 128:(bi + 1) * 128], ident_b[:])
                k.cp(HB[:, (d + b4 * 4) * 128:(d + b4 * 4 + 4) * 128], view(PS[6], PSb6.ap[:, 0:512]))
            if s == 0:
                k.dma(histv_d[g][h][:, :, :], HB.v(HB.t[:, 16 * 128:(16 + d) * 128].rearrange("p (b c) -> p b c", b=d)),
                      q=k.pool)
            def scores(rd_):
                pa, pb_ = ((PS[2], PS[3]), (PS[6], PS[7]))[rd_ % 2]
                for q in range(4):
                    bi = rd_ * 4 + q
                    qa = HC[:, bi * 128:(bi + 1) * 128]
                    kprev = HA[:, bi * 128:(bi + 1) * 128]
                    kcur = HA[:, span + bi * 128: span + (bi + 1) * 128]
                    k.mm(pa[:, q * 128:(q + 1) * 128], kprev, qa)
                    k.mm(pb_[:, q * 128:(q + 1) * 128], kcur, qa)
            def accum(rd_):
                nbk, dbk = ((PS[4], PS[5]), (PS[0], PS[1]))[rd_ % 2]
                for accb, pbank in ((acc_n, nbk), (acc_d, dbk)):
                    if d == 1:
                        dst = accb[:, rd_ * 512:(rd_ + 1) * 512]
                        src = pbank[:, :]
                    elif d == 4:
                        dst = accb.v(accb.t[:, rd_ * 512:(rd_ + 1) * 512].rearrange("p (i r) -> p r i", r=4))
                        src = pbank.v(pbank.t[:, :].rearrange("p (r i) -> p r i", r=4))
                    else:
                        dst = accb.v(accb.t[:, 0:2048].rearrange("p (i r) -> p r i", r=16)[:, rd_ * 4:(rd_ + 1) * 4, :])
                        src = pbank.v(pbank.t[:, :].rearrange("p (r i) -> p r i", r=4))
                    if g == 0:
                        k.cp(dst, src)
                    else:
                        k.tt(dst, dst, src, ALU.add)

            scores(0)
            for rd in range(4):
                pa, pb_ = ((PS[2], PS[3]), (PS[6], PS[7]))[rd % 2]
                nbk, dbk = ((PS[4], PS[5]), (PS[0], PS[1]))[rd % 2]
                if rd + 1 < 4:
                    scores(rd + 1)
                k.actf(Pt[0][:], pa[:, :], AF.Exp, scale=SCALE)
                k.actf(Pt[1][:], pb_[:, :], AF.Exp, scale=SCALE)
                if s == 0 and ((d == 1 and rd == 0)):
                    mp = mP_f1
                elif s == 0 and ((d == 4 and rd == 0) or d == 16):
                    mp = mP_zero
                else:
                    mp = mP_gen
                k.tt(Pt[0][:], Pt[0][:], mp.v(mp.t[:, :, :].rearrange("p a b -> p (a b)")), ALU.mult)
                k.tt(Pt[1][:], Pt[1][:], mC_gen.v(mC_gen.t[:, :, :].rearrange("p a b -> p (a b)")), ALU.mult)
                for q in range(4):
                    bi = rd * 4 + q
                    vprev = HB[:, bi * 128:(bi + 1) * 128]
                    vcur = HB[:, (d + bi) * 128:(d + bi + 1) * 128]
                    k.mm(nbk[:, q * 128:(q + 1) * 128], vprev, Pt[0][:, q * 128:(q + 1) * 128], start=True, stop=False)
                    k.mm(nbk[:, q * 128:(q + 1) * 128], vcur, Pt[1][:, q * 128:(q + 1) * 128], start=False, stop=True)
                k.mm(dbk[:, :], ones_b[:], Pt[0][:], start=True, stop=False)
                k.mm(dbk[:, :], ones_b[:], Pt[1][:], start=False, stop=True)
                if rd >= 1:
                    accum(rd - 1)
            accum(3)
        wg = getw()
        for tb in range(4):
            pb = proj(wg, tb)
            k.actf(blk(gate, tb), pb[:, :], AF.Silu)
        k.actf(acc_d[:, 0:ST], acc_d[:, 0:ST], AF.Ln)
        k.actf(acc_d[:, 0:ST], acc_d[:, 0:ST], AF.Exp, scale=-1.0)
        k.tt(acc_n[:, 0:ST], acc_n[:, 0:ST], acc_d[:, 0:ST], ALU.mult)
        k.tt(mixT[4 + h][:, :], acc_n[:, 0:ST], gate[:, 0:ST], ALU.mult)


    gs = {nm: k.sb("g_" + nm, [128, 64]) for nm in ("beta", "g", "gc", "ngc", "bg", "egl", "glb", "egs", "tmp")}
    wba_f = k.sb("wba_sf", [128, 8, 8])
    wba_b = k.sb("wba_b", [128, 8, 8], BF16)
    gconvw = k.sb("gconvw_sb", [128, 12, 4])
    alog = k.sb("alog_sb", [128, 64])
    dtb = k.sb("dtb_sb", [128, 64])
    negA = k.sb("negA", [128, 64])
    gnorm = k.sb("gnorm_sb", [128, 1])
    halo = k.sb("halo", [128, 12, 3])
    S = k.sb("S", [128, 4, 128])
    Sb = k.sb("Sb", [128, 4, 128], BF16)
    alt = k.es.enter_context(nc.sbuf_tensor("alt", [128, 3072], F32))
    qt = {}
    for i, nm in enumerate(("A", "Al", "T", "M", "X", "Kbg", "Vb", "EGB", "WT0", "WT1", "qg0", "qg1")):
        qt[nm] = Tl(alt[:, i * 256:(i + 1) * 256].bitcast(BF16), "q_" + nm)
    allb = [t.b for t in qt.values()]
    fa_names, fe_names = [], []
    for nm, lo, hi, dt_ in (("gRep", 0, 512, F32), ("D1", 512, 1024, F32), ("D2", 1024, 1536, F32),
                            ("attnT0", 1536, 1792, BF16), ("attnT1", 1792, 2048, BF16)):
        ap_ = FA.t[:, lo:hi]
        qt[nm] = Tl(ap_.bitcast(BF16) if dt_ == BF16 else ap_, "q_" + nm)
        fa_names.append(nm)
    for nm, lo, hi, dt_ in (("U0", 0, 512, F32), ("U1", 512, 1024, F32), ("kdec0", 1024, 1280, BF16),
                            ("kdec1", 1280, 1536, BF16), ("vnew0", 1536, 1600, BF16), ("vnew1", 1600, 1664, BF16)):
        ap_ = FE.t[:, lo:hi]
        qt[nm] = Tl(ap_.bitcast(BF16) if dt_ == BF16 else ap_, "q_" + nm)
        fe_names.append(nm)
    fa_bufs = [qt[n].b for n in fa_names]
    fe_bufs = [qt[n].b for n in fe_names]
    ALV = [Tl(hbig[:, i * 512:(i + 1) * 512], "alv%d" % i) for i in range(6)]
    alv_bufs = [t.b for t in ALV]
    cTRIU = cst[:, C_TRIU, :]

    def even_init(l):
        j = l // 2
        k.dma(wba_f[:], wba_d[j, :, :, :])
        k.cp(wba_b[:], wba_f[:])
        k.dma(gconvw[:], gconvw_d[j, :, :, :])
        k.dma(alog[:], alog_d[j, :, :])
        k.dma(dtb[:], dtb_d[j, :, :])
        k.dma(gnorm[:], gnorm_d[j, :, :])
        k.actf(negA[:], alog[:], AF.Exp)
        k.ts(negA[:], negA[:], -1.0, ALU.mult)
        k.memset(halo[:], 0.0)
        k.memset(S[:], 0.0)
        k.memset(Sb[:], 0.0)

    def gdn_prep():
        for bl in range(16):
            for kk in range(8):
                k.mm(PS[7][:, bl * 8:(bl + 1) * 8], hT[kk][:, bl * 128:(bl + 1) * 128], wba_b[:, kk, :],
                     start=(kk == 0), stop=(kk == 7))
        ba = PS[7].t[:, 0:128].rearrange("p (b c) -> p b c", c=8)

        def g3(tl):
            return tl.v(tl.t[:, :].rearrange("p (b c) -> p b c", c=4))
        k.actf(g3(gs["beta"]), PS[7].v(ba[:, :, 0:4]), AF.Sigmoid)
        k.tt(g3(gs["tmp"]), PS[7].v(ba[:, :, 4:8]), g3(dtb), ALU.add)
        k.actf(gs["tmp"][:], gs["tmp"][:], AF.Exp)
        k.actf(gs["tmp"][:], gs["tmp"][:], AF.Ln, bias=1.0)
        k.tt(gs["g"][:], gs["tmp"][:], negA[:], ALU.mult)
        k.mm(PS[7][:, 128:192], cTRIU, gs["g"][:])
        k.cp(gs["gc"][:], PS[7][:, 128:192])
        k.ts(gs["ngc"][:], gs["gc"][:], -1.0, ALU.mult)
        k.mm(PS[7][:, 192:256], cst[:, C_SEL, :], gs["gc"][:])
        k.cp(gs["glb"][:], PS[7][:, 192:256])
        k.actf(gs["egs"][:], gs["glb"][:], AF.Exp)
        k.tt(gs["tmp"][:], gs["glb"][:], gs["gc"][:], ALU.subtract)
        k.actf(gs["egl"][:], gs["tmp"][:], AF.Exp)
        k.actf(gs["tmp"][:], gs["gc"][:], AF.Exp)
        k.tt(gs["bg"][:], gs["tmp"][:], gs["beta"][:], ALU.mult)

    rs4 = [Tl(hbig[:, i * 1024:(i + 1) * 1024].bitcast(F32), "rs4_%d" % i) for i in range(4)]
    for t_ in rs4:
        t_.b = HA.b

    def gdn_item(l, s, h):
        j = l // 2
        dsts = (FB, FC, FD)
        hdst = (HC, HDt, None)
        for t in range(3):
            ci = t * 4 + h
            wv = getw()
            Fp = (FA, FE, FA)[t]
            k.cp(Fp[:, 0:3], halo[:, ci, :])
            for tb in range(4):
                pb = proj(wv, tb)
                k.acp(blk(Fp, tb, off=3), pb[:, :])
            k.cp(halo[:, ci, :], Fp[:, ST:ST + 3])
            Fd = dsts[t]
            k.ts(Fd[:, 0:ST], Fp[:, 0:ST], gconvw[:, ci, 0:1], ALU.mult)
            for jj in range(1, 4):
                k.stt(Fd[:, 0:ST], Fp[:, jj:jj + ST], gconvw[:, ci, jj:jj + 1], Fd[:, 0:ST], ALU.mult, ALU.add)
            k.actf(Fd[:, 0:ST], Fd[:, 0:ST], AF.Silu)
            if t < 2:
                k.actf(HB[:, 0:ST], Fd[:, 0:ST], AF.Square)
                for tb in range(4):
                    k.mm(PS[2 + tb][:, :], ones_b[:], blk(HB, tb))
                for tb in range(4):
                    rstd_act(rs4[tb][:], PS[2 + tb][:, :], 1.0)
                for tb in range(4):
                    k.stt(blk(hdst[t], tb), blk(Fd, tb), (SCALE if t == 0 else 1.0), rs4[tb][:], ALU.mult, ALU.mult)

        def q3(tl):
            return tl.v(tl.t[:, :].rearrange("p (q c) -> p q c", q=4))

        def bc_in(nm, Q):
            t = gs[nm]
            ap_ = t.t[:, 16 * Q:16 * Q + 16].rearrange("p (q h) -> p q h", h=4)[:, :, h]
            return t.v(ap_.unsqueeze(2).to_broadcast([128, 4, 128]))

        def bc_mid(slot):
            return cst.v(cst.t[:, slot, :].unsqueeze(1).to_broadcast([128, 4, 128]))

        def prep(Q):
            par = Q % 2
            WT, qg, attnT, kdec, U = (qt["WT%d" % par], qt["qg%d" % par], qt["attnT%d" % par], qt["kdec%d" % par],
                                      qt["U%d" % par])
            A, Al, T, M, X, Kbg, Vb, EGB = (qt[n] for n in ("A", "Al", "T", "M", "X", "Kbg", "Vb", "EGB"))
            gRep, D1, D2 = qt["gRep"], qt["D1"], qt["D2"]
            c0 = Q * 512
            PSb4 = PS[4].v(PS[4].t[:, :].bitcast(BF16)[:, 0:512])
            blks = [(q_, c0 + q_ * 128, c0 + (q_ + 1) * 128) for q_ in range(4)]
            sl = lambda tl, q_: tl[:, q_ * 128:(q_ + 1) * 128]
            for q_, a0, a1 in blks:
                k.tr(V(PSb4.ap[:, q_ * 128:(q_ + 1) * 128], PSb4.bufs), HDt[:, a0:a1], ident_b[:])
            for q_, a0, a1 in blks:
                k.tr(sl(PS[5], q_), FD[:, a0:a1], ident_f)
            k.tt(q3(gRep), bc_mid(C_ONE), bc_in("g", Q), ALU.mult)
            yield
            k.tt(q3(Kbg), V(PSb4.ap.rearrange("p (q c) -> p q c", q=4), PSb4.bufs), bc_in("bg", Q), ALU.mult)
            k.tt(q3(kdec), V(PSb4.ap.rearrange("p (q c) -> p q c", q=4), PSb4.bufs), bc_in("egl", Q), ALU.mult)
            k.tt(q3(Vb), q3(PS[5]), bc_in("beta", Q), ALU.mult)
            for q_, a0, a1 in blks:
                k.mm(sl(PS[1], q_), sl(gRep, q_), cTRIU)
            yield
            k.actf(EGB[:, :], PS[1][:, :], AF.Exp)
            k.stt(q3(D1), q3(PS[1]), -1.0, bc_mid(C_NSL), ALU.mult, ALU.add)
            k.tt(q3(D2), q3(PS[1]), bc_mid(C_NIU), ALU.add)
            for q_, a0, a1 in blks:
                k.mm(sl(PS[7], q_), HDt[:, a0:a1], HDt[:, a0:a1])
            for q_, a0, a1 in blks:
                k.mm(sl(PS[5], q_), HDt[:, a0:a1], HC[:, a0:a1])
            yield
            k.tt(q3(D1), q3(D1), bc_in("gc", Q), ALU.add)
            k.tt(q3(D2), q3(D2), bc_in("ngc", Q), ALU.add)
            k.tt(qg[:, :], HC[:, c0:c0 + 512], EGB[:, :], ALU.mult)
            yield
            k.actf(D1[:, :], D1[:, :], AF.Exp)
            k.actf(D2[:, :], D2[:, :], AF.Exp)
            yield
            k.tt(D1[:, :], D1[:, :], PS[7][:, :], ALU.mult)
            k.tt(q3(A), q3(D1), bc_in("beta", Q), ALU.mult)
            k.tt(attnT[:, :], PS[5][:, :], D2[:, :], ALU.mult)
            yield
            k.tt(q3(Al), q3(A), bc_mid(C_NM1), ALU.mult)
            k.tt(q3(T), q3(Al), bc_mid(C_ID), ALU.add)
            for lv in range(2, 8):
                k.tt(q3(ALV[lv - 2]), q3(A), bc_mid(C_LV + lv - 2), ALU.mult)
            yield
            for q_, a0, a1 in blks:
                k.tr(V(PSb4.ap[:, q_ * 128:(q_ + 1) * 128], PSb4.bufs), sl(T, q_), ident_b[:])
            yield
            k.acp(M[:, :], PSb4)
            for lv in range(2, 8):
                for q_, a0, a1 in blks:
                    k.mm(sl(PS[2], q_), sl(ALV[lv - 2], q_), sl(M, q_))
                yield
                k.acp(X[:, :], PS[2][:, :])
                yield
                for q_, a0, a1 in blks:
                    k.mm(sl(PS[3], q_), sl(T, q_), sl(X, q_))
                yield
                k.tt(M[:, :], M[:, :], PS[3][:, :], ALU.subtract)
                yield
                if lv < 7:
                    for q_, a0, a1 in blks:
                        k.tr(V(PSb4.ap[:, q_ * 128:(q_ + 1) * 128], PSb4.bufs), sl(M, q_), ident_b[:])
                    yield
                    k.acp(T[:, :], PSb4)
                    yield
            for q_, a0, a1 in blks:
                k.mm(sl(PS[2], q_), sl(M, q_), sl(Vb, q_))
            for q_, a0, a1 in blks:
                k.mm(sl(PS[3], q_), sl(Kbg, q_), sl(M, q_))
            yield
            k.acp(U[:, :], PS[2][:, :])
            k.acp(WT[:, :], PS[3][:, :])
            yield

        def seq(Q):
            par = Q % 2
            WT, qg, attnT, kdec, U = (qt["WT%d" % par], qt["qg%d" % par], qt["attnT%d" % par], qt["kdec%d" % par],
                                      qt["U%d" % par])
            sl = lambda tl, q_: tl[:, q_ * 128:(q_ + 1) * 128]
            for q_ in range(4):
                b_ = Q * 4 + q_
                col = b_ * 4 + h
                c0, c1 = b_ * 128, (b_ + 1) * 128
                vnew = qt["vnew%d" % (q_ % 2)]
                k.mm(PS[0][:, 0:128], sl(WT, q_), Sb[:, h, :])
                yield
                k.tt(vnew[:, :], sl(U, q_), PS[0][:, 0:128], ALU.subtract)
                yield
                k.mm(PS[0][:, 128:256], Sb[:, h, :], sl(qg, q_), start=True, stop=False)
                k.mm(PS[0][:, 128:256], vnew[:, :], sl(attnT, q_), start=False, stop=True)
                k.mm(PS[6][:, 0:128], sl(kdec, q_), vnew[:, :])
                yield
                k.stt(Sb[:, h, :], S[:, h, :], gs["egs"][:, col:col + 1], PS[6][:, 0:128], ALU.mult, ALU.add)
                k.stt(S[:, h, :], S[:, h, :], gs["egs"][:, col:col + 1], PS[6][:, 0:128], ALU.mult, ALU.add)
                k.acp(FB[:, c0:c1], PS[0][:, 128:256])
                yield

        def run_interleaved(gens):
            gens = [g for g in gens if g is not None]
            while gens:
                for g in list(gens):
                    try:
                        next(g)
                    except StopIteration:
                        gens.remove(g)

        inherit(fa_bufs, [FA.b])
        inherit(fe_bufs, [FE.b])
        inherit(alv_bufs, [HA.b])
        run_interleaved([prep(0)])
        for Q in range(4):
            run_interleaved([prep(Q + 1) if Q + 1 < 4 else None, seq(Q)])
        inherit([FA.b], fa_bufs)
        inherit([FE.b], fe_bufs)
        inherit([HA.b], alv_bufs)
        wz = getw()
        for tb in range(4):
            pb = proj(wz, tb)
            k.actf(blk(FC, tb), pb[:, :], AF.Silu)
        k.actf(HB[:, 0:ST], FB[:, 0:ST], AF.Square)
        for tb in range(4):
            k.mm(PS[2 + tb][:, :], ones_b[:], blk(HB, tb))
        for tb in range(4):
            rstd_act(rs4[tb][:], PS[2 + tb][:, :], 1.0 / 128)
        for tb in range(4):
            k.stt(blk(FB, tb), blk(FB, tb), gnorm[:, 0:1], rs4[tb][:], ALU.mult, ALU.mult)
            k.tt(blk(mixT[h], tb), blk(FB, tb), blk(FC, tb), ALU.mult)
        if s == NST - 1:
            k.dma(ogdn_d[j, h, :, :], S[:, h, :], q=k.pool, is_output=True)
            if h == 3:
                k.dma(ogconv_d[j, :, :, :], halo[:], q=k.pool, is_output=True)

    lconvw = k.sb("lconvw_sb", [128, 8, 4])
    lvec = k.sb("lvec_sb", [128, 4, 8])
    nc8 = k.sb("nc8", [128, 8])
    class _AliasTl:
        def __init__(self, ap, bufs):
            self.t = ap
            self.bufs = bufs

        def __getitem__(self, key):
            return V(self.t[key], self.bufs)

        def v(self, ap):
            return V(ap, self.bufs)
    lwa_b = _AliasTl(alt[:, 0:1024].bitcast(BF16).rearrange("p (a b c) -> p a b c", a=4, b=2), allb)
    lwx_b = _AliasTl(alt[:, 1024:2048].bitcast(BF16).rearrange("p (a b c) -> p a b c", a=4, b=2), allb)
    hstate = k.sb("hstate", [128, 8])
    lhalo = k.sb("lhalo", [128, 8, 3])

    def odd_init(l):
        j = l // 2
        k.dma(lconvw[:], lconvw_d[j, :, :, :])
        k.dma(lvec[:], lvec_d[j, :, :, :])
        k.dma(view(FA, FA.t[:, 0:2048].rearrange("p (a b c) -> p a b c", a=4, b=2)), lwa_d[j, :, :, :, :])
        k.dma(view(FB, FB.t[:, 0:2048].rearrange("p (a b c) -> p a b c", a=4, b=2)), lwx_d[j, :, :, :, :])
        k.cp(view(lwa_b, lwa_b.t[:, :, :, :].rearrange("p a b c -> p (a b c)")), FA[:, 0:2048])
        k.cp(view(lwx_b, lwx_b.t[:, :, :, :].rearrange("p a b c -> p (a b c)")), FB[:, 0:2048])
        k.actf(nc8[:], lvec[:, 3, :], AF.Exp, scale=-1.0)
        k.actf(nc8[:], nc8[:], AF.Ln, bias=1.0)
        k.ts(nc8[:], nc8[:], -8.0, ALU.mult)
        k.memset(hstate[:], 0.0)
        k.memset(lhalo[:], 0.0)

    def lru_item(l, s, n):
        j = l // 2
        xcs = (FB, FC)
        xbs = (HC, HDt)
        for cc in range(2):
            c = 2 * n + cc
            wv = getw()
            k.cp(FA[:, 0:3], lhalo[:, c, :])
            for tb in range(4):
                pb = proj(wv, tb)
                k.acp(blk(FA, tb, off=3), pb[:, :])
            k.cp(lhalo[:, c, :], FA[:, ST:ST + 3])
            Fx = xcs[cc]
            k.ts(Fx[:, 0:ST], FA[:, 0:ST], lconvw[:, c, 0:1], ALU.mult, lvec[:, 0, c:c + 1], ALU.add)
            for jj in range(1, 4):
                k.stt(Fx[:, 0:ST], FA[:, jj:jj + ST], lconvw[:, c, jj:jj + 1], Fx[:, 0:ST], ALU.mult, ALU.add)
            k.acp(xbs[cc][:, 0:ST], Fx[:, 0:ST])
        for oc in range(2):
            c = 2 * n + oc
            for tb in range(4):
                for kc in range(2):
                    k.mm(PS[2][:, :], lwa_b[:, n, kc, oc * 128:(oc + 1) * 128], blk(xbs[kc], tb),
                         start=(kc == 0), stop=(kc == 1))
                k.actf(blk(FD, tb), PS[2][:, :], AF.Sigmoid, bias=lvec[:, 1, c:c + 1])
                for kc in range(2):
                    k.mm(PS[3][:, :], lwx_b[:, n, kc, oc * 128:(oc + 1) * 128], blk(xbs[kc], tb),
                         start=(kc == 0), stop=(kc == 1))
                k.actf(blk(FE, tb), PS[3][:, :], AF.Sigmoid, bias=lvec[:, 2, c:c + 1])
            k.actf(FD[:, 0:ST], FD[:, 0:ST], AF.Exp, scale=nc8[:, c:c + 1])
            k.tt(FA[:, 0:ST], FD[:, 0:ST], FD[:, 0:ST], ALU.mult)
            k.actf(FA[:, 0:ST], FA[:, 0:ST], AF.Sqrt, bias=1.0, scale=-1.0)
            k.tt(FE[:, 0:ST], FE[:, 0:ST], xcs[oc][:, 0:ST], ALU.mult)
            k.tt(FE[:, 0:ST], FE[:, 0:ST], FA[:, 0:ST], ALU.mult)
            k.scan(FA[:, 0:ST], FD[:, 0:ST], FE[:, 0:ST], (hstate[:, c:c + 1] if s > 0 else 0.0))
            k.cp(hstate[:, c:c + 1], FA[:, ST - 1:ST])
            wg = getw()
            for tb in range(4):
                pb = proj(wg, tb)
                k.actf(blk(FD, tb), pb[:, :], AF.Silu)
            k.tt(mixT[c][:, :], FA[:, 0:ST], FD[:, 0:ST], ALU.mult)

    dbg_d = k.dout("dbg_mix", [12, 128, SEQ]) if cfg.get("debug") else None

    def out_phase(l, s, xsrc, xdst, last):
        if dbg_d is not None and l == cfg.get("debug_layer", 0):
            for kk in range(12):
                k.acp(FA[:, 0:ST], mixT[kk][:, :])
                k.dma(dbg_d[kk, :, s * ST:(s + 1) * ST], FA[:, 0:ST], q=k.pool, is_output=True)
        for c in range(8):
            for hf in range(2):
                stg = (FA, FB, FC)[(2 * c + hf) % 3]
                sv = stg.v(stg.t[:, 0:768].rearrange("p (k m) -> p k m", k=6))
                k.dma(sv, wout_d[l, c, :, hf * 6:(hf + 1) * 6, :])
                k.cp(V(WOUT.ap[:, hf * 6:(hf + 1) * 6, c * 128:(c + 1) * 128], WOUT.bufs), sv)
        if sctx["on"] and s == 0:
            sample_out(l, last)
        for tb in range(4):
            t0 = s * ST + tb * 512
            for hf, Fx in enumerate((FD, FE)):
                k.dma(view(Fx, Fx.t[:, 0:2048].rearrange("p (c t) -> p c t", c=4)),
                      xsrc.v(xsrc.t[hf * 4:(hf + 1) * 4, :, t0:t0 + 512].rearrange("c p t -> p c t")))
            for c in range(8):
                pb = PS[c % 2]
                for kk in range(12):
                    k.mm(pb[:, :], V(WOUT.ap[:, kk, c * 128:(c + 1) * 128], WOUT.bufs), blk(mixT[kk], tb),
                         start=(kk == 0), stop=(kk == 11))
                if c >= 1:
                    k.mm(PS[6][:, :], ones_b[:], Pt[(c - 1) % 2][:], start=(c == 1), stop=False)
                Fy = (FA, FB)[c // 4]
                k.acp(blk(Fy, c % 4), pb[:, :])
                k.tt(Pt[c % 2][:], blk(Fy, c % 4), blk(Fy, c % 4), ALU.mult)
            k.mm(PS[6][:, :], ones_b[:], Pt[7 % 2][:], start=False, stop=True)
            rstd_act(rs[:], PS[6][:, :], 1.0 / D)
            for c in range(8):
                Fy = (FA, FB)[c // 4]
                Fx = (FD, FE)[c // 4]
                k.stt(blk(Fy, c % 4), blk(Fy, c % 4), gpost[:, l, c:c + 1], rs[:], ALU.mult, ALU.mult)
                k.tt(blk(Fx, c % 4), blk(Fx, c % 4), blk(Fy, c % 4), ALU.add)
            for hf, Fx in enumerate((FD, FE)):
                k.dma(xdst.v(xdst.t[hf * 4:(hf + 1) * 4, :, t0:t0 + 512].rearrange("c p t -> p c t")),
                      view(Fx, Fx.t[:, 0:2048].rearrange("p (c t) -> p c t", c=4)), q=k.pool, is_output=last)


    def sv(tl, ap):
        return tl.v(ap)

    xs = k.sb("xs_sb", [128, 8, 4])
    mix_s = k.sb("mix_s", [128, 12, 4], BF16)
    sgconv_sb = k.sb("sgconv_sb", [128, 12, 4, 3])
    slru_sb = k.sb("slru_sb", [128, 8, 4])
    slconv_sb = k.sb("slconv_sb", [128, 8, 4, 3])
    sT = [k.sb("sT%d" % i, [128, 64]) for i in range(6)]
    srow = k.sb("srow", [128, 128])
    sBeta = k.sb("sBeta", [128, 16])
    sNBeta = k.sb("sNBeta", [128, 16])
    sEg = k.sb("sEg", [128, 16])
    k.dma(xs[:], xsT_d[:, :, :])

    def bulk_copies():
        for g in range(3):
            L = GROUPS[g][0]
            for j in range(2):
                for tok in range(4):
                    r0 = 1
                    while r0 < L:
                        r1 = min(L, r0 + 512)
                        k.bulk_copy(oswas_d[g].t[j, tok, r0 - 1:r1 - 1].rearrange("r a h d -> r (a h d)"),
                                    cswa_d[g].t[j, tok, r0:r1].rearrange("r a h d -> r (a h d)"))
                        r0 = r1
        for j in range(2):
            k.bulk_copy(ogconvs_d.t[j, :, 0:2, :], sgconv_nat.t[j, :, 1:3, :])
            k.bulk_copy(olconvs_d.t[j, :, 0:2, :], slconv_nat.t[j, :, 1:3, :])

    def emit_rows(src, n, dst, is_out=True):
        k.tr(PS[7][0:n, 0:128], src, ident_f)
        k.cp(srow[0:n, :], PS[7][0:n, 0:128])
        k.dma(dst, srow[0:n, :], q=k.pool, is_output=is_out)

    def sample_norm(l):
        k.tt(sT[0][:, 0:32], xs.v(xs.t[:, :, :].rearrange("p c t -> p (c t)")),
             xs.v(xs.t[:, :, :].rearrange("p c t -> p (c t)")), ALU.mult)
        for c in range(8):
            k.mm(PS[7][:, 0:4], ones_f, sT[0][:, c * 4:(c + 1) * 4], start=(c == 0), stop=(c == 7))
        k.actf(sT[1][:, 0:4], PS[7][:, 0:4], AF.Sqrt, bias=EPS, scale=1.0 / D)
        k.recip(sT[1][:, 0:4], sT[1][:, 0:4])
        for c in range(8):
            k.stt(hs[:, c, :], xs[:, c, :], gpre[:, l, c:c + 1], sT[1][:, 0:4], ALU.mult, ALU.mult)

    def zero_bank(pb, n):
        k.mm(pb[:, 0:n], mP_zero[:, 0, :], mP_zero.v(mP_zero.t[:, :, :].rearrange("p a b -> p (a b)")[:, 0:n]),
             start=True, stop=True, sgc=True)

    def s_qB(qcol):
        for h in range(4):
            k.ts(FA[:, h * 128:(h + 1) * 128], ident_f, qcol(h), ALU.mult)
        k.mm(PS[6][:, :], ones_f, FA[:, 0:512])
        k.acp(FA[:, 512:1024], PS[6][:, :])

    kvbuf = ((FB, FC), (FE, FE))

    def s_kvload(i, Kd, Vd):
        kb_, vb_ = kvbuf[i % 2]
        if i % 2 == 0:
            k.dma(kb_[:, 0:512], Kd)
            k.dma(vb_[:, 0:512], Vd)
        else:
            k.dma(kb_[:, 0:512], Kd)
            k.dma(vb_[:, 512:1024], Vd)

    def s_keyblock(i, tok):
        kb_, vb_ = kvbuf[i % 2]
        Kt = kb_[:, 0:512]
        voff = 0 if i % 2 == 0 else 512
        k.tt(FD[:, 0:512], Kt, FA[:, 512:1024], ALU.mult)
        k.reduce_x(sT[2][:, 0:4], sv(FD, FD.t[:, 0:512].rearrange("p (h d) -> p h d", h=4)))
        k.actf(sT[3][:, 0:4], sT[2][:, 0:4], AF.Exp, scale=SCALE)
        for h in range(4):
            k.mm(PS[4][:, tok * 4 + h:tok * 4 + h + 1], vb_[:, voff + h * 128:voff + (h + 1) * 128], sT[3][:, h:h + 1],
                 start=False, stop=True, sgc=True)
        k.mm(PS[5][:, tok * 4:tok * 4 + 4], ones_f, sT[3][:, 0:4], start=False, stop=True, sgc=True)

    def s_run_blocks(blocks):
        if blocks:
            s_kvload(0, blocks[0][2], blocks[0][3])
        for i, (tok, qfn, Kd, Vd) in enumerate(blocks):
            if i + 1 < len(blocks):
                s_kvload(i + 1, blocks[i + 1][2], blocks[i + 1][3])
            if qfn is not None:
                s_qB(qfn)
            s_keyblock(i, tok)

    def s_mem(l):
        even = (l % 2 == 0)
        qs = (lambda h: 2 * h) if even else (lambda h: 16 + 2 * h)
        zero_bank(PS[4], 16)
        zero_bank(PS[5], 16)
        blocks = []
        for tok in range(4):
            for kb in range(2):
                blocks.append((tok, (lambda h, tok=tok: sproj[:, qs(h), tok:tok + 1]) if kb == 0 else None,
                               cmem_d.v(cmem_d.t[l, tok, kb * 128:(kb + 1) * 128, 0].rearrange("r h d -> r (h d)")),
                               cmem_d.v(cmem_d.t[l, tok, kb * 128:(kb + 1) * 128, 1].rearrange("r h d -> r (h d)"))))
        s_run_blocks(blocks)
        k.recip(sT[2][:, 0:16], PS[5][:, 0:16])
        k.tt(sT[2][:, 0:16], sT[2][:, 0:16], PS[4][:, 0:16], ALU.mult)
        for h in range(4):
            k.actf(sT[3][:, 0:4], sproj[:, qs(h) + 1, :], AF.Silu)
            k.tt(mix_s[:, 8 + h, :], sv(sT[2], sT[2].t[:, 0:16].rearrange("p (t h) -> p h t", h=4)[:, h, :]),
                 sT[3][:, 0:4], ALU.mult)

    def s_swa(l):
        j = l // 2
        zero_bank(PS[4], 16)
        zero_bank(PS[5], 16)
        qslot = lambda g, h: 8 + 10 * h + 3 * g
        blocks = []
        for tok in range(4):
            for g in range(3):
                L, d = GROUPS[g]
                cd = cswa_d[g]
                blocks.append((tok, (lambda h, tok=tok, g=g: sproj[:, qslot(g, h), tok:tok + 1]),
                               cd.v(cd.t[j, tok, :, 0].rearrange("(i s) h d -> i s (h d)", s=d)[:, 0, :]),
                               cd.v(cd.t[j, tok, :, 1].rearrange("(i s) h d -> i s (h d)", s=d)[:, 0, :])))
        s_run_blocks(blocks)
        for tok in range(4):
            for g in range(3):
                for h in range(4):
                    c = g * 4 + h
                    k.tt(sT[0][:, c:c + 1], sproj[:, qslot(g, h), tok:tok + 1], sproj[:, qslot(g, h) + 1, tok:tok + 1],
                         ALU.mult)
            k.mm(PS[6][:, 0:12], ones_f, sT[0][:, 0:12])
            k.actf(sT[1][:, 0:12], PS[6][:, 0:12], AF.Exp, scale=SCALE)
            for h in range(4):
                col = tok * 4 + h
                k.cp(sT[4][:, col:col + 1], PS[4][:, col:col + 1])
                k.cp(sT[5][:, col:col + 1], PS[5][:, col:col + 1])
                for g in range(3):
                    c = g * 4 + h
                    k.stt(sT[4][:, col:col + 1], sproj[:, qslot(g, h) + 2, tok:tok + 1], sT[1][:, c:c + 1],
                          sT[4][:, col:col + 1], ALU.mult, ALU.add)
                    k.tt(sT[5][:, col:col + 1], sT[5][:, col:col + 1], sT[1][:, c:c + 1], ALU.add)
            for g in range(3):
                for kv in range(2):
                    for h in range(4):
                        c = g * 8 + kv * 4 + h
                        k.cp(sT[2][:, c:c + 1], sproj[:, qslot(g, h) + 1 + kv, tok:tok + 1])
            k.tr(PS[7][0:24, 0:128], sT[2][:, 0:24], ident_f)
            k.cp(srow[0:24, :], PS[7][0:24, 0:128])
            for g in range(3):
                L = GROUPS[g][0]
                k.dma(oswas_d[g].v(oswas_d[g].t[j, tok, L - 1].rearrange("a h d -> (a h) d")),
                      srow[g * 8:(g + 1) * 8, :], q=k.pool, is_output=True)
        k.recip(sT[5][:, 0:16], sT[5][:, 0:16])
        k.tt(sT[4][:, 0:16], sT[4][:, 0:16], sT[5][:, 0:16], ALU.mult)
        for h in range(4):
            k.actf(sT[3][:, 0:4], sproj[:, 8 + 10 * h + 9, :], AF.Silu)
            k.tt(mix_s[:, 4 + h, :], sv(sT[4], sT[4].t[:, 0:16].rearrange("p (t h) -> p h t", h=4)[:, h, :]),
                 sT[3][:, 0:4], ALU.mult)

    def s_gdn(l):
        j = l // 2
        k.dma(sgconv_sb[:], sgconv_d[j, :, :, :, :])
        for tok in range(4):
            for kk in range(8):
                k.ts(HC[:, kk * 128:(kk + 1) * 128], ones_b[:], hs[:, kk, tok:tok + 1], ALU.mult)
            for kk in range(8):
                k.mm(PS[6][:, tok * 8:(tok + 1) * 8], HC[:, kk * 128:(kk + 1) * 128], wba_b[:, kk, :],
                     start=(kk == 0), stop=(kk == 7))
        ba = PS[6].t[:, 0:32].rearrange("p (t c) -> p t c", c=8)
        g3 = lambda tl: tl.v(tl.t[:, 0:16].rearrange("p (t c) -> p t c", c=4))
        k.actf(g3(sBeta), PS[6].v(ba[:, :, 0:4]), AF.Sigmoid)
        k.ts(sNBeta[:], sBeta[:], -1.0, ALU.mult)
        k.tt(g3(sEg), PS[6].v(ba[:, :, 4:8]), g3(dtb), ALU.add)
        k.actf(sEg[:], sEg[:], AF.Exp)
        k.actf(sEg[:], sEg[:], AF.Ln, bias=1.0)
        k.tt(sEg[:], sEg[:], negA[:, 0:16], ALU.mult)
        k.actf(sEg[:], sEg[:], AF.Exp)
        cq = sv(FD, FD.t[:, 0:48].rearrange("p (c t) -> p c t", t=4))
        for ci in range(12):
            t_, h_ = ci // 4, ci % 4
            xsl = sproj[:, 48 + 4 * h_ + t_, :]
            dst = sv(FD, FD.t[:, ci * 4:(ci + 1) * 4])
            k.ts(dst, sgconv_sb[:, ci, :, 0], gconvw[:, ci, 0:1], ALU.mult)
            for r in (1, 2):
                k.stt(dst, sgconv_sb[:, ci, :, r], gconvw[:, ci, r:r + 1], dst, ALU.mult, ALU.add)
            k.stt(dst, xsl, gconvw[:, ci, 3:4], dst, ALU.mult, ALU.add)
            k.cp(sT[0][:, ci * 4:(ci + 1) * 4], xsl)
        for tok in range(4):
            emit_rows(sv(sT[0], sT[0].t[:, 0:48].rearrange("p (c t) -> p c t", t=4)[:, :, tok]), 12,
                      ogconvs_d.v(ogconvs_d.t[j, tok, 2].rearrange("(c p) -> c p", p=128)))
        k.actf(FD[:, 0:48], FD[:, 0:48], AF.Silu)
        k.tt(sT[1][:, 0:32], FD[:, 0:32], FD[:, 0:32], ALU.mult)
        k.mm(PS[6][:, 64:96], ones_f, sT[1][:, 0:32])
        k.actf(sT[1][:, 0:32], PS[6][:, 64:96], AF.Sqrt, bias=EPS)
        k.recip(sT[1][:, 0:32], sT[1][:, 0:32])
        k.stt(FD[:, 0:16], FD[:, 0:16], SCALE, sT[1][:, 0:16], ALU.mult, ALU.mult)
        k.tt(FD[:, 16:32], FD[:, 16:32], sT[1][:, 16:32], ALU.mult)
        for tok in range(4):
            for h in range(4):
                col = tok * 4 + h
                qc = FD[:, (0 + h) * 4 + tok:(0 + h) * 4 + tok + 1]
                kc = FD[:, (4 + h) * 4 + tok:(4 + h) * 4 + tok + 1]
                vc = FD[:, (8 + h) * 4 + tok:(8 + h) * 4 + tok + 1]
                k.dma(FE[:, 0:128], sgdn_d[j, tok, h, :, :])
                k.mm(PS[6][:, 128:129], FE[:, 0:128], kc)
                k.stt(sT[2][:, 0:1], PS[6][:, 128:129], sEg[:, col:col + 1], vc, ALU.mult, ALU.subtract)
                k.ts(sT[2][:, 0:1], sT[2][:, 0:1], sNBeta[:, col:col + 1], ALU.mult)
                k.ts(FB[:, 0:128], ident_f, sT[2][:, 0:1], ALU.mult)
                k.mm(PS[5][:, 256:384], ones_f, FB[:, 0:128])
                k.ts(FE[:, 0:128], FE[:, 0:128], sEg[:, col:col + 1], ALU.mult)
                k.stt(FE[:, 128:256], PS[5][:, 256:384], kc, FE[:, 0:128], ALU.mult, ALU.add)
                k.dma(ogdns_d[j, tok, h, :, :], FE[:, 128:256], q=k.pool, is_output=True)
                k.mm(PS[6][:, 129:130], FE[:, 128:256], qc)
                k.cp(sT[3][:, col:col + 1], PS[6][:, 129:130])
        k.tt(sT[1][:, 0:16], sT[3][:, 0:16], sT[3][:, 0:16], ALU.mult)
        k.mm(PS[6][:, 64:80], ones_f, sT[1][:, 0:16])
        k.actf(sT[1][:, 0:16], PS[6][:, 64:80], AF.Sqrt, bias=EPS, scale=1.0 / 128)
        k.recip(sT[1][:, 0:16], sT[1][:, 0:16])
        k.stt(sT[3][:, 0:16], sT[3][:, 0:16], gnorm[:, 0:1], sT[1][:, 0:16], ALU.mult, ALU.mult)
        for h in range(4):
            k.actf(sT[2][:, 0:4], sproj[:, 48 + 4 * h + 3, :], AF.Silu)
            k.tt(mix_s[:, h, :], sv(sT[3], sT[3].t[:, 0:16].rearrange("p (t h) -> p h t", h=4)[:, h, :]),
                 sT[2][:, 0:4], ALU.mult)

    def s_lru(l):
        j = l // 2
        k.dma(slru_sb[:], slru_d[j, :, :, :])
        k.dma(slconv_sb[:], slconv_d[j, :, :, :, :])
        xc = lambda c: FD[:, c * 4:(c + 1) * 4]
        xcb = lambda c: HC[:, c * 4:(c + 1) * 4]
        for c in range(8):
            n, cc = c // 2, c % 2
            xsl = sproj[:, 4 * n + cc, :]
            k.ts(xc(c), slconv_sb[:, c, :, 0], lconvw[:, c, 0:1], ALU.mult, lvec[:, 0, c:c + 1], ALU.add)
            for r in (1, 2):
                k.stt(xc(c), slconv_sb[:, c, :, r], lconvw[:, c, r:r + 1], xc(c), ALU.mult, ALU.add)
            k.stt(xc(c), xsl, lconvw[:, c, 3:4], xc(c), ALU.mult, ALU.add)
            k.cp(sT[0][:, c * 4:(c + 1) * 4], xsl)
        for tok in range(4):
            emit_rows(sv(sT[0], sT[0].t[:, 0:32].rearrange("p (c t) -> p c t", t=4)[:, :, tok]), 8,
                      olconvs_d.v(olconvs_d.t[j, tok, 2].rearrange("(c p) -> c p", p=128)))
        k.cp(HC[:, 0:32], FD[:, 0:32])
        for c in range(8):
            n, oc = c // 2, c % 2
            for gi, (wb_, bi) in enumerate(((lwa_b, 1), (lwx_b, 2))):
                for kc in range(2):
                    k.mm(PS[6][:, gi * 32 + c * 4:gi * 32 + (c + 1) * 4], wb_[:, n, kc, oc * 128:(oc + 1) * 128],
                         xcb(2 * n + kc), start=(kc == 0), stop=(kc == 1))
                k.actf(sT[1 + gi][:, c * 4:(c + 1) * 4], PS[6][:, gi * 32 + c * 4:gi * 32 + (c + 1) * 4], AF.Sigmoid,
                       bias=lvec[:, bi, c:c + 1])
            k.actf(sT[1][:, c * 4:(c + 1) * 4], sT[1][:, c * 4:(c + 1) * 4], AF.Exp, scale=nc8[:, c:c + 1])
        k.tt(sT[3][:, 0:32], sT[1][:, 0:32], sT[1][:, 0:32], ALU.mult)
        k.actf(sT[3][:, 0:32], sT[3][:, 0:32], AF.Sqrt, bias=1.0, scale=-1.0)
        k.tt(sT[2][:, 0:32], sT[2][:, 0:32], FD[:, 0:32], ALU.mult)
        k.tt(sT[2][:, 0:32], sT[2][:, 0:32], sT[3][:, 0:32], ALU.mult)
        k.tt(sT[1][:, 0:32], sT[1][:, 0:32], slru_sb.v(slru_sb.t[:, :, :].rearrange("p c t -> p (c t)")), ALU.mult)
        k.tt(sT[1][:, 0:32], sT[1][:, 0:32], sT[2][:, 0:32], ALU.add)
        k.cp(sv(sT[5], sT[5].t[:, 0:32].rearrange("p (t c) -> p t c", c=8)),
             sv(sT[1], sT[1].t[:, 0:32].rearrange("p (c t) -> p t c", t=4)))
        emit_rows(sT[5][:, 0:32], 32, olrus_d.v(olrus_d.t[j].rearrange("t (c p) -> (t c) p", p=128)))
        for c in range(8):
            n, cc = c // 2, c % 2
            k.actf(sT[3][:, 0:4], sproj[:, 4 * n + 2 + cc, :], AF.Silu)
            k.tt(mix_s[:, c, :], sT[1][:, c * 4:(c + 1) * 4], sT[3][:, 0:4], ALU.mult)

    def sample_mixers(l):
        if l % 2 == 0:
            s_mem(l)
            s_swa(l)
            s_gdn(l)
        else:
            s_lru(l)
            s_mem(l)

    def sample_out(l, last):
        for c in range(8):
            for kk in range(12):
                k.mm(PS[7][:, c * 4:(c + 1) * 4], V(WOUT.ap[:, kk, c * 128:(c + 1) * 128], WOUT.bufs), mix_s[:, kk, :],
                     start=(kk == 0), stop=(kk == 11))
        k.cp(sT[0][:, 0:32], PS[7][:, 0:32])
        k.tt(sT[1][:, 0:32], sT[0][:, 0:32], sT[0][:, 0:32], ALU.mult)
        for c in range(8):
            k.mm(PS[7][:, 32:36], ones_f, sT[1][:, c * 4:(c + 1) * 4], start=(c == 0), stop=(c == 7))
        k.actf(sT[2][:, 0:4], PS[7][:, 32:36], AF.Sqrt, bias=EPS, scale=1.0 / D)
        k.recip(sT[2][:, 0:4], sT[2][:, 0:4])
        for c in range(8):
            k.stt(sT[0][:, c * 4:(c + 1) * 4], sT[0][:, c * 4:(c + 1) * 4], gpost[:, l, c:c + 1], sT[2][:, 0:4],
                  ALU.mult, ALU.mult)
            k.tt(xs[:, c, :], xs[:, c, :], sT[0][:, c * 4:(c + 1) * 4], ALU.add)
        if last:
            k.cp(sv(sT[5], sT[5].t[:, 0:32].rearrange("p (t c) -> p t c", c=8)),
                 xs.v(xs.t[:, :, :].rearrange("p c t -> p t c")))
            emit_rows(sT[5][:, 0:32], 32, oys_d.v(oys_d.t[:, :].rearrange("t (c p) -> (t c) p", p=128)))
    SAMPLE = sctx["on"]
    if SAMPLE:
        bulk_copies()
    for l in range(NL):
        last = (l == NL - 1)
        xsrc = xin_d if l == 0 else xs_d[(l - 1) % 2]
        xdst = yT_d if last else xs_d[l % 2]
        mem_phase(l)
        if l % 2 == 0:
            even_init(l)
        else:
            odd_init(l)
        if SAMPLE:
            sample_norm(l)
        for s in range(NST):
            sctx["s"] = s
            if s == 0:
                sctx["slot"] = 0
            norm_phase(l, s, xsrc)
            if l % 2 == 0:
                gdn_prep()
                for kind, h, _ in even_order():
                    if kind == "mem":
                        mem_item(l, h, 8 + h)
                    elif kind == "swa":
                        swa_item(l, s, h)
                    else:
                        gdn_item(l, s, h)
            else:
                for kind, h, _ in odd_order():
                    if kind == "lru":
                        lru_item(l, s, h)
                    else:
                        mem_item(l, h, 8 + h)
                if s == NST - 1:
                    j = l // 2
                    k.dma(olru_d[j, :, :], hstate[:], q=k.pool, is_output=True)
                    k.dma(olconv_d[j, :, :, :], lhalo[:], q=k.pool, is_output=True)
            if SAMPLE and s == 0:
                sample_mixers(l)
            out_phase(l, s, xsrc, xdst, last)
    assert W.n_use == len(plan), (W.n_use, len(plan))
    k.finish()
    return k


def _consts():
    c = np.zeros((128, NCST, 128), np.float32)
    p = np.arange(128)[:, None]
    f = np.arange(128)[None, :]
    c[:, C_ID] = (p == f)
    c[:, C_ONE] = 1.0
    c[:, C_NSL] = np.where(p > f, 0.0, NEG)
    c[:, C_NIU] = np.where(f >= p, 0.0, NEG)
    c[:, C_TRIU] = (p <= f)
    c[:, C_SEL] = (p == 127) * np.ones((1, 128))
    c[:, C_MP] = (p >= f)
    c[:, C_MC] = (p <= f)
    for lv in range(1, 8):
        m = (p > f) & ((p >> lv) == (f >> lv)) & ((p >> (lv - 1)) != (f >> (lv - 1)))
        if lv == 1:
            c[:, C_NM1] = -1.0 * m
        else:
            c[:, C_LV + lv - 2] = m
    return np.ascontiguousarray(c)


def _f(a):
    return np.ascontiguousarray(np.asarray(a, dtype=np.float32))


def _fm(v, n):
    v = _f(v)
    lead = v.shape[:-1]
    v = v.reshape(lead + (n, 128))
    return _f(np.moveaxis(v, -1, 0))


def _tile_cols(Wm, cols, nk):
    out = np.empty((len(cols), 128, nk, 128), np.float32)
    for i, c0 in enumerate(cols):
        out[i] = Wm[:, c0:c0 + 128].reshape(nk, 128, 128).transpose(1, 0, 2)
    return out


def kernel(**inp):
    return kernel_impl(inp)


def kernel_impl(inp, cfg=None):
    cfg = cfg or {}
    NL = cfg.get("nlayers", 4)
    k = build_program(cfg)
    shared = {"cst": _consts()}
    shared["wkv"] = _f(_f(inp["w_mem_kv"]).reshape(4, 8, 128, 1024).transpose(0, 2, 1, 3))
    shared["gmem"] = _fm(inp["mem_norm"], 8)
    shared["gpre"] = _fm(inp["norm_pre"], 8)
    shared["gpost"] = _fm(inp["norm_post"], 8)
    we = _f(inp["w_in_even"])
    wo = _f(inp["w_in_odd"])
    ecols = [c for _, _, cs in even_order() for c in cs]
    ocols = [c for _, _, cs in odd_order() for c in cs]
    shared["w_even"] = np.stack([_tile_cols(we[j], ecols, 8) for j in range(2)])
    shared["w_odd"] = np.stack([_tile_cols(wo[j], ocols, 8) for j in range(2)])
    woe = _f(inp["w_out_even"])
    woo = _f(inp["w_out_odd"])
    shared["w_out"] = np.stack([_tile_cols((woe if l % 2 == 0 else woo)[l // 2], [c * 128 for c in range(8)], 12)
                                for l in range(4)])
    shared["wba"] = _f(we[:, :, 2048:2056].reshape(2, 8, 128, 8).transpose(0, 2, 1, 3))
    shared["gconvw"] = _f(_f(inp["gdn_conv_w"]).transpose(0, 2, 1).reshape(2, 12, 128, 4).transpose(0, 2, 1, 3))
    shared["alog"] = _f(np.broadcast_to(np.tile(_f(inp["gdn_a_log"]), (1, 16))[:, None, :], (2, 128, 64)))
    shared["dtb"] = _f(np.broadcast_to(np.tile(_f(inp["gdn_dt_bias"]), (1, 16))[:, None, :], (2, 128, 64)))
    shared["gnorm"] = _f(_f(inp["gdn_norm"]).reshape(2, 128, 1))
    shared["lconvw"] = _f(_f(inp["lru_conv_w"]).transpose(0, 2, 1).reshape(2, 8, 128, 4).transpose(0, 2, 1, 3))
    lv = np.stack([_f(inp["lru_conv_b"]).reshape(2, 1024), _f(inp["lru_ba"]).reshape(2, 1024),
                   _f(inp["lru_bx"]).reshape(2, 1024), _f(inp["lru_lambda"]).reshape(2, 1024)], axis=1)
    shared["lvec"] = _f(lv.reshape(2, 4, 8, 128).transpose(0, 3, 1, 2))
    shared["lwa"] = _f(_f(inp["lru_wa"]).reshape(2, 4, 2, 128, 256).transpose(0, 3, 1, 2, 4))
    shared["lwx"] = _f(_f(inp["lru_wx"]).reshape(2, 4, 2, 128, 256).transpose(0, 3, 1, 2, 4))
    xp = _f(inp["x_prompt"])
    mp = _f(inp["mem_prompt"])
    SAMPLE = cfg.get("sample", True)
    if SAMPLE:
        xsm = _f(inp["x_sample"]).reshape(32, 8, 128)
        sg = _f(inp["state_gdn"])
        sgc = _f(inp["state_gdn_conv"])
        csw = [_f(inp["cache_swa1"]), _f(inp["cache_swa2"]), _f(inp["cache_swa3"])]
        cme = _f(inp["cache_mem"])
        sl = _f(inp["state_lru"])
        slc = _f(inp["state_lru_conv"])
    in_maps = []
    for c in range(NCORES):
        b = c % 4
        m = dict(shared)
        m["xT"] = _f(xp[b].T.reshape(8, 128, SEQ))
        m["memT"] = _f(mp[b].T.reshape(8, 128, MEM).transpose(1, 0, 2))
        if SAMPLE:
            b0 = c * SPC
            sl_ = slice(b0, b0 + SPC)
            m["xsT"] = _f(xsm[sl_].transpose(2, 1, 0))
            m["sgdn"] = _f(sg[:, sl_])
            m["sgconv"] = _f(sgc[:, sl_].reshape(2, SPC, 3, 12, 128).transpose(0, 4, 3, 1, 2))
            m["sgconv_nat"] = _f(sgc[:, sl_])
            for g in range(3):
                m["cswa%d" % (g + 1)] = _f(csw[g][:, sl_])
            m["cmem"] = _f(cme[:, sl_])
            m["slru"] = _f(sl[:, sl_].reshape(2, SPC, 8, 128).transpose(0, 3, 2, 1))
            m["slconv"] = _f(slc[:, sl_].reshape(2, SPC, 3, 8, 128).transpose(0, 4, 3, 1, 2))
            m["slconv_nat"] = _f(slc[:, sl_])
        in_maps.append(m)
    res = run_bass_kernel_spmd(k.nc, in_maps, core_ids=list(range(NCORES)))
    R = res.results
    B = 4
    y_prompt = np.zeros((B, SEQ, D), np.float32)
    mem_p = np.zeros((4, B, MEM, 2, 4, 128), np.float32)
    gdn_p = np.zeros((2, B, 4, 128, 128), np.float32)
    gconv_p = np.zeros((2, B, 3, 1536), np.float32)
    swa_p = [np.zeros((2, B, GROUPS[g][0], 2, 4, 128), np.float32) for g in range(3)]
    lru_p = np.zeros((2, B, 1024), np.float32)
    lconv_p = np.zeros((2, B, 3, 1024), np.float32)
    for b in range(B):
        r = R[b]
        y_prompt[b] = r["o_yT"].reshape(1024, SEQ).T
        mem_p[:, b] = r["o_memT"].reshape(4, 2, 4, 128, MEM).transpose(0, 4, 1, 2, 3)
        gdn_p[:, b] = r["o_gdn"]
        gconv_p[:, b] = r["o_gconv"].transpose(0, 3, 2, 1).reshape(2, 3, 1536)
        for g in range(3):
            swa_p[g][:, b] = r["o_swa%d" % (g + 1)].transpose(0, 4, 1, 2, 3)
        lru_p[:, b] = r["o_lru"].transpose(0, 2, 1).reshape(2, 1024)
        lconv_p[:, b] = r["o_lconv"].transpose(0, 3, 2, 1).reshape(2, 3, 1024)
    SB = 32
    outs = {
        "y_prompt": y_prompt,
        "gdn_p": gdn_p, "gdn_conv_p": gconv_p,
        "swa1_p": swa_p[0], "swa2_p": swa_p[1], "swa3_p": swa_p[2],
        "lru_p": lru_p, "lru_conv_p": lconv_p, "mem_p": mem_p,
    }
    if cfg.get("debug"):
        outs["dbg_mix"] = R[0]["dbg_mix"].reshape(1536, SEQ).T
    y_sample = np.zeros((SB, 1, D), np.float32)
    gdn_s = np.zeros((2, SB, 4, 128, 128), np.float32)
    gconv_s = np.zeros((2, SB, 3, 1536), np.float32)
    swa_s = [np.zeros((2, SB, GROUPS[g][0], 2, 4, 128), np.float32) for g in range(3)]
    lru_s = np.zeros((2, SB, 1024), np.float32)
    lconv_s = np.zeros((2, SB, 3, 1024), np.float32)
    if SAMPLE:
        for c in range(NCORES):
            r = R[c]
            sl_ = slice(c * SPC, (c + 1) * SPC)
            y_sample[sl_, 0] = r["o_ys"]
            gdn_s[:, sl_] = r["o_gdn_s"]
            gconv_s[:, sl_] = r["o_gconv_s"]
            for g in range(3):
                swa_s[g][:, sl_] = r["o_swa%d_s" % (g + 1)]
            lru_s[:, sl_] = r["o_lru_s"]
            lconv_s[:, sl_] = r["o_lconv_s"]
    if cfg.get("as_dict"):
        outs.update({"y_sample": y_sample, "gdn_s": gdn_s, "gdn_conv_s": gconv_s, "swa1_s": swa_s[0],
                     "swa2_s": swa_s[1], "swa3_s": swa_s[2], "lru_s": lru_s, "lru_conv_s": lconv_s})
        return outs
    return (y_prompt, y_sample, gdn_p, gdn_s, gconv_p, gconv_s, swa_p[0], swa_s[0], swa_p[1], swa_s[1],
            swa_p[2], swa_s[2], lru_p, lru_s, lconv_p, lconv_s, mem_p)
```
